# Optimizing a Trainium2 kernel written in Bass

```python
import jax, jax.numpy as jnp
from jax import lax
import numpy as np

D_MODEL = 1024
BATCH = 8
SEQ = 2048
DEPTH = 2
DEC_BATCH = 128
DEC_SEQ = 4
PAST_LEN = 16384
PAGE_SIZE = 128

D_MIX = D_MODEL
A_HEADS = 4
A_HEAD_DIM = D_MIX // 16
A_WIDTH = A_HEADS * A_HEAD_DIM
A_CHUNK = 128
B_HEADS = 8
B_KEY_DIM = D_MIX // 16
B_VAL_DIM = D_MIX // 16
B_KEY_WIDTH = B_HEADS * B_KEY_DIM
B_WIDTH = B_HEADS * B_VAL_DIM
B_CHUNK = 16
C_WIDTH = D_MIX // 4
C_CONV = 31
D_FF = 4 * D_MODEL
N_IN = 2 * A_WIDTH + 2 * B_KEY_WIDTH + 2 * B_WIDTH + 2 * C_WIDTH
EPS = 1e-6

kernel_name = "hybrid_gmlp_hgrn2_conformer_adaln_step"


def _rmsnorm(x, g):
    xf = x.astype(jnp.float32)
    y = xf * lax.rsqrt(jnp.mean(xf * xf, axis=-1, keepdims=True) + EPS)
    return (y * g.astype(jnp.float32)).astype(x.dtype)


def _layernorm(x, g, b):
    xf = x.astype(jnp.float32)
    xc = xf - jnp.mean(xf, axis=-1, keepdims=True)
    y = xc * lax.rsqrt(jnp.mean(xc * xc, axis=-1, keepdims=True) + EPS)
    return (y * g.astype(jnp.float32) + b.astype(jnp.float32)).astype(x.dtype)


def _chunk_gmlp(z, ln_g, ln_b, w_s, b_s):
    n, L, _ = z.shape
    z = jax.nn.gelu(z, approximate=False)
    u, v = jnp.split(z, 2, axis=-1)
    v = _layernorm(v, ln_g, ln_b)
    lc = min(L, A_CHUNK)
    nc = L // lc
    vh = v.reshape(n, nc, lc, A_HEADS, A_HEAD_DIM)
    mask = jnp.tril(jnp.ones((lc, lc), dtype=bool))
    w = jnp.where(mask, w_s[:, :lc, :lc], 0).astype(v.dtype)
    bias = b_s[:, :lc].T[None, None, :, :, None].astype(v.dtype)
    mixed = jnp.einsum('hts,ncshd->ncthd', w, vh) + bias
    return u * mixed.reshape(n, L, A_WIDTH), v


def _hgrn2(zq, zf, zi, zg, lb, gn_g, s0):
    f32 = jnp.float32
    n, L, _ = zq.shape
    q = (jax.nn.silu(zq.astype(f32)) * B_KEY_DIM ** -0.5).reshape(n, L, B_HEADS, B_KEY_DIM)
    f = lb.astype(f32) + (1.0 - lb.astype(f32)) * jax.nn.sigmoid(zf.astype(f32))
    logf = jnp.log(f).reshape(n, L, B_HEADS, B_KEY_DIM)
    k = (1.0 - f).reshape(n, L, B_HEADS, B_KEY_DIM)
    v = zi.astype(f32).reshape(n, L, B_HEADS, B_VAL_DIM)
    nchunk = -(-L // B_CHUNK)
    pad = nchunk * B_CHUNK - L

    def to_chunks(a):
        a = jnp.pad(a, ((0, 0), (0, pad), (0, 0), (0, 0)))
        return a.reshape(n, nchunk, B_CHUNK, B_HEADS, a.shape[-1]).transpose(1, 0, 3, 2, 4)

    mask = jnp.tril(jnp.ones((B_CHUNK, B_CHUNK), dtype=bool))

    def step(S, xs):
        qc, kc, vc, lfc = xs
        b = jnp.cumsum(lfc, axis=2)
        q_in = qc * jnp.exp(b)
        k_in = kc * jnp.exp(-b)
        scores = jnp.where(mask, jnp.einsum('nhtk,nhsk->nhts', q_in, k_in), 0.0)
        o = jnp.einsum('nhtk,nhkv->nhtv', q_in, S) + jnp.einsum('nhts,nhsv->nhtv', scores, vc)
        b_last = b[:, :, -1:, :]
        S = jnp.exp(b_last[:, :, 0, :, None]) * S + jnp.einsum('nhsk,nhsv->nhkv', kc * jnp.exp(b_last - b), vc)
        return S, o

    S, o = lax.scan(step, s0.astype(f32), (to_chunks(q), to_chunks(k), to_chunks(v), to_chunks(logf)))
    o = o.transpose(1, 0, 3, 2, 4).reshape(n, nchunk * B_CHUNK, B_HEADS, B_VAL_DIM)[:, :L]
    o = _rmsnorm(o, gn_g.reshape(B_HEADS, B_VAL_DIM)).reshape(n, L, B_WIDTH)
    o = o * jax.nn.silu(zg.astype(f32))
    return o.astype(zq.dtype), S


def _conformer_conv(z, buf, w_dw, b_dw, ln_g, ln_b):
    a, gate = jnp.split(z, 2, axis=-1)
    xg = a * jax.nn.sigmoid(gate)
    xx = jnp.concatenate([buf.astype(xg.dtype), xg], axis=1)
    y = lax.conv_general_dilated(
        xx, w_dw[:, None, :].astype(xg.dtype), window_strides=(1,), padding='VALID',
        dimension_numbers=('NWC', 'WIO', 'NWC'), feature_group_count=C_WIDTH)
    y = y + b_dw.astype(xg.dtype)
    y = jax.nn.silu(_layernorm(y, ln_g, ln_b))
    return y, xx[:, -(C_CONV - 1):]


def _trunk(x, c, s_hgrn, s_conv, lb_all, w_ada, b_ada, g_pre_mix, g_post_mix, g_pre_mlp, g_post_mlp,
           w_in, a_ln_g, a_ln_b, a_w_s, a_b_s, b_gn_g, c_w_dw, c_b_dw, c_ln_g, c_ln_b,
           w_out, w_up, w_down):
    sizes = [2 * A_WIDTH, B_KEY_WIDTH, B_KEY_WIDTH, B_WIDTH, B_WIDTH, 2 * C_WIDTH]
    split_at = [int(s) for s in np.cumsum(sizes)[:-1]]
    hg_out, cv_out, v_out = [], [], []
    for l in range(DEPTH):
        mod = jax.nn.silu(c) @ w_ada[l] + b_ada[l]
        sh1, sc1, gt1, sh2, sc2, gt2 = [m[:, None, :] for m in jnp.split(mod, 6, axis=-1)]
        h = _rmsnorm(x, g_pre_mix[l]) * (1.0 + sc1) + sh1
        z = h @ w_in[l]
        za, zq, zf, zi, zg, zc = jnp.split(z, split_at, axis=-1)
        ya, v_rows = _chunk_gmlp(za, a_ln_g[l], a_ln_b[l], a_w_s[l], a_b_s[l])
        yb, S = _hgrn2(zq, zf, zi, zg, lb_all[l], b_gn_g[l], s_hgrn[l])
        yc, buf = _conformer_conv(zc, s_conv[l], c_w_dw[l], c_b_dw[l], c_ln_g[l], c_ln_b[l])
        y = jnp.concatenate([ya, yb, yc], axis=-1) @ w_out[l]
        x = x + gt1 * _rmsnorm(y, g_post_mix[l])
        h = _rmsnorm(x, g_pre_mlp[l]) * (1.0 + sc2) + sh2
        y = jnp.square(jax.nn.relu(h @ w_up[l])) @ w_down[l]
        x = x + gt2 * _rmsnorm(y, g_post_mlp[l])
        hg_out.append(S.astype(s_hgrn.dtype))
        cv_out.append(buf.astype(s_conv.dtype))
        v_out.append(v_rows)
    return x, jnp.stack(hg_out), jnp.stack(cv_out), jnp.stack(v_out)


def setup_inputs(seed: int = 0) -> dict:
    key = jax.random.key(seed)
    ks = iter(jax.random.split(key, 32))
    nrm = lambda shape, s=1.0: jax.random.normal(next(ks), shape, jnp.float32) * s
    return {
        'x_prompt': nrm((BATCH, SEQ, D_MODEL)),
        'x_sample': nrm((DEC_BATCH, DEC_SEQ, D_MODEL)),
        'state_hgrn': nrm((DEPTH, DEC_BATCH, B_HEADS, B_KEY_DIM, B_VAL_DIM), 0.5),
        'state_conv': nrm((DEPTH, DEC_BATCH, C_CONV - 1, C_WIDTH), 0.5),
        'c_prompt': nrm((BATCH, D_MODEL)),
        'c_sample': nrm((DEC_BATCH, D_MODEL)),
        'w_ada': nrm((DEPTH, D_MODEL, 6 * D_MODEL), D_MODEL ** -0.5),
        'b_ada': nrm((DEPTH, 6 * D_MODEL), 0.01),
        'g_pre_mix': 1.0 + nrm((DEPTH, D_MODEL), 0.05),
        'g_post_mix': 1.0 + nrm((DEPTH, D_MODEL), 0.05),
        'g_pre_mlp': 1.0 + nrm((DEPTH, D_MODEL), 0.05),
        'g_post_mlp': 1.0 + nrm((DEPTH, D_MODEL), 0.05),
        'w_in': nrm((DEPTH, D_MODEL, N_IN), D_MODEL ** -0.5),
        'a_ln_g': 1.0 + nrm((DEPTH, A_WIDTH), 0.05),
        'a_ln_b': nrm((DEPTH, A_WIDTH), 0.02),
        'a_w_s': nrm((DEPTH, A_HEADS, A_CHUNK, A_CHUNK), A_CHUNK ** -0.5),
        'a_b_s': 1.0 + nrm((DEPTH, A_HEADS, A_CHUNK), 0.1),
        'b_lb': nrm((DEPTH, B_KEY_WIDTH), 1.0),
        'b_gn_g': 1.0 + nrm((DEPTH, B_WIDTH), 0.05),
        'c_w_dw': nrm((DEPTH, C_CONV, C_WIDTH), C_CONV ** -0.5),
        'c_b_dw': nrm((DEPTH, C_WIDTH), 0.02),
        'c_ln_g': 1.0 + nrm((DEPTH, C_WIDTH), 0.05),
        'c_ln_b': nrm((DEPTH, C_WIDTH), 0.02),
        'w_out': nrm((DEPTH, D_MIX, D_MODEL), D_MIX ** -0.5),
        'w_up': nrm((DEPTH, D_MODEL, D_FF), D_MODEL ** -0.5),
        'w_down': nrm((DEPTH, D_FF, D_MODEL), D_FF ** -0.5),
    }


def reference(x_prompt, x_sample, state_hgrn, state_conv, c_prompt, c_sample,
              w_ada, b_ada, g_pre_mix, g_post_mix, g_pre_mlp, g_post_mlp,
              w_in, a_ln_g, a_ln_b, a_w_s, a_b_s, b_lb, b_gn_g,
              c_w_dw, c_b_dw, c_ln_g, c_ln_b, w_out, w_up, w_down):
    lb_all = jnp.cumsum(jax.nn.softmax(b_lb.astype(jnp.float32), axis=0), axis=0)
    lb_all = lb_all - lb_all[0:1]
    weights = (w_ada, b_ada, g_pre_mix, g_post_mix, g_pre_mlp, g_post_mlp,
               w_in, a_ln_g, a_ln_b, a_w_s, a_b_s, b_gn_g, c_w_dw, c_b_dw, c_ln_g, c_ln_b,
               w_out, w_up, w_down)
    nb = x_prompt.shape[0]
    hg0 = jnp.zeros((DEPTH, nb, B_HEADS, B_KEY_DIM, B_VAL_DIM), x_prompt.dtype)
    cv0 = jnp.zeros((DEPTH, nb, C_CONV - 1, C_WIDTH), x_prompt.dtype)
    y_prompt, hgrn_prompt, conv_prompt, _ = _trunk(x_prompt, c_prompt, hg0, cv0, lb_all, *weights)
    y_sample, hgrn_sample, conv_sample, gmlp_v_sample = _trunk(
        x_sample, c_sample, state_hgrn, state_conv, lb_all, *weights)
    return (y_prompt, y_sample, hgrn_prompt, hgrn_sample, conv_prompt, conv_sample, gmlp_v_sample)
```

```python
import numpy as np
from contextlib import ExitStack
import concourse.bass as bass
import concourse.mybir as mybir
from concourse.bass_utils import run_bass_kernel_spmd

F32 = mybir.dt.float32
BF16 = mybir.dt.bfloat16
AF = mybir.ActivationFunctionType
ALU = mybir.AluOpType
AX = mybir.AxisListType

NCORES = 8
D = 1024
SEQ = 2048
NSEQ_S = 16
LS = 4
TS = NSEQ_S * LS
DEPTH = 2
NIN = 3072
DFF = 4096
EPS = 1e-6
ENGS = ['pe', 'act', 'dve', 'pool', 'sp']
STRICT_SAME_ENGINE = True


class Tok:
    __slots__ = ('name', 'w', 'wl', 'r')

    def __init__(self, name=''):
        self.name = name
        self.w = None
        self.wl = []
        self.r = {}


class Op:
    __slots__ = ('eng', 'fn', 'deps', 'idx', 'sig', 'cnt', 'dma', 'dsem', 'dval')


class Prog:
    def __init__(self):
        self.ops = {e: [] for e in ENGS}
        self.pools = {}
        self.sems = {}
        self.chains = []
        self.cur_chain = None

    def set_dma_pool(self, queue, sems):
        self.pools[queue] = {'slots': [[s, 0, None] for s in sems], 'i': 0}

    def _dep(self, op, d, kind):
        if d is op or d is None:
            return
        if kind == 'waw' and d.dma and op.dma:
            return
        if (not d.dma) and d.eng == op.eng:
            if op.eng == 'pe':
                return
            if kind != 'raw' and not op.dma and not STRICT_SAME_ENGINE:
                return
        op.deps.append(d)

    def begin_chain(self):
        self.cur_chain = []

    def end_chain(self):
        self.chains.append(self.cur_chain)
        self.cur_chain = None

    def merge_chains(self):
        chains, self.chains = self.chains, []
        while any(chains):
            for c in chains:
                if c:
                    a = c.pop(0)
                    self.op(*a[0], **a[1])

    def op(self, eng, fn, reads=(), writes=(), dma=False):
        if getattr(self, 'cur_chain', None) is not None:
            self.cur_chain.append(((eng, fn), dict(reads=list(reads), writes=list(writes), dma=dma)))
            return None
        o = Op()
        o.eng = eng
        o.fn = fn
        o.idx = len(self.ops[eng])
        o.deps = []
        o.sig = False
        o.dma = dma
        o.cnt = 0
        o.dsem = None
        o.dval = 0
        for t in reads:
            self._dep(o, t.w, 'raw')
            for d in t.wl:
                self._dep(o, d, 'raw')
        for t in writes:
            self._dep(o, t.w, 'waw')
            for d in t.wl:
                self._dep(o, d, 'waw')
            for r in t.r.values():
                self._dep(o, r, 'war')
        for t in reads:
            t.r[id(o) if dma else eng] = o
        for t in writes:
            if dma:
                t.wl.append(o)
            else:
                t.w = o
                t.wl = []
            t.r = {}
        if dma:
            pool = self.pools[eng]
            slot = pool['slots'][pool['i']]
            pool['i'] = (pool['i'] + 1) % len(pool['slots'])
            if slot[2] is not None:
                o.deps.append(slot[2])
            slot[1] += 16
            slot[2] = o
            o.dsem = slot[0]
            o.dval = slot[1]
        self.ops[eng].append(o)
        return o

    def dma(self, queue, out, in_, reads=(), writes=(), **kw):
        return self.op(queue, lambda e: e.dma_start(out=out, in_=in_, **kw), reads, writes, dma=True)

    def finalize(self):
        for e in ENGS:
            for o in self.ops[e]:
                for d in o.deps:
                    if not d.dma:
                        d.sig = True
        for e in ENGS:
            c = 0
            for o in self.ops[e]:
                if o.sig and not o.dma:
                    c += 1
                o.cnt = c

    def emit_engine(self, eng, e):
        seen = {}
        for o in self.ops[eng]:
            waits = {}
            for d in o.deps:
                if d.dma:
                    sem, val = d.dsem, d.dval
                else:
                    sem, val = self.sems[d.eng], d.cnt
                k = id(sem)
                if k not in waits or waits[k][1] < val:
                    waits[k] = (sem, val)
            for k, (sem, val) in waits.items():
                if seen.get(k, 0) >= val:
                    continue
                e.wait_ge(sem, val)
                seen[k] = val
            ins = o.fn(e)
            if o.dma:
                ins.then_inc(o.dsem, 16)
            elif o.sig:
                ins.then_inc(self.sems[eng], 1)

    def final_waits(self, eng, e):
        for q, pool in self.pools.items():
            for sem, val, last in pool['slots']:
                if val > 0:
                    e.wait_ge(sem, val)


def AP(t, off, pat):
    return bass.AP(t.tensor, t.offset + off, pat)


def build(debug=None):
    nc = bass.Bass("TRN2", target_bir_lowering=False, dynamic_dma_scratch_size=4096)
    P = Prog()
    es = ExitStack()

    def din(name, shape):
        return nc.dram_tensor(name, list(shape), F32, kind="ExternalInput").ap()

    def dout(name, shape):
        return nc.dram_tensor(name, list(shape), F32, kind="ExternalOutput").ap()

    xp = din("xp", [SEQ, D])
    xs = din("xs", [TS, D])
    sh = din("sh", [DEPTH, NSEQ_S, 8, 64, 64])
    scv = din("scv", [DEPTH, NSEQ_S, 30, 256])
    cin = din("c", [1 + NSEQ_S, D])
    w_ada = din("w_ada", [DEPTH, D, 6 * D])
    b_ada = din("b_ada", [DEPTH, 6 * D])
    g_pre_mix = din("g_pre_mix", [DEPTH, D])
    g_post_mix = din("g_post_mix", [DEPTH, D])
    g_pre_mlp = din("g_pre_mlp", [DEPTH, D])
    g_post_mlp = din("g_post_mlp", [DEPTH, D])
    w_in = din("w_in", [DEPTH, D, NIN])
    a_ln_g = din("a_ln_g", [DEPTH, 256])
    a_ln_b = din("a_ln_b", [DEPTH, 256])
    a_w_s = din("a_w_s", [DEPTH, 4, 128, 128])
    a_b_s = din("a_b_s", [DEPTH, 4, 128])
    b_lb = din("b_lb", [DEPTH, 512])
    b_gn_g = din("b_gn_g", [DEPTH, 512])
    c_w_dw = din("c_w_dw", [DEPTH, 31, 256])
    c_b_dw = din("c_b_dw", [DEPTH, 256])
    c_ln_g = din("c_ln_g", [DEPTH, 256])
    c_ln_b = din("c_ln_b", [DEPTH, 256])
    w_out = din("w_out", [DEPTH, D, D])
    w_up = din("w_up", [DEPTH, D, DFF])
    w_down = din("w_down", [DEPTH, DFF, D])

    yp = dout("yp", [SEQ, D])
    ys = dout("ys", [TS, D])
    hp_o = dout("hp", [DEPTH, 8, 64, 64])
    hs_o = dout("hs", [DEPTH, NSEQ_S, 8, 64, 64])
    cp_o = dout("cp", [DEPTH, 30, 256])
    cs_o = dout("cs", [DEPTH, NSEQ_S, 30, 256])
    gv_o = dout("gv", [DEPTH, NSEQ_S, LS, 256])
    dbg = None
    if debug:
        dbg = dout("dbg", [128, 2048])

    def sb(name, shape, dt=F32):
        return es.enter_context(nc.sbuf_tensor(name, list(shape), dt))

    def sem(name):
        return es.enter_context(nc.semaphore(name))

    for e in ENGS:
        P.sems[e] = sem("s_" + e)
    P.set_dma_pool('sp', [sem("dsp%d" % i) for i in range(16)])
    P.set_dma_pool('pool', [sem("dpl%d" % i) for i in range(16)])
    P.set_dma_pool('act', [sem("dac%d" % i) for i in range(4)])

    ps = es.enter_context(nc.psum_tensor("ps", [128, 8, 512], F32))
    ps_tok = [Tok("ps%d" % i) for i in range(8)]
    ps_rr = [0]

    ps_held = set()

    def psum(hold=False):
        for _ in range(8):
            i = ps_rr[0]
            ps_rr[0] = (i + 1) % 7
            if i not in ps_held:
                break
        else:
            raise RuntimeError("no free PSUM bank")
        if hold:
            ps_held.add(i)
        return ps[:, i, :], ps_tok[i]

    def psum_release(tok):
        ps_held.discard(ps_tok.index(tok))


    W = [sb("W0", [128, 32768], BF16), sb("W1", [128, 32768], BF16)]
    Wt = [[Tok("W%d_%d" % (r, i)) for i in range(8)] for r in range(2)]

    def wtoks(r, c0, c1):
        return [Wt[r][i] for i in range(c0 // 4096, (c1 - 1) // 4096 + 1)]

    ident_f = sb("ident_f", [128, 128], F32)
    ident_b = sb("ident_b", [128, 128], BF16)
    ones_f = sb("ones_f", [128, 128], F32)
    bones_b = sb("bones_b", [128, 128], BF16)
    mask_p = sb("mask_p", [128, 128], F32)
    mask_s = sb("mask_s", [64, 64], F32)
    mres_p = sb("mres_p", [128, 128], F32)
    mres_s = sb("mres_s", [128, 64], F32)
    mhl = sb("mhl", [128, 2], F32)
    oneh = sb("oneh", [64, 16], F32)
    oneh16 = sb("oneh16", [128, 8], F32)
    mpar = sb("mpar", [128, 2], F32)
    t_const = Tok("const")
    C_ = lambda fn: P.op('pool', fn, reads=[t_const], writes=[t_const])
    C_(lambda e: e.memset(ident_f[:, :], 1.0))
    C_(lambda e: e.affine_select(ident_f[:, :], ident_f[:, :], [[-1, 128]], ALU.is_equal, 0.0, base=0, channel_multiplier=1))
    C_(lambda e: e.memset(ones_f[:, :], 1.0))
    C_(lambda e: e.memset(bones_b[:, :], 0.0))
    C_(lambda e: e.memset(bones_b[0:64, 0:64], 1.0))
    C_(lambda e: e.memset(bones_b[64:128, 64:128], 1.0))
    C_(lambda e: e.memset(mask_p[:, :], 1.0))
    C_(lambda e: e.affine_select(mask_p[:, :], mask_p[:, :], [[1, 128]], ALU.is_ge, 0.0, base=0, channel_multiplier=-1))
    C_(lambda e: e.affine_select(mask_p[:, :], mask_p[:, :], [[-16, 8], [0, 16]], ALU.is_ge, 0.0, base=0, channel_multiplier=1))
    C_(lambda e: e.memset(mask_s[:, :], 1.0))
    C_(lambda e: e.affine_select(mask_s[:, :], mask_s[:, :], [[1, 64]], ALU.is_ge, 0.0, base=0, channel_multiplier=-1))
    C_(lambda e: e.affine_select(mask_s[:, :], mask_s[:, :], [[-4, 16], [0, 4]], ALU.is_ge, 0.0, base=0, channel_multiplier=1))
    C_(lambda e: e.memset(mres_p[:, :], 1.0))
    C_(lambda e: e.memset(mres_p[:, :].rearrange("p (c i) -> p c i", i=16)[:, :, 0:1], 0.0))
    C_(lambda e: e.memset(mres_s[:, :], 1.0))
    C_(lambda e: e.memset(mres_s[:, :].rearrange("p (c i) -> p c i", i=4)[:, :, 0:1], 0.0))
    C_(lambda e: e.memset(mhl[:, :], 0.0))
    C_(lambda e: e.memset(mhl[0:64, 0:1], 1.0))
    C_(lambda e: e.memset(mhl[64:128, 1:2], 1.0))
    C_(lambda e: e.memset(oneh[:, :], 1.0))
    C_(lambda e: e.affine_select(oneh[:, :], oneh[:, :], [[-4, 16]], ALU.is_ge, 0.0, base=0, channel_multiplier=1))
    C_(lambda e: e.affine_select(oneh[:, :], oneh[:, :], [[4, 16]], ALU.is_ge, 0.0, base=3, channel_multiplier=-1))
    C_(lambda e: e.memset(oneh16[:, :], 1.0))
    C_(lambda e: e.affine_select(oneh16[:, :], oneh16[:, :], [[-16, 8]], ALU.is_ge, 0.0, base=0, channel_multiplier=1))
    C_(lambda e: e.affine_select(oneh16[:, :], oneh16[:, :], [[16, 8]], ALU.is_ge, 0.0, base=15, channel_multiplier=-1))
    P.op('dve', lambda e: e.tensor_reduce(mpar[:, 0:1], oneh16[:, 0:8:2], AX.X, ALU.add), reads=[t_const], writes=[t_const])
    P.op('dve', lambda e: e.tensor_reduce(mpar[:, 1:2], oneh16[:, 1:8:2], AX.X, ALU.add), reads=[t_const], writes=[t_const])
    P.op('dve', lambda e: e.tensor_copy(ident_b[:, :], ident_f[:, :]), reads=[t_const], writes=[t_const])

    t_par = Tok("params")

    def fm_load(name, src, nch):
        t = sb(name, [128, DEPTH, nch], F32)
        for l in range(DEPTH):
            P.dma('sp', t[:, l, :], src[l].rearrange("(k p) -> p k", p=128), writes=[t_par],
                  allow_slow_non_contiguous=True)
        return t

    gpm = fm_load("gpm", g_pre_mix, 8)
    gpo = fm_load("gpo", g_post_mix, 8)
    gpl = fm_load("gpl", g_pre_mlp, 8)
    gpol = fm_load("gpol", g_post_mlp, 8)
    gn = fm_load("gn", b_gn_g, 4)
    lbr = fm_load("lbr", b_lb, 4)
    cb = fm_load("cb", c_b_dw, 2)
    clg = fm_load("clg", c_ln_g, 2)
    clb = fm_load("clb", c_ln_b, 2)
    cw = sb("cw", [128, DEPTH, 2, 31], F32)
    for l in range(DEPTH):
        for cc in range(2):
            P.dma('sp', cw[:, l, cc, :], c_w_dw[l][:, cc * 128:(cc + 1) * 128].rearrange("j p -> p j"),
                  writes=[t_par], allow_slow_non_contiguous=True)
    lb1 = sb("lb1", [128, 4], F32)
    oml1 = sb("oml1", [128, 4], F32)
    P.op('dve', lambda e: e.tensor_tensor(lb1[:, :], lbr[:, 1, :], lbr[:, 0, :], ALU.subtract), reads=[t_par], writes=[t_par])
    P.op('act', lambda e: e.activation(lb1[:, :], lb1[:, :], AF.Sigmoid), reads=[t_par], writes=[t_par])
    P.op('dve', lambda e: e.tensor_scalar(oml1[:, :], lb1[:, :], -1.0, 1.0, ALU.mult, ALU.add), reads=[t_par], writes=[t_par])
    lng = sb("lng", [128, 256], F32)
    lnb = sb("lnb", [128, 256], F32)
    t_ln = Tok("ln")

    NS = 1 + NSEQ_S
    modT = sb("modT", [128, DEPTH, 48, NS], F32)
    t_mod = Tok("modT")
    t2 = sb("t2", [128, D], F32)
    xn = sb("xn", [128, D], BF16)
    c_sb = t2[0:NS, :]
    c_bf = xn[0:NS, :]
    cT = sb("cT", [128, 8, NS], BF16)
    badaT = sb("badaT", [128, DEPTH, 48], F32)
    t_c = Tok("c")
    t_cT = Tok("cT")
    P.dma('sp', c_sb, cin[:, :], writes=[t_c])
    for l in range(DEPTH):
        P.dma('sp', badaT[:, l, :], b_ada[l].rearrange("(j p) -> p j", p=128), writes=[t_par],
              allow_slow_non_contiguous=True)
    P.op('act', lambda e: e.activation(c_bf, c_sb, AF.Silu), reads=[t_c], writes=[t_c])
    pst, pstok = psum()
    pst_b = pst.bitcast(BF16)
    for k in range(8):
        P.op('pe', lambda e, k=k: e.transpose(pst_b[0:128, k * 32:k * 32 + NS], xn[0:NS, k * 128:(k + 1) * 128],
                                              ident_b[0:NS, 0:NS]),
             reads=[t_c, t_const], writes=[pstok])
    P.op('dve', lambda e: e.tensor_copy(cT[:, :, :], pst_b[:, 0:256].rearrange("p (k s) -> p k s", s=32)[:, :, 0:NS]),
         reads=[pstok], writes=[t_cT])

    def load_w_in_out(l, r):
        wv = w_in[l].rearrange("(k p) n -> p k n", p=128)
        for k in range(8):
            P.dma('pool', W[r][:, k * 3072:(k + 1) * 3072], wv[:, k, :], writes=wtoks(r, k * 3072, (k + 1) * 3072),
                  max_dma_last_dim=4096)
        wo = w_out[l].rearrange("(k p) n -> p k n", p=128)
        for k in range(8):
            c0 = 24576 + k * 1024
            P.dma('pool', W[r][:, c0:c0 + 1024], wo[:, k, :], writes=wtoks(r, c0, c0 + 1024))

    def fold_gn(l, r):
        for hp in range(4):
            c0 = 24576 + (2 + hp) * 1024
            P.op('dve', lambda e, c0=c0, hp=hp: e.tensor_scalar(W[r][:, c0:c0 + 1024], W[r][:, c0:c0 + 1024],
                                                                gn[:, l, hp:hp + 1], None, ALU.mult),
                 reads=wtoks(r, c0, c0 + 1024) + [t_par], writes=wtoks(r, c0, c0 + 1024))

    def load_w_up(l, r, ks=range(8)):
        wv = w_up[l].rearrange("(k p) n -> p k n", p=128)
        for k in ks:
            for hf in range(2):
                c0 = k * 4096 + hf * 2048
                P.dma('pool', W[r][:, c0:c0 + 2048], wv[:, k, hf * 2048:(hf + 1) * 2048], writes=[Wt[r][k]],
                      max_dma_last_dim=4096)

    def load_w_down(l, r, js=range(32)):
        wv = w_down[l].rearrange("(j p) n -> p j n", p=128)
        for j in js:
            P.dma('pool', W[r][:, j * 1024:(j + 1) * 1024], wv[:, j, :], writes=[Wt[r][j // 4]])

    def mod_pieces(l, pieces=range(12)):
        wv = w_ada[l].rearrange("(k p) n -> p k n", p=128)
        for piece in pieces:
            ri = piece % 3
            buf = W[1][:, ri * 4096:(ri + 1) * 4096].rearrange("p (k n) -> p k n", k=8)
            tk = Wt[1][ri]
            P.dma('pool', buf[:, :, :], wv[:, :, piece * 512:(piece + 1) * 512], writes=[tk])
            pt, ptk = psum()
            for jj in range(4):
                col = jj * NS
                for k in range(8):
                    P.op('pe', lambda e, pt=pt, buf=buf, jj=jj, k=k, col=col: e.matmul(
                        pt[:, col:col + NS], buf[:, k, jj * 128:(jj + 1) * 128], cT[:, k, :],
                        start=(k == 0), stop=(k == 7)), reads=[tk, t_cT], writes=[ptk])
            P.op('dve', lambda e, pt=pt, piece=piece: e.tensor_tensor(
                modT[:, l, piece * 4:(piece + 1) * 4, :],
                pt[:, 0:4 * NS].rearrange("p (j s) -> p j s", s=NS),
                badaT[:, l, piece * 4:(piece + 1) * 4].unsqueeze(2).to_broadcast([128, 4, NS]),
                ALU.add), reads=[ptk, t_par], writes=[t_mod])
            yield

    for _ in mod_pieces(0, range(4)):
        pass
    load_w_in_out(0, 0)

    def _bg_pieces():
        yield from mod_pieces(0, range(4, 12))
        yield from mod_pieces(1)
    mod1_gen = _bg_pieces()
    mod1_state = {'done': False}

    def mod1_bg():
        if mod1_state['done']:
            return
        try:
            next(mod1_gen)
            next(mod1_gen)
        except StopIteration:
            mod1_state['done'] = True
            load_w_up(0, 1, range(6))

    xscr = nc.dram_tensor("xscr", [SEQ + TS, D], F32).ap()
    t_x = [Tok("x%d" % i) for i in range(17)]

    cur = {'first': False, 'last': False}

    def x_src(l, ph, ti):
        if cur['first']:
            return xp[ti * 128:(ti + 1) * 128, :] if ti < 16 else xs[:, :]
        return xscr[ti * 128:(ti + 1) * 128, :] if ti < 16 else xscr[SEQ:SEQ + TS, :]

    def x_dst(l, ph, ti):
        if cur['last']:
            return yp[ti * 128:(ti + 1) * 128, :] if ti < 16 else ys[:, :]
        return xscr[ti * 128:(ti + 1) * 128, :] if ti < 16 else xscr[SEQ:SEQ + TS, :]

    xb = [sb("xb%d" % i, [128, 1, D], F32) for i in range(2)]
    t_xb = [Tok("xb0"), Tok("xb1")]
    t_xn = Tok("xn")
    t_t2 = Tok("t2")
    junk = t2[:, 0:512].bitcast(BF16)
    stat = sb("stat", [128, 16], F32)
    t_stat = Tok("stat")
    t_stat_glob = t_stat
    t_stat_glob2 = Tok("stat_warm")
    t_stat_pre = Tok("stat_pre")
    t_stat_lnv = Tok("stat_lnv")
    hT = [sb("hT%d" % i, [128, 8, 128], BF16) for i in range(2)]
    t_hT = [Tok("hT0"), Tok("hT1")]
    gsT = sb("gsT", [128, 8, NS], F32)
    shT = sb("shT", [128, 8, NS], F32)
    t_gs = Tok("gs")
    t_ggT = Tok("ggT")
    ggT = sb("ggT", [128, 8, NS], F32)
    gg = sb("gg", [128, D], F32)
    t_gg = Tok("gg")

    def setup_mod(l, ph, part=3):
        base = 3 * ph
        gpre = gpm if ph == 0 else gpl
        gpost = gpo if ph == 0 else gpol
        if part & 1:
            P.op('dve', lambda e: e.tensor_scalar(gsT[:, :, :], modT[:, l, (base + 1) * 8:(base + 2) * 8, :], 1.0, None, ALU.add),
                 reads=[t_mod], writes=[t_gs])
            P.op('dve', lambda e: e.tensor_tensor(gsT[:, :, :], gsT[:, :, :], gpre[:, l, :].unsqueeze(2).to_broadcast([128, 8, NS]), ALU.mult),
                 reads=[t_gs, t_par], writes=[t_gs])
            P.op('dve', lambda e: e.tensor_copy(shT[:, :, :], modT[:, l, base * 8:(base + 1) * 8, :]), reads=[t_mod], writes=[t_gs])
        if part & 2:
            P.op('dve', lambda e: e.tensor_tensor(ggT[:, :, :], modT[:, l, (base + 2) * 8:(base + 3) * 8, :],
                                                  gpost[:, l, :].unsqueeze(2).to_broadcast([128, 8, NS]), ALU.mult),
                 reads=[t_mod, t_par], writes=[t_ggT])

    def build_gate(sample):
        M = TS if sample else 128
        rep = t2[:, :].rearrange("p (k m) -> p k m", k=8)
        if sample:
            P.op('dve', lambda e: e.tensor_copy(
                rep[:, :, 0:TS].rearrange("p k (n t) -> p k n t", t=LS),
                ggT[:, :, 1:NS].unsqueeze(3).to_broadcast([128, 8, NSEQ_S, LS])), reads=[t_ggT], writes=[t_t2])
        else:
            P.op('dve', lambda e: e.tensor_copy(rep[:, :, :], ggT[:, :, 0:1].to_broadcast([128, 8, 128])),
                 reads=[t_ggT], writes=[t_t2])
        for half in range(2):
            pt, ptk = psum()
            for kk in range(4):
                k = half * 4 + kk
                P.op('pe', lambda e, pt=pt, k=k, kk=kk: e.transpose(pt[0:M, kk * 128:(kk + 1) * 128], rep[:, k, 0:M], ident_f[:, :]),
                     reads=[t_t2, t_const], writes=[ptk])
            P.op('act', lambda e, pt=pt, half=half: e.copy(gg[0:M, half * 512:(half + 1) * 512], pt[0:M, :]),
                 reads=[ptk], writes=[t_gg])

    def load_x(l, ph, ti, par, j):
        npart = TS if ti == 16 else 128
        P.dma('sp', xb[par][0:npart, j, :], x_src(l, ph, ti), reads=[t_x[ti]], writes=[t_xb[par]] + ([T_["xxT"]] if par == 2 else []))

    def rstd_from(ssum_ap, out_ap, n, np_, tk=None):
        t_stat = tk if tk is not None else t_stat_glob
        P.op('act', lambda e: e.activation(out_ap, ssum_ap, AF.Ln, bias=EPS, scale=1.0 / n), reads=[t_stat], writes=[t_stat])
        P.op('act', lambda e: e.activation(out_ap, out_ap, AF.Exp, scale=-0.5), reads=[t_stat], writes=[t_stat])

    def prenorm(par, j, hpar, col0, sample, part=3):
        np_ = TS if sample else 128
        xt = xb[par][0:np_, j, :]
        if part & 1:
            prenorm_a(par, j, sample)
        if part & 2:
            prenorm_b(hpar, col0, sample, evac_dve=True)

    def prenorm_a(par, j, sample):
        np_ = TS if sample else 128
        xt = xb[par][0:np_, j, :]
        P.op('act', lambda e: e.activation(xn[0:np_, :], xt, AF.Square, accum_out=stat[0:np_, 0:1]),
             reads=[t_xb[par]], writes=[t_xn, t_stat_pre])
        rstd_from(stat[0:np_, 0:1], stat[0:np_, 1:2], D, np_, t_stat_pre)
        P.op('dve', lambda e: e.tensor_scalar(xn[0:np_, :], xt, stat[0:np_, 1:2], None, ALU.mult),
             reads=[t_xb[par], t_stat_pre], writes=[t_xn])

    def prenorm_b(hpar, col0, sample, evac_dve=False):
        np_ = TS if sample else 128
        pt, ptk = psum()
        ptb = pt.bitcast(BF16)
        for k in range(8):
            P.op('pe', lambda e, k=k: e.transpose(ptb[:, k * 128:k * 128 + np_], xn[0:np_, k * 128:(k + 1) * 128],
                                                  ident_b[0:np_, 0:np_]), reads=[t_xn, t_const], writes=[ptk])
        if not sample and evac_dve:
            for k in range(8):
                P.op('dve', lambda e, k=k: e.tensor_scalar(hT[hpar][:, k, col0:col0 + 128], ptb[:, k * 128:(k + 1) * 128],
                                                           gsT[:, k, 0:1], shT[:, k, 0:1], ALU.mult, ALU.add),
                     reads=[ptk, t_gs], writes=[t_hT[hpar]])
        elif not sample:
            for k in range(8):
                P.op('act', lambda e, k=k: e.activation(hT[hpar][:, k, col0:col0 + 128], ptb[:, k * 128:(k + 1) * 128],
                                                        AF.Identity, bias=shT[:, k, 0:1], scale=gsT[:, k, 0:1]),
                     reads=[ptk, t_gs], writes=[t_hT[hpar]])
        else:
            hv = hT[hpar][:, :, col0:col0 + TS].rearrange("p k (n t) -> p k n t", t=LS)
            pv = ptb[:, :].rearrange("p (k m) -> p k m", k=8)[:, :, 0:TS].rearrange("p k (n t) -> p k n t", t=LS)
            tmpv = t2[:, 0:512].rearrange("p (k n t) -> p k n t", k=8, t=LS)
            P.op('dve', lambda e: e.tensor_tensor(tmpv, pv, gsT[:, :, 1:NS].unsqueeze(3).to_broadcast([128, 8, NSEQ_S, LS]), ALU.mult),
                 reads=[ptk, t_gs], writes=[t_t2])
            P.op('dve', lambda e: e.tensor_tensor(hv, tmpv, shT[:, :, 1:NS].unsqueeze(3).to_broadcast([128, 8, NSEQ_S, LS]), ALU.add),
                 reads=[t_t2, t_gs], writes=[t_hT[hpar]])

    def postnorm_store(l, ph, ti, par, j, ybanks, sample):
        np_ = TS if sample else 128
        for h2 in range(2):
            pt, ptk = ybanks[h2]
            P.op('act', lambda e, pt=pt, h2=h2: e.activation(junk[0:np_, h2 * 512:(h2 + 1) * 512], pt[0:np_, :], AF.Square,
                                                             accum_out=stat[0:np_, 4 + h2:5 + h2]),
                 reads=[ptk], writes=[t_t2, t_stat])
        P.op('dve', lambda e: e.tensor_tensor(stat[0:np_, 6:7], stat[0:np_, 4:5], stat[0:np_, 5:6], ALU.add), reads=[t_stat], writes=[t_stat])
        rstd_from(stat[0:np_, 6:7], stat[0:np_, 7:8], D, np_)
        for h2 in range(2):
            pt, ptk = ybanks[h2]
            P.op('dve', lambda e, pt=pt, h2=h2: e.tensor_tensor(t2[0:np_, h2 * 512:(h2 + 1) * 512], pt[0:np_, :],
                                                                gg[0:np_, h2 * 512:(h2 + 1) * 512], ALU.mult),
                 reads=[ptk, t_gg], writes=[t_t2])
        xt = xb[par][0:np_, j, :]
        P.op('dve', lambda e: e.scalar_tensor_tensor(xt, t2[0:np_, :], stat[0:np_, 7:8], xt, ALU.mult, ALU.add),
             reads=[t_t2, t_stat, t_xb[par]], writes=[t_xb[par]])
        P.dma('sp', x_dst(l, ph, ti), xt, reads=[t_xb[par]], writes=[t_x[ti]])

    arena = sb("arena", [128, 4160], F32)
    ffT = arena[:, 0:2048].bitcast(BF16).rearrange("p (j t) -> p j t", j=32)
    t_S0b = Tok("S0b")
    t_ff = t_S0b
    rbuf = [arena[:, 2048:2304].bitcast(BF16), arena[:, 2304:2560].bitcast(BF16)]
    t_rb = [Tok("rb0"), Tok("rb1")]

    def phase_mlp(l, ru, rd, after_last_up=None):
        setup_mod(l, 1)
        wup = W[ru][:, :].rearrange("p (k n) -> p k n", k=8)
        wdn = W[rd][:, :].rearrange("p (j n) -> p j n", j=32)
        blocks = [(ti, 1, ti == 16) for ti in range(17)]
        build_gate(False)

        def up(bi, jfs):
            ti0, nt, sample = blocks[bi]
            par = bi % 2
            T = TS if sample else 128
            for jg in jfs:
                pt, ptk = psum()
                for jj in range(4):
                    jf = jg * 4 + jj
                    for k in range(8):
                        P.op('pe', lambda e, pt=pt, k=k, jf=jf, jj=jj, T=T, par=par: e.matmul(
                            pt[:, jj * 128:jj * 128 + T], wup[:, k, jf * 128:(jf + 1) * 128], hT[par][:, k, 0:T], start=(k == 0), stop=(k == 7)),
                            reads=[Wt[ru][k], t_hT[par]], writes=[ptk])
                rp = jg % 2
                pv_ = pt[:, :].rearrange("p (j t) -> p j t", j=4)[:, :, 0:T]
                rv_ = rbuf[rp][:, :].rearrange("p (j t) -> p j t", j=4)[:, :, 0:T]
                P.op('act', lambda e, pv_=pv_, rv_=rv_: e.activation(rv_, pv_, AF.Relu), reads=[ptk], writes=[t_rb[rp], T_["S0f"]])
                P.op('dve', lambda e, jg=jg, T=T, rv_=rv_: e.tensor_tensor(ffT[:, jg * 4:(jg + 1) * 4, 0:T], rv_, rv_, ALU.mult),
                     reads=[t_rb[rp]], writes=[t_ff])

        nb = len(blocks)
        load_x(l, 1, 0, 0, 0)
        load_x(l, 1, 1, 1, 0)
        prenorm(0, 0, 0, 0, False)
        for bi, (ti0, nt, sample) in enumerate(blocks):
            par = bi % 2
            xpar = bi % 3
            if bi + 2 < nb:
                load_x(l, 1, bi + 2, (bi + 2) % 3, 0)
            if bi + 1 < nb:
                prenorm_a((bi + 1) % 3, 0, blocks[bi + 1][2])
            up(bi, range(0, 4))
            if bi + 1 < nb:
                prenorm_b((bi + 1) % 2, 0, blocks[bi + 1][2], evac_dve=True)
            up(bi, range(4, 8))
            if bi + 1 == nb and after_last_up is not None:
                after_last_up()
            if sample:
                build_gate(True)
            np_ = TS if sample else 128
            yb = [psum(), psum()]
            for h2 in range(2):
                pt, ptk = yb[h2]
                for jf in range(32):
                    P.op('pe', lambda e, pt=pt, jf=jf, h2=h2, np_=np_: e.matmul(
                        pt[0:np_, :], ffT[:, jf, 0:np_], wdn[:, jf, h2 * 512:(h2 + 1) * 512],
                        start=(jf == 0), stop=(jf == 31)), reads=[t_ff, Wt[rd][jf // 4]], writes=[ptk])
            postnorm_store(l, 1, ti0, xpar, 0, yb, sample)

    LN8 = float(np.log(8.0))
    sigc_ = sb("sigc", [128, 2, 128], F32)
    sigc = sigc_
    uT = sb("uT", [128, 2, 128], BF16)
    vg = sb("vg", [128, 256], F32)
    vln = sb("vln", [128, 256], BF16)
    sqT = sb("sqT", [128, 4, 128], BF16)
    fT = sb("fT", [128, 4, 128], F32)
    lfT = sb("lfT", [128, 4, 128], F32)
    bT = sb("bT", [128, 4, 128], F32)
    r1 = sb("r1", [128, 4, 128], F32)
    t1 = r1
    osq = sqT
    acc2 = sigc_
    gT = sb("gT", [128, 4, 16], F32)
    qinT = sb("qinT", [128, 4, 128], BF16)
    kinT = sb("kinT", [128, 4, 128], BF16)
    kdT = sb("kdT", [128, 4, 128], BF16)
    qinTz = sb("qinTz", [128, 4, 2, 128], BF16)
    sgT = sb("sgT", [128, 4, 128], BF16)
    v_tok = sb("v_tok", [128, 512], BF16)
    kd_tok = sb("kd_tok", [128, 512], BF16)
    PT = sb("PT", [128, 2, 4, 128], BF16)
    S = sb("S", [128, 4, 64], F32)
    Sb = sb("Sb", [128, 8, 4, 64], BF16)
    kdmp = sb("kdmp", [128, 2, 512], BF16)
    ycatT = sb("ycatT", [128, 8, 128], BF16)
    xgT = [sb("xgT%d" % i, [128, 2, 30 + 128], BF16) for i in range(2)]
    Sg = sb("Sg", [128, 4, 64], F32)
    acc = sb("acc", [128, 2, 128], F32)
    cst = sb("cst", [128, 3, 128], F32)
    wsT = sb("wsT", [128, 4, 128], BF16)
    wsTf = bT
    bs_row = arena[0:1, 2048:2560].rearrange("p (h t) -> p h t", h=4)
    wsTs = sb("wsTs", [64, 4, 64], BF16)
    wsTsf = lfT[0:64, :, 0:64]
    bs_row_s4 = arena[0:1, 2560:2816].rearrange("p (c a b) -> p c a b", c=2, a=2)
    xg_tok = acc[:, :, :].rearrange("p c t -> p (c t)")[0:64, :]
    sig_tok = cst[:, 0:2, :].rearrange("p c t -> p (c t)")[0:64, :]
    S0b = arena[:, 0:2048].bitcast(BF16).rearrange("p (n h v) -> p n h v", n=16, v=64)
    S0f = arena[:, 2048:3072].rearrange("p (n h v) -> p n h v", n=4, v=64)
    xxT = arena[:, 3072:4160].rearrange("p (c n j) -> p c n j", c=2, j=34)
    kdm = kdmp[0:64, :, :]
    scv_t = t2[0:120, :].rearrange("p (g c) -> p g c", g=4)
    tmpc = lfT[:, :, :].rearrange("p h t -> p (h t)")[:, 0:496].rearrange("p (n j) -> p n j", j=31)
    T_ = {n: Tok(n) for n in ["uT", "vg", "vln", "sqT", "fT", "lfT", "bT", "gT", "qinT", "kinT", "kdT", "qinTz", "sgT",
                              "v_tok", "kd_tok", "PT", "S", "Sb", "osq", "r1", "t1", "ycatT", "xg0", "xg1", "sigc", "acc",
                              "acc2", "cst", "ws", "wss", "xg_tok", "S0f", "S0b", "kdm", "xxT", "scv_t", "tmpc"]}
    T_["S0b"] = t_S0b
    for _n in ("ycat_a", "ycat_b", "ycat_c"):
        T_[_n] = Tok(_n)
    T_["Sg"] = Tok("Sg")
    T_["S0"] = T_["S"]
    T_["S1"] = Tok("S2")
    T_["t1"] = T_["r1"]
    T_["osq"] = T_["sqT"]
    T_["acc2"] = T_["sigc"]
    T_["scv_t"] = t_t2
    T_["tmpc"] = T_["lfT"]

    def prep_layer_mixer(l, rw):
        Dg = W[1 - rw][:, 24576:32768].rearrange("p (i c) -> p i c", c=128)
        dgtk = [Wt[1 - rw][6], Wt[1 - rw][7]]
        P.op('dve', lambda e: e.tensor_tensor(Dg[:, 0:62, :], ident_b[:, :].unsqueeze(1).to_broadcast([128, 62, 128]),
                                              cw[:, l, :, :].rearrange("p c j -> p (c j)").unsqueeze(2).to_broadcast([128, 62, 128]), ALU.mult),
             reads=[t_const, t_par], writes=dgtk)

    def prep_layer_mixer_b(l, rw):
        P.dma('sp', lng[:, :], a_ln_g[l:l + 1, :].to_broadcast([128, 256]), writes=[t_ln])
        P.dma('sp', lnb[:, :], a_ln_b[l:l + 1, :].to_broadcast([128, 256]), writes=[t_ln])
        P.dma('sp', wsTf[:, :, :], a_w_s[l].rearrange("h t s -> t h s"), writes=[T_["ws"], T_["bT"]])
        pt, ptk = psum()
        for h in range(4):
            P.op('pe', lambda e, h=h: e.transpose(pt[:, h * 128:(h + 1) * 128], wsTf[:, h, :], ident_f[:, :]),
                 reads=[T_["ws"], T_["bT"], t_const], writes=[ptk])
        P.op('dve', lambda e: e.tensor_copy(wsTf[:, :, :].rearrange("s h t -> s (h t)"), pt[:, :]), reads=[ptk], writes=[T_["ws"], T_["bT"]])
        P.op('pool', lambda e: e.affine_select(wsT[:, :, :], wsTf[:, :, :], [[0, 4], [1, 128]], ALU.is_ge, 0.0, base=0,
                                               channel_multiplier=-1), reads=[T_["ws"], T_["bT"]], writes=[T_["ws"]])
        P.dma('sp', bs_row, a_b_s[l:l + 1, :, :], writes=[T_["ws"], T_["S0f"]])
        P.op('pool', lambda e: e.memset(wsTsf, 0.0), writes=[T_["wss"], T_["lfT"]])
        for n in range(NSEQ_S):
            P.dma('sp', lfT[4 * n:4 * n + 4, :, 4 * n:4 * n + 4], wsTf[0:4, :, 0:4], reads=[T_["ws"], T_["bT"]],
                  writes=[T_["wss"], T_["lfT"]])
        P.op('dve', lambda e: e.tensor_tensor(wsTs[:, :, :], wsTsf, mask_s[:, :].unsqueeze(1).to_broadcast([64, 4, 64]), ALU.mult),
             reads=[T_["wss"], T_["lfT"], t_const], writes=[T_["wss"]])


    _a = arena
    FS = [(uT, vg, vln, sqT, fT, sgT, v_tok),
          (_a[:, 0:128].bitcast(BF16).rearrange("p (c t) -> p c t", c=2),
           _a[:, 128:384],
           _a[:, 384:512].bitcast(BF16),
           _a[:, 512:768].bitcast(BF16).rearrange("p (c t) -> p c t", c=4),
           _a[:, 768:1280].rearrange("p (c t) -> p c t", c=4),
           _a[:, 1280:1536].bitcast(BF16).rearrange("p (c t) -> p c t", c=4),
           _a[:, 1536:1792].bitcast(BF16))]
    FTOK = [{k: T_[k] for k in ["uT", "vg", "vln", "sqT", "fT", "sgT", "v_tok"]},
            {k: Tok(k + "_1") for k in ["uT", "vg", "vln", "sqT", "fT", "sgT", "v_tok"]}]
    xb.append(arena[:, 3072:4096].rearrange("p (j d) -> p j d", j=1))
    t_xb.append(Tok("xb2"))

    def mixer_tile(l, rw, ti, sample, last_prompt):
        Dg = W[1 - rw][:, 24576:32768].rearrange("p (i c) -> p i c", c=128)
        dgtk = [Wt[1 - rw][6], Wt[1 - rw][7]]
        par = ti % 2
        xpar = ti % 3
        T = TS if sample else 128
        C = LS if sample else 16
        nch = T // C
        win = W[rw][:, 0:24576].rearrange("p (k n) -> p k n", k=8)
        wout = W[rw][:, 24576:32768].rearrange("p (k n) -> p k n", k=8)
        wtk = [Wt[rw][i] for i in range(6)]
        wotk = [Wt[rw][6], Wt[rw][7]]
        hpar = par
        h_ = hT[hpar]
        th = t_hT[hpar]
        mask = mask_s if sample else mask_p
        mres = mres_s if sample else mres_p
        xg = xgT[par]
        txg = T_["xg%d" % par]

        def fm_group(c0, nchunk):
            pt, ptk = psum()
            for jc in range(nchunk):
                for k in range(8):
                    P.op('pe', lambda e, pt=pt, jc=jc, k=k: e.matmul(pt[:, jc * 128:jc * 128 + T], win[:, k, c0 + jc * 128:c0 + (jc + 1) * 128],
                                                                     h_[:, k, 0:T], start=(k == 0), stop=(k == 7)),
                         reads=wtk + [th], writes=[ptk])
            return pt, ptk

        def tm_group(c0, ncol, t0=0, tn=None):
            tn = T if tn is None else tn
            pt, ptk = psum()
            for k in range(8):
                P.op('pe', lambda e, pt=pt, k=k: e.matmul(pt[0:tn, 0:ncol], h_[:, k, t0:t0 + tn], win[:, k, c0:c0 + ncol],
                                                          start=(k == 0), stop=(k == 7)), reads=wtk + [th], writes=[ptk])
            return pt, ptk

        def v3(pt, n):
            return pt[:, 0:n * 128].rearrange("p (c t) -> p c t", t=128)[:, :, 0:T]

        uT, vg, vln, sqT, fT, sgT, v_tok = FS[par]
        osq = sqT
        TT = dict(T_)
        TT.update(FTOK[par])
        TT["osq"] = TT["sqT"]
        prenorm(xpar, 0, par, 0, sample)
        yield
        pq, pqk = fm_group(512, 4)
        P.op('act', lambda e: e.activation(sqT[:, :, 0:T], v3(pq, 4), AF.Silu), reads=[pqk], writes=[TT["sqT"]])
        pg, pgk = fm_group(2048, 4)
        P.op('act', lambda e: e.activation(sgT[:, :, 0:T], v3(pg, 4), AF.Silu), reads=[pgk], writes=[TT["sgT"]])
        pu, puk = fm_group(0, 2)
        P.op('act', lambda e: e.activation(uT[:, :, 0:T], v3(pu, 2), AF.Gelu), reads=[puk], writes=[TT["uT"]])
        pv, pvk = tm_group(256, 256)
        P.op('act', lambda e: e.activation(vg[0:T, :], pv[0:T, 0:256], AF.Gelu), reads=[pvk], writes=[TT["vg"]])
        pf, pfk = fm_group(1024, 4)
        P.op('act', lambda e: e.activation(fT[:, :, 0:T], v3(pf, 4), AF.Sigmoid), reads=[pfk], writes=[TT["fT"]])
        pc, pck = fm_group(2560, 4)
        P.op('act', lambda e: e.activation(sigc[:, :, 0:T], v3(pc, 4)[:, 2:4, :], AF.Sigmoid), reads=[pck], writes=[TT["sigc"]])
        pi_, pik = tm_group(1536, 512)
        P.op('act', lambda e: e.copy(v_tok[0:T, :], pi_[0:T, :]), reads=[pik], writes=[TT["v_tok"]])
        if sample or last_prompt:
            t0, tn = (0, TS) if sample else (96, 32)
            pz, pzk = tm_group(2560, 512, t0, tn)
            P.op('act', lambda e: e.activation(sig_tok[0:tn, :], pz[0:tn, 256:512], AF.Sigmoid), reads=[pzk], writes=[TT["cst"]])
        P.op('act', lambda e: e.activation(stat[:, 2:3], ones_f[:, 0:1], AF.Ln), reads=[t_const], writes=[t_stat_glob2])
        yield
        if sample:
            xgdst = xxT[:, :, :, 30:34]
            P.op('dve', lambda e: e.tensor_tensor(xgdst, v3(pc, 4)[:, 0:2, :].rearrange("p c (n t) -> p c n t", t=LS),
                                                  sigc[:, :, 0:T].rearrange("p c (n t) -> p c n t", t=LS), ALU.mult),
                 reads=[pck, TT["sigc"]], writes=[TT["xxT"], t_xb[2]])
        else:
            P.op('dve', lambda e: e.tensor_tensor(xg[:, :, 30:30 + T], v3(pc, 4)[:, 0:2, :], sigc[:, :, 0:T], ALU.mult),
                 reads=[pck, TT["sigc"]], writes=[txg])
        if sample or last_prompt:
            P.op('dve', lambda e: e.tensor_tensor(xg_tok[0:tn, :], pz[0:tn, 0:256], sig_tok[0:tn, :], ALU.mult),
                 reads=[pzk, TT["cst"]], writes=[TT["acc"]])
            if sample:
                for t in range(LS):
                    P.dma('sp', cs_o[l, :, 26 + t, :], xg_tok[t:TS:LS, :], reads=[TT["acc"]])
                P.dma('sp', cs_o[l, :, 0:26, :], scv[l, :, 4:30, :])
            else:
                P.dma('sp', cp_o[l, :, :], xg_tok[2:32, :], reads=[TT["acc"]])
        if not sample:
            nx = xgT[1 - par]
            P.op('pool', lambda e: e.tensor_copy(nx[:, :, 0:30], xg[:, :, T:T + 30]), reads=[txg], writes=[TT["xg%d" % (1 - par)]])
            pcv, pcvk = ps[:, 7, :], ps_tok[7]
            for cc in range(2):
                for j in range(31):
                    P.op('pe', lambda e, cc=cc, j=j: e.matmul(pcv[:, cc * 128:cc * 128 + T], Dg[:, cc * 31 + j, :], xg[:, cc, j:j + T],
                                                              start=(j == 0), stop=(j == 30)), reads=[txg] + dgtk, writes=[pcvk])
        yield
        P.begin_chain()
        P.op('dve', lambda e: e.bn_stats(stat[0:T, 8:14], vg[0:T, :]), reads=[TT["vg"]], writes=[t_stat_lnv])
        P.op('dve', lambda e: e.bn_aggr(stat[0:T, 14:16], stat[0:T, 8:14]), reads=[t_stat_lnv], writes=[t_stat_lnv])
        P.op('act', lambda e: e.activation(stat[0:T, 15:16], stat[0:T, 15:16], AF.Ln, bias=EPS, scale=1.0), reads=[t_stat_lnv], writes=[t_stat_lnv])
        P.op('act', lambda e: e.activation(stat[0:T, 15:16], stat[0:T, 15:16], AF.Exp, scale=-0.5), reads=[t_stat_lnv], writes=[t_stat_lnv])
        P.op('dve', lambda e: e.tensor_scalar(vg[0:T, :], vg[0:T, :], stat[0:T, 14:15], stat[0:T, 15:16], ALU.subtract, ALU.mult),
             reads=[TT["vg"], t_stat_lnv], writes=[TT["vg"]])
        P.op('dve', lambda e: e.tensor_tensor(vg[0:T, :], vg[0:T, :], lng[0:T, :], ALU.mult), reads=[TT["vg"], t_ln], writes=[TT["vg"]])
        P.op('dve', lambda e: e.tensor_tensor(vg[0:T, :], vg[0:T, :], lnb[0:T, :], ALU.add), reads=[TT["vg"], t_ln], writes=[TT["vg"]])
        P.op('act', lambda e: e.copy(vln[0:T, :], vg[0:T, :]), reads=[TT["vg"]], writes=[TT["vln"]])
        if sample:
            P.dma('sp', gv_o[l].rearrange("n t c -> (n t) c"), vg[0:T, :], reads=[TT["vg"]])
        P.end_chain()
        P.begin_chain()
        if l == 1:
            for hp in range(4):
                P.op('dve', lambda e, hp=hp: e.tensor_scalar(fT[:, hp, 0:T], fT[:, hp, 0:T], oml1[:, hp:hp + 1], lb1[:, hp:hp + 1], ALU.mult, ALU.add),
                     reads=[TT["fT"], t_par], writes=[TT["fT"]])
        P.op('act', lambda e: e.activation(lfT[:, :, 0:T], fT[:, :, 0:T], AF.Ln), reads=[TT["fT"]], writes=[TT["lfT"]])
        P.op('dve', lambda e: e.tensor_scalar(fT[:, :, 0:T], fT[:, :, 0:T], -1.0, 1.0, ALU.mult, ALU.add), reads=[TT["fT"], TT["lfT"]], writes=[TT["fT"]])
        for hp in range(4):
            P.op('dve', lambda e, hp=hp: e.tensor_tensor_scan(bT[:, hp, 0:T], mres[:, 0:T], lfT[:, hp, 0:T], 0.0, ALU.mult, ALU.add),
                 reads=[TT["lfT"], t_const], writes=[TT["bT"]])
        bview = bT[:, :, 0:T].rearrange("p h (c i) -> p h c i", i=C)[:, :, :, C - 1]
        P.op('act', lambda e: e.activation(gT[:, :, 0:nch], bview, AF.Exp), reads=[TT["bT"]], writes=[TT["gT"]])
        P.op('act', lambda e: e.activation(lfT[:, :, 0:T], bT[:, :, 0:T], AF.Exp, bias=-LN8), reads=[TT["bT"], TT["lfT"]], writes=[TT["lfT"]])
        P.op('dve', lambda e: e.tensor_tensor(qinT[:, :, 0:T], sqT[:, :, 0:T], lfT[:, :, 0:T], ALU.mult), reads=[TT["sqT"], TT["lfT"]], writes=[TT["qinT"]])
        P.op('act', lambda e: e.activation(bT[:, :, 0:T], bT[:, :, 0:T], AF.Exp, scale=-1.0), reads=[TT["bT"], TT["gT"], TT["lfT"]], writes=[TT["bT"]])
        P.op('dve', lambda e: e.tensor_tensor(kinT[:, :, 0:T], fT[:, :, 0:T], bT[:, :, 0:T], ALU.mult), reads=[TT["fT"], TT["bT"]], writes=[TT["kinT"]])
        P.op('dve', lambda e: e.tensor_tensor(kdT[:, :, 0:T].rearrange("p h (c i) -> p h c i", i=C),
                                              kinT[:, :, 0:T].rearrange("p h (c i) -> p h c i", i=C),
                                              gT[:, :, 0:nch].unsqueeze(3).to_broadcast([128, 4, nch, C]), ALU.mult),
             reads=[TT["kinT"], TT["gT"]], writes=[TT["kdT"]])
        P.end_chain()
        P.begin_chain()
        if sample:
            for g4 in range(4):
                P.dma('sp', scv_t[:, g4, :], scv[l, 4 * g4:4 * g4 + 4].rearrange("n j c -> (n j) c"), writes=[TT["scv_t"]])
            for cc in range(2):
                pt, ptk = psum()
                for g4 in range(4):
                    P.op('pe', lambda e, pt=pt, g4=g4, cc=cc: e.transpose(pt[:, g4 * 120:(g4 + 1) * 120], scv_t[:, g4, cc * 128:(cc + 1) * 128], ident_f[0:120, 0:120]),
                         reads=[TT["scv_t"], t_const], writes=[ptk])
                P.op('act', lambda e, pt=pt, cc=cc: e.copy(xxT[:, cc, :, 0:30], pt[:, 0:480].rearrange("p (n j) -> p n j", j=30)),
                     reads=[ptk], writes=[TT["xxT"], t_xb[2]])
            for cc in range(2):
                for t in range(LS):
                    P.op('dve', lambda e, cc=cc, t=t: e.tensor_tensor(tmpc[:, :, :], xxT[:, cc, :, t:t + 31],
                                                                      cw[:, l, cc, :].unsqueeze(1).to_broadcast([128, 16, 31]), ALU.mult),
                         reads=[TT["xxT"], t_par], writes=[TT["tmpc"]])
                    P.op('dve', lambda e, cc=cc, t=t: e.tensor_reduce(acc[:, cc, t:TS:LS], tmpc[:, :, :], AX.X, ALU.add),
                         reads=[TT["tmpc"]], writes=[TT["acc"]])
                P.op('dve', lambda e, cc=cc: e.tensor_scalar(acc[:, cc, 0:T], acc[:, cc, 0:T], cb[:, l, cc:cc + 1], None, ALU.add),
                     reads=[TT["acc"], t_par], writes=[TT["acc"]])
        else:
            for cc in range(2):
                P.op('dve', lambda e, cc=cc: e.tensor_scalar(acc[:, cc, 0:T], pcv[:, cc * 128:cc * 128 + T], cb[:, l, cc:cc + 1], None, ALU.add),
                     reads=[pcvk, t_par], writes=[TT["acc"]])
        P.op('act', lambda e: e.activation(acc2[:, :, 0:T], acc[:, :, 0:T], AF.Square), reads=[TT["acc"]], writes=[TT["acc2"]])
        pl_, plk = psum()
        for cc in range(2):
            P.op('pe', lambda e, cc=cc: e.matmul(pl_[:, 0:T], ones_f[:, :], acc[:, cc, 0:T], start=(cc == 0), stop=(cc == 1)),
                 reads=[TT["acc"], t_const], writes=[plk])
        for cc in range(2):
            P.op('pe', lambda e, cc=cc: e.matmul(pl_[:, 128:128 + T], ones_f[:, :], acc2[:, cc, 0:T], start=(cc == 0), stop=(cc == 1)),
                 reads=[TT["acc2"], t_const], writes=[plk])
        P.op('dve', lambda e: e.tensor_scalar(cst[:, 0, 0:T], pl_[:, 0:T], 1.0 / 256, None, ALU.mult), reads=[plk], writes=[TT["cst"]])
        P.op('dve', lambda e: e.tensor_tensor(cst[:, 1, 0:T], cst[:, 0, 0:T], cst[:, 0, 0:T], ALU.mult), reads=[TT["cst"]], writes=[TT["cst"]])
        P.op('dve', lambda e: e.scalar_tensor_tensor(cst[:, 1, 0:T], pl_[:, 128:128 + T], 1.0 / 256, cst[:, 1, 0:T], ALU.mult, ALU.subtract),
             reads=[plk, TT["cst"]], writes=[TT["cst"]])
        P.op('act', lambda e: e.activation(cst[:, 1, 0:T], cst[:, 1, 0:T], AF.Ln, bias=EPS, scale=1.0), reads=[TT["cst"]], writes=[TT["cst"]])
        P.op('act', lambda e: e.activation(cst[:, 1, 0:T], cst[:, 1, 0:T], AF.Exp, scale=-0.5), reads=[TT["cst"]], writes=[TT["cst"]])
        P.op('dve', lambda e: e.tensor_tensor(acc[:, :, 0:T], acc[:, :, 0:T], cst[:, 0:1, 0:T].to_broadcast([128, 2, T]), ALU.subtract),
             reads=[TT["acc"], TT["cst"]], writes=[TT["acc"]])
        P.op('dve', lambda e: e.tensor_tensor(acc[:, :, 0:T], acc[:, :, 0:T], cst[:, 1:2, 0:T].to_broadcast([128, 2, T]), ALU.mult),
             reads=[TT["acc"], TT["cst"]], writes=[TT["acc"]])
        P.end_chain()
        yield
        for hl in range(2):
            P.op('dve', lambda e, hl=hl: e.tensor_scalar(qinTz[:, :, hl, 0:T], qinT[:, :, 0:T], mhl[:, hl:hl + 1], None, ALU.mult),
                 reads=[TT["qinT"], t_const], writes=[TT["qinTz"]])
        pm, pmk = psum()
        wsv = wsTs if sample else wsT
        twsv = TT["wss"] if sample else TT["ws"]
        tbsr = TT["S0f"]
        for h in range(4):
            c2, hl = h // 2, h % 2
            o_ap = pm[hl * 64:(hl + 1) * 64, c2 * 128:c2 * 128 + T]
            P.op('pe', lambda e, o_ap=o_ap, h=h: e.matmul(o_ap, vln[0:T, h * 64:(h + 1) * 64], wsv[0:T, h, 0:T], start=True, stop=False),
                 reads=[TT["vln"], twsv], writes=[pmk])
            P.op('pe', lambda e, o_ap=o_ap, h=h: e.matmul(o_ap, ones_f[0:1, 0:64], (bs_row_s4[0:1, h // 2, h % 2, 0:T] if sample else bs_row[0:1, h, 0:T]), start=False, stop=True),
                 reads=[twsv, tbsr, t_const], writes=[pmk])
        P.op('dve', lambda e: e.tensor_tensor(ycatT[:, 0:2, 0:T], uT[:, :, 0:T], v3(pm, 2), ALU.mult), reads=[pmk, TT["uT"]], writes=[TT["ycat_a"]])

        pk, pkk = psum()
        pkb = pk.bitcast(BF16)
        for hp in range(4):
            P.op('pe', lambda e, hp=hp: e.transpose(pkb[0:T, hp * 128:(hp + 1) * 128], kdT[:, hp, 0:T], ident_b[:, :]),
                 reads=[TT["kdT"], t_const], writes=[pkk])
        P.op('act', lambda e: e.copy(kd_tok[0:T, :], pkb[0:T, 0:512]), reads=[pkk], writes=[TT["kd_tok"]])
        sc = [psum(), psum()]
        for h in range(8):
            hp, hl = h // 2, h % 2
            pt, ptk = sc[hl]
            P.op('pe', lambda e, pt=pt, hp=hp, hl=hl: e.matmul(pt[0:T, hp * 128:hp * 128 + T], kinT[hl * 64:(hl + 1) * 64, hp, 0:T],
                                                               qinT[hl * 64:(hl + 1) * 64, hp, 0:T], start=True, stop=True),
                 reads=[TT["kinT"], TT["qinT"]], writes=[ptk])
        for hl in range(2):
            pt, ptk = sc[hl]
            P.op('dve', lambda e, pt=pt, hl=hl: e.tensor_tensor(PT[0:T, hl, :, 0:T], v3(pt, 4)[0:T], mask[0:T, 0:T].unsqueeze(1).to_broadcast([T, 4, T]), ALU.mult),
                 reads=[ptk, t_const], writes=[TT["PT"]])
        if not sample:
            for q in range(2):
                P.op('dve', lambda e, q=q: e.tensor_scalar(kdmp[:, q, :], kd_tok[:, :], mpar[:, q:q + 1], None, ALU.mult),
                     reads=[TT["kd_tok"], t_const], writes=[TT["kdm"]])
            ubanks = []
            for c in range(8):
                if c % 2 == 0:
                    ubanks.append(psum(hold=True))
                pu_, puk_ = ubanks[c // 2]
                q = c % 2
                b32 = 32 * (c // 2)
                for h in range(8):
                    hp, hl = h // 2, h % 2
                    P.op('pe', lambda e, pu_=pu_, hp=hp, hl=hl, h=h, q=q, b32=b32: e.matmul(
                        pu_[hl * 64:(hl + 1) * 64, q * 256 + hp * 64:q * 256 + (hp + 1) * 64], kdmp[b32:b32 + 32, q, h * 64:(h + 1) * 64],
                        v_tok[b32:b32 + 32, h * 64:(h + 1) * 64], start=True, stop=True, tile_position=(b32, 64 * hl)),
                        reads=[TT["kdm"], TT["v_tok"]], writes=[puk_])
        if sample:
            s0bufs = [S0f, xb[0][:, 0, :].rearrange("p (n h v) -> p n h v", n=4, v=64)]
            s0toks = [TT["S0f"], t_xb[0]]

            def s0_load(g4):
                for hl in range(2):
                    P.dma('sp', s0bufs[g4 % 2][hl * 64:(hl + 1) * 64, :, :, :],
                          sh[l, 4 * g4:4 * g4 + 4].rearrange("n (hp hl) k v -> hl k n hp v", hl=2)[hl], writes=[s0toks[g4 % 2]])
            s0_load(0)
            s0_load(1)
            for g4 in range(4):
                sbuf_, stok_ = s0bufs[g4 % 2], s0toks[g4 % 2]
                P.op('act', lambda e, g4=g4, sbuf_=sbuf_: e.copy(S0b[:, 4 * g4:4 * g4 + 4, :, :], sbuf_[:, :, :, :]), reads=[stok_],
                     writes=[TT["S0b"]] + list(FTOK[1].values()))
                for pr in range(2):
                    pu_, puk_ = psum()
                    for q in range(2):
                        n = 4 * g4 + 2 * pr + q
                        P.op('dve', lambda e, n=n, q=q: e.tensor_scalar(kdm[:, q, :], kd_tok[0:TS, :], oneh[:, n:n + 1], None, ALU.mult),
                             reads=[TT["kd_tok"], t_const], writes=[TT["kdm"]])
                        for h in range(8):
                            hp, hl = h // 2, h % 2
                            P.op('pe', lambda e, pu_=pu_, q=q, hp=hp, hl=hl, h=h: e.matmul(
                                pu_[hl * 64:(hl + 1) * 64, q * 256 + hp * 64:q * 256 + (hp + 1) * 64], kdm[:, q, h * 64:(h + 1) * 64],
                                v_tok[0:TS, h * 64:(h + 1) * 64], start=True, stop=True), reads=[TT["kdm"], TT["v_tok"]], writes=[puk_])
                    n0 = 2 * pr
                    sv = sbuf_[:, n0:n0 + 2, :, :]
                    gb = gT[:, :, 4 * g4 + n0:4 * g4 + n0 + 2].rearrange("p h n -> p n h").unsqueeze(3).to_broadcast([128, 2, 4, 64])
                    P.op('dve', lambda e, sv=sv, gb=gb: e.tensor_tensor(sv, sv, gb, ALU.mult), reads=[stok_, TT["gT"], TT["S0b"]], writes=[stok_])
                    P.op('dve', lambda e, sv=sv, pu_=pu_: e.tensor_tensor(sv, sv, pu_[:, 0:512].rearrange("p (n h v) -> p n h v", n=2, v=64), ALU.add),
                         reads=[stok_, puk_], writes=[stok_])
                for hl in range(2):
                    P.dma('sp', hs_o[l, 4 * g4:4 * g4 + 4].rearrange("n (hp hl) k v -> hl k n hp v", hl=2)[hl],
                          sbuf_[hl * 64:(hl + 1) * 64, :, :, :], reads=[stok_])
                if g4 + 2 < 4:
                    s0_load(g4 + 2)
        for cc in range(2):
            P.op('act', lambda e, cc=cc: e.activation(ycatT[:, 6 + cc, 0:T], acc[:, cc, 0:T], AF.Silu, bias=clb[:, l, cc:cc + 1], scale=clg[:, l, cc:cc + 1]),
                 reads=[TT["acc"], t_par], writes=[TT["ycat_c"]])

        yield
        if not sample:
            for c in range(8):
                pu_, puk_ = ubanks[c // 2]
                q = c % 2
                P.op('dve', lambda e, c=c: e.tensor_tensor(Sg[:, :, :], S[:, :, :], gT[:, :, c:c + 1].to_broadcast([128, 4, 64]), ALU.mult),
                     reads=[TT["S"], TT["gT"]], writes=[TT["Sg"]])
                P.op('dve', lambda e, c=c: e.tensor_copy(Sb[:, c, :, :], S[:, :, :]), reads=[TT["S"]], writes=[TT["Sb"]])
                P.op('dve', lambda e, pu_=pu_, q=q: e.tensor_tensor(S[:, :, :], Sg[:, :, :], pu_[:, q * 256:(q + 1) * 256].rearrange("p (h v) -> p h v", v=64), ALU.add),
                     reads=[TT["Sg"], puk_], writes=[TT["S"]])
                if c % 2 == 1:
                    psum_release(puk_)
        yield
        po, pok = psum()
        for h in range(8):
            hp, hl = h // 2, h % 2
            o_ap = po[hl * 64:(hl + 1) * 64, hp * 128:hp * 128 + T]
            P.op('pe', lambda e, o_ap=o_ap, h=h, hp=hp, hl=hl: e.matmul(o_ap, v_tok[0:T, h * 64:(h + 1) * 64], PT[0:T, hl, hp, 0:T], start=True, stop=False),
                 reads=[TT["v_tok"], TT["PT"]], writes=[pok])
            for c in range(nch):
                st_ap = S0b[:, c, hp, :] if sample else Sb[:, c, hp, :]
                P.op('pe', lambda e, st_ap=st_ap, hp=hp, hl=hl, c=c: e.matmul(
                    po[hl * 64:(hl + 1) * 64, hp * 128 + c * C:hp * 128 + (c + 1) * C], st_ap, qinTz[:, hp, hl, c * C:(c + 1) * C],
                    start=False, stop=(c == nch - 1)), reads=[TT["S0b"] if sample else TT["Sb"], TT["qinTz"]], writes=[pok])
        yield
        P.op('act', lambda e: e.activation(osq[:, :, 0:T], v3(po, 4), AF.Square), reads=[pok], writes=[TT["osq"]])
        pss, pssk = psum()
        if sample:
            for hp in range(4):
                P.op('pe', lambda e, hp=hp: e.matmul(pss[:, hp * 128:hp * 128 + T], bones_b[:, :], osq[:, hp, 0:T], start=True, stop=True),
                     reads=[TT["osq"], t_const], writes=[pssk])
        else:
            P.op('pe', lambda e: e.matmul(pss[:, 0:512], bones_b[:, :], osq[:, :, :].rearrange("p c t -> p (c t)"), start=True, stop=True),
                 reads=[TT["osq"], t_const], writes=[pssk])
        P.op('act', lambda e: e.activation(r1[:, :, 0:T], v3(pss, 4), AF.Ln, bias=EPS, scale=1.0 / 64), reads=[pssk], writes=[TT["r1"]])
        P.op('act', lambda e: e.activation(r1[:, :, 0:T], r1[:, :, 0:T], AF.Exp, scale=-0.5), reads=[TT["r1"]], writes=[TT["r1"]])
        P.op('dve', lambda e: e.tensor_tensor(t1[:, :, 0:T], v3(po, 4), r1[:, :, 0:T], ALU.mult), reads=[pok, TT["r1"]], writes=[TT["t1"]])
        P.op('dve', lambda e: e.tensor_tensor(ycatT[:, 2:6, 0:T], t1[:, :, 0:T], sgT[:, :, 0:T], ALU.mult), reads=[TT["t1"], TT["sgT"]], writes=[TT["ycat_b"]])
        if last_prompt:
            for hl in range(2):
                P.dma('sp', hp_o[l].rearrange("(hp hl) k v -> hl k hp v", hl=2)[hl], S[hl * 64:(hl + 1) * 64, :, :], reads=[TT["S"]])

        yb = [psum(), psum()]
        korder = [0, 1, 6, 7, 2, 3, 4, 5]
        ytok = {0: "ycat_a", 1: "ycat_a", 6: "ycat_c", 7: "ycat_c", 2: "ycat_b", 3: "ycat_b", 4: "ycat_b", 5: "ycat_b"}
        for kk in range(0, 8, 4):
            for h2 in range(2):
                pt, ptk = yb[h2]
                for k in korder[kk:kk + 4]:
                    P.op('pe', lambda e, pt=pt, k=k, h2=h2: e.matmul(pt[0:T, :], ycatT[:, k, 0:T], wout[:, k, h2 * 512:(h2 + 1) * 512],
                                                                      start=(k == korder[0]), stop=(k == korder[-1])),
                         reads=[TT[ytok[k]]] + wotk, writes=[ptk])
        postnorm_store(l, 0, ti, xpar, 0, yb, sample)
        yield


    def phase_mix(l, rw, hooks=None, bg=None):
        setup_mod(l, 0, part=1)
        gens = [mixer_tile(l, rw, ti, ti == 16, ti == 15) for ti in range(17)]
        load_x(l, 0, 0, 0, 0)
        load_x(l, 0, 1, 1, 0)
        P.op('pool', lambda e: e.memset(S[:, :, :], 0.0), writes=[T_["S"]])
        P.op('pool', lambda e: e.memset(xgT[0][:, :, 0:30], 0.0), writes=[T_["xg0"]])
        next(gens[0])
        prep_layer_mixer(l, rw)
        next(gens[0])
        next(gens[0])
        fold_gn(l, rw)
        prep_layer_mixer_b(l, rw)
        if bg is not None:
            bg()
        setup_mod(l, 0, part=2)
        build_gate(False)
        P.begin_chain()
        next(gens[1])
        P.end_chain()
        next(gens[0])
        P.merge_chains()
        for ti in range(17):
            nxt = ti + 1 < 17
            if bg is not None:
                bg()
            if ti + 2 < 17:
                load_x(l, 0, ti + 2, (ti + 2) % 3, 0)
            if ti == 16:
                build_gate(True)
                for h in range(4):
                    P.dma('sp', bs_row_s4[0:1, h // 2, h % 2, :].rearrange("p (n t) -> p n t", t=LS),
                          a_b_s[l:l + 1, h, 0:4].unsqueeze(1).to_broadcast([1, NSEQ_S, LS]), writes=[T_["S0f"]])
            next(gens[ti])
            if nxt:
                next(gens[ti + 1])
                if ti + 1 == 16 and hooks is not None:
                    hooks[1]()
            next(gens[ti])
            if nxt:
                next(gens[ti + 1])
                if ti + 1 == 15 and hooks is not None:
                    hooks[0]()
            next(gens[ti])
            P.begin_chain()
            next(gens[ti])
            P.end_chain()
            if ti + 2 < 17:
                P.begin_chain()
                next(gens[ti + 2])
                P.end_chain()
            if nxt:
                next(gens[ti + 1])
            P.merge_chains()

    if debug == 'mlp':
        while not mod1_state['done']:
            mod1_bg()
        load_w_up(0, 1, [6, 7])
        load_w_down(0, 0)
        cur['first'] = True
        cur['last'] = True
        phase_mlp(0, 1, 0)
    elif debug == 'mix':
        cur['first'] = True
        cur['last'] = True
        while not mod1_state['done']:
            mod1_bg()
        phase_mix(0, 0)
    else:
        cur['first'] = True
        phase_mix(0, 0, hooks=(lambda: load_w_up(0, 1, [6, 7]), lambda: load_w_down(0, 0, range(24))), bg=mod1_bg)
        while not mod1_state['done']:
            mod1_bg()
        cur['first'] = False
        load_w_down(0, 0, range(24, 32))
        phase_mlp(0, 1, 0, after_last_up=lambda: load_w_in_out(1, 1))
        load_w_up(1, 0, range(6))
        phase_mix(1, 1, hooks=(lambda: load_w_up(1, 0, [6, 7]), lambda: load_w_down(1, 1, range(24))))
        load_w_down(1, 1, range(24, 32))
        cur['last'] = True
        phase_mlp(1, 0, 1)

    P.finalize()
    with nc.Block() as block:
        @block.tensor
        def _(e):
            P.emit_engine('pe', e)

        @block.scalar
        def _(e):
            P.emit_engine('act', e)

        @block.vector
        def _(e):
            P.emit_engine('dve', e)

        @block.gpsimd
        def _(e):
            P.emit_engine('pool', e)

        @block.sync
        def _(e):
            P.emit_engine('sp', e)
            P.final_waits('sp', e)
    es.close()
    return nc


def make_in_maps(inputs):
    f = lambda a: np.ascontiguousarray(np.asarray(a, dtype=np.float32))
    maps = []
    for i in range(NCORES):
        m = {}
        m["xp"] = f(inputs["x_prompt"][i])
        m["xs"] = f(inputs["x_sample"][NSEQ_S * i:NSEQ_S * (i + 1)].reshape(TS, D))
        m["sh"] = f(inputs["state_hgrn"][:, NSEQ_S * i:NSEQ_S * (i + 1)])
        m["scv"] = f(inputs["state_conv"][:, NSEQ_S * i:NSEQ_S * (i + 1)])
        m["c"] = f(np.concatenate([inputs["c_prompt"][i:i + 1], inputs["c_sample"][NSEQ_S * i:NSEQ_S * (i + 1)]], 0))
        for k in ["w_ada", "b_ada", "g_pre_mix", "g_post_mix", "g_pre_mlp", "g_post_mlp", "w_in", "a_ln_g",
                  "a_ln_b", "a_w_s", "a_b_s", "b_lb", "b_gn_g", "c_w_dw", "c_b_dw", "c_ln_g", "c_ln_b",
                  "w_out", "w_up", "w_down"]:
            m[k] = f(inputs[k])
        maps.append(m)
    return maps


def kernel(**inputs):
    nc = build()
    maps = make_in_maps(inputs)
    res = run_bass_kernel_spmd(nc, maps, core_ids=list(range(NCORES)))
    r = res.results
    y_prompt = np.stack([r[i]["yp"] for i in range(NCORES)], 0)
    y_sample = np.concatenate([r[i]["ys"].reshape(NSEQ_S, LS, D) for i in range(NCORES)], 0)
    hgrn_prompt = np.stack([r[i]["hp"] for i in range(NCORES)], 1)
    hgrn_sample = np.concatenate([r[i]["hs"] for i in range(NCORES)], 1)
    conv_prompt = np.stack([r[i]["cp"] for i in range(NCORES)], 1)
    conv_sample = np.concatenate([r[i]["cs"] for i in range(NCORES)], 1)
    gmlp_v = np.concatenate([r[i]["gv"] for i in range(NCORES)], 1)
    return (y_prompt, y_sample, hgrn_prompt, hgrn_sample, conv_prompt, conv_sample, gmlp_v)
```

```python
import numpy as np
from contextlib import ExitStack
import concourse.bass as bass
import concourse.mybir as mybir
from concourse.bass_utils import run_bass_kernel_spmd

F32 = mybir.dt.float32
BF16 = mybir.dt.bfloat16
AF = mybir.ActivationFunctionType
ALU = mybir.AluOpType
AX = mybir.AxisListType

NCORES = 8
D = 1024
SEQ = 2048
NSEQ_S = 16
LS = 4
TS = NSEQ_S * LS
DEPTH = 2
NIN = 3072
DFF = 4096
EPS = 1e-6
ENGS = ['pe', 'act', 'dve', 'pool', 'sp']
STRICT_SAME_ENGINE = True


class Tok:
    __slots__ = ('name', 'w', 'wl', 'r')

    def __init__(self, name=''):
        self.name = name
        self.w = None
        self.wl = []
        self.r = {}


class Op:
    __slots__ = ('eng', 'fn', 'deps', 'idx', 'sig', 'cnt', 'dma', 'dsem', 'dval')


class Prog:
    def __init__(self):
        self.ops = {e: [] for e in ENGS}
        self.pools = {}
        self.sems = {}
        self.chains = []
        self.cur_chain = None

    def set_dma_pool(self, queue, sems):
        self.pools[queue] = {'slots': [[s, 0, None] for s in sems], 'i': 0}

    def _dep(self, op, d, kind):
        if d is op or d is None:
            return
        if kind == 'waw' and d.dma and op.dma:
            return
        if (not d.dma) and d.eng == op.eng:
            if op.eng == 'pe':
                return
            if kind != 'raw' and not op.dma and not STRICT_SAME_ENGINE:
                return
        op.deps.append(d)

    def begin_chain(self):
        self.cur_chain = []

    def end_chain(self):
        self.chains.append(self.cur_chain)
        self.cur_chain = None

    def merge_chains(self):
        chains, self.chains = self.chains, []
        while any(chains):
            for c in chains:
                if c:
                    a = c.pop(0)
                    self.op(*a[0], **a[1])

    def op(self, eng, fn, reads=(), writes=(), dma=False):
        if getattr(self, 'cur_chain', None) is not None:
            self.cur_chain.append(((eng, fn), dict(reads=list(reads), writes=list(writes), dma=dma)))
            return None
        o = Op()
        o.eng = eng
        o.fn = fn
        o.idx = len(self.ops[eng])
        o.deps = []
        o.sig = False
        o.dma = dma
        o.cnt = 0
        o.dsem = None
        o.dval = 0
        for t in reads:
            self._dep(o, t.w, 'raw')
            for d in t.wl:
                self._dep(o, d, 'raw')
        for t in writes:
            self._dep(o, t.w, 'waw')
            for d in t.wl:
                self._dep(o, d, 'waw')
            for r in t.r.values():
                self._dep(o, r, 'war')
        for t in reads:
            t.r[id(o) if dma else eng] = o
        for t in writes:
            if dma:
                t.wl.append(o)
            else:
                t.w = o
                t.wl = []
            t.r = {}
        if dma:
            pool = self.pools[eng]
            slot = pool['slots'][pool['i']]
            pool['i'] = (pool['i'] + 1) % len(pool['slots'])
            if slot[2] is not None:
                o.deps.append(slot[2])
            slot[1] += 16
            slot[2] = o
            o.dsem = slot[0]
            o.dval = slot[1]
        self.ops[eng].append(o)
        return o

    def dma(self, queue, out, in_, reads=(), writes=(), **kw):
        return self.op(queue, lambda e: e.dma_start(out=out, in_=in_, **kw), reads, writes, dma=True)

    def finalize(self):
        for e in ENGS:
            for o in self.ops[e]:
                for d in o.deps:
                    if not d.dma:
                        d.sig = True
        for e in ENGS:
            c = 0
            for o in self.ops[e]:
                if o.sig and not o.dma:
                    c += 1
                o.cnt = c

    def emit_engine(self, eng, e):
        seen = {}
        for o in self.ops[eng]:
            waits = {}
            for d in o.deps:
                if d.dma:
                    sem, val = d.dsem, d.dval
                else:
                    sem, val = self.sems[d.eng], d.cnt
                k = id(sem)
                if k not in waits or waits[k][1] < val:
                    waits[k] = (sem, val)
            for k, (sem, val) in waits.items():
                if seen.get(k, 0) >= val:
                    continue
                e.wait_ge(sem, val)
                seen[k] = val
            ins = o.fn(e)
            if o.dma:
                ins.then_inc(o.dsem, 16)
            elif o.sig:
                ins.then_inc(self.sems[eng], 1)

    def final_waits(self, eng, e):
        for q, pool in self.pools.items():
            for sem, val, last in pool['slots']:
                if val > 0:
                    e.wait_ge(sem, val)


def AP(t, off, pat):
    return bass.AP(t.tensor, t.offset + off, pat)


def build(debug=None):
    nc = bass.Bass("TRN2", target_bir_lowering=False, dynamic_dma_scratch_size=4096)
    P = Prog()
    es = ExitStack()

    def din(name, shape):
        return nc.dram_tensor(name, list(shape), F32, kind="ExternalInput").ap()

    def dout(name, shape):
        return nc.dram_tensor(name, list(shape), F32, kind="ExternalOutput").ap()

    xp = din("xp", [SEQ, D])
    xs = din("xs", [TS, D])
    sh = din("sh", [DEPTH, NSEQ_S, 8, 64, 64])
    scv = din("scv", [DEPTH, NSEQ_S, 30, 256])
    cin = din("c", [1 + NSEQ_S, D])
    w_ada = din("w_ada", [DEPTH, D, 6 * D])
    b_ada = din("b_ada", [DEPTH, 6 * D])
    g_pre_mix = din("g_pre_mix", [DEPTH, D])
    g_post_mix = din("g_post_mix", [DEPTH, D])
    g_pre_mlp = din("g_pre_mlp", [DEPTH, D])
    g_post_mlp = din("g_post_mlp", [DEPTH, D])
    w_in = din("w_in", [DEPTH, D, NIN])
    a_ln_g = din("a_ln_g", [DEPTH, 256])
    a_ln_b = din("a_ln_b", [DEPTH, 256])
    a_w_s = din("a_w_s", [DEPTH, 4, 128, 128])
    a_b_s = din("a_b_s", [DEPTH, 4, 128])
    b_lb = din("b_lb", [DEPTH, 512])
    b_gn_g = din("b_gn_g", [DEPTH, 512])
    c_w_dw = din("c_w_dw", [DEPTH, 31, 256])
    c_b_dw = din("c_b_dw", [DEPTH, 256])
    c_ln_g = din("c_ln_g", [DEPTH, 256])
    c_ln_b = din("c_ln_b", [DEPTH, 256])
    w_out = din("w_out", [DEPTH, D, D])
    w_up = din("w_up", [DEPTH, D, DFF])
    w_down = din("w_down", [DEPTH, DFF, D])

    yp = dout("yp", [SEQ, D])
    ys = dout("ys", [TS, D])
    hp_o = dout("hp", [DEPTH, 8, 64, 64])
    hs_o = dout("hs", [DEPTH, NSEQ_S, 8, 64, 64])
    cp_o = dout("cp", [DEPTH, 30, 256])
    cs_o = dout("cs", [DEPTH, NSEQ_S, 30, 256])
    gv_o = dout("gv", [DEPTH, NSEQ_S, LS, 256])
    dbg = None
    if debug:
        dbg = dout("dbg", [128, 2048])

    def sb(name, shape, dt=F32):
        return es.enter_context(nc.sbuf_tensor(name, list(shape), dt))

    def sem(name):
        return es.enter_context(nc.semaphore(name))

    for e in ENGS:
        P.sems[e] = sem("s_" + e)
    P.set_dma_pool('sp', [sem("dsp%d" % i) for i in range(16)])
    P.set_dma_pool('pool', [sem("dpl%d" % i) for i in range(16)])
    P.set_dma_pool('act', [sem("dac%d" % i) for i in range(4)])

    ps = es.enter_context(nc.psum_tensor("ps", [128, 8, 512], F32))
    ps_tok = [Tok("ps%d" % i) for i in range(8)]
    ps_rr = [0]

    ps_held = set()

    def psum(hold=False):
        for _ in range(8):
            i = ps_rr[0]
            ps_rr[0] = (i + 1) % 7
            if i not in ps_held:
                break
        else:
            raise RuntimeError("no free PSUM bank")
        if hold:
            ps_held.add(i)
        return ps[:, i, :], ps_tok[i]

    def psum_release(tok):
        ps_held.discard(ps_tok.index(tok))


    W = [sb("W0", [128, 32768], BF16), sb("W1", [128, 32768], BF16)]
    Wt = [[Tok("W%d_%d" % (r, i)) for i in range(8)] for r in range(2)]

    def wtoks(r, c0, c1):
        return [Wt[r][i] for i in range(c0 // 4096, (c1 - 1) // 4096 + 1)]

    ident_f = sb("ident_f", [128, 128], F32)
    ident_b = sb("ident_b", [128, 128], BF16)
    ones_f = sb("ones_f", [128, 128], F32)
    bones_b = sb("bones_b", [128, 128], BF16)
    mask_p = sb("mask_p", [128, 128], F32)
    mask_s = sb("mask_s", [64, 64], F32)
    mres_p = sb("mres_p", [128, 128], F32)
    mres_s = sb("mres_s", [128, 64], F32)
    mhl = sb("mhl", [128, 2], F32)
    oneh = sb("oneh", [64, 16], F32)
    oneh16 = sb("oneh16", [128, 8], F32)
    mpar = sb("mpar", [128, 2], F32)
    t_const = Tok("const")
    C_ = lambda fn: P.op('pool', fn, reads=[t_const], writes=[t_const])
    C_(lambda e: e.memset(ident_f[:, :], 1.0))
    C_(lambda e: e.affine_select(ident_f[:, :], ident_f[:, :], [[-1, 128]], ALU.is_equal, 0.0, base=0, channel_multiplier=1))
    C_(lambda e: e.memset(ones_f[:, :], 1.0))
    C_(lambda e: e.memset(bones_b[:, :], 0.0))
    C_(lambda e: e.memset(bones_b[0:64, 0:64], 1.0))
    C_(lambda e: e.memset(bones_b[64:128, 64:128], 1.0))
    C_(lambda e: e.memset(mask_p[:, :], 1.0))
    C_(lambda e: e.affine_select(mask_p[:, :], mask_p[:, :], [[1, 128]], ALU.is_ge, 0.0, base=0, channel_multiplier=-1))
    C_(lambda e: e.affine_select(mask_p[:, :], mask_p[:, :], [[-16, 8], [0, 16]], ALU.is_ge, 0.0, base=0, channel_multiplier=1))
    C_(lambda e: e.memset(mask_s[:, :], 1.0))
    C_(lambda e: e.affine_select(mask_s[:, :], mask_s[:, :], [[1, 64]], ALU.is_ge, 0.0, base=0, channel_multiplier=-1))
    C_(lambda e: e.affine_select(mask_s[:, :], mask_s[:, :], [[-4, 16], [0, 4]], ALU.is_ge, 0.0, base=0, channel_multiplier=1))
    C_(lambda e: e.memset(mres_p[:, :], 1.0))
    C_(lambda e: e.memset(mres_p[:, :].rearrange("p (c i) -> p c i", i=16)[:, :, 0:1], 0.0))
    C_(lambda e: e.memset(mres_s[:, :], 1.0))
    C_(lambda e: e.memset(mres_s[:, :].rearrange("p (c i) -> p c i", i=4)[:, :, 0:1], 0.0))
    C_(lambda e: e.memset(mhl[:, :], 0.0))
    C_(lambda e: e.memset(mhl[0:64, 0:1], 1.0))
    C_(lambda e: e.memset(mhl[64:128, 1:2], 1.0))
    C_(lambda e: e.memset(oneh[:, :], 1.0))
    C_(lambda e: e.affine_select(oneh[:, :], oneh[:, :], [[-4, 16]], ALU.is_ge, 0.0, base=0, channel_multiplier=1))
    C_(lambda e: e.affine_select(oneh[:, :], oneh[:, :], [[4, 16]], ALU.is_ge, 0.0, base=3, channel_multiplier=-1))
    C_(lambda e: e.memset(oneh16[:, :], 1.0))
    C_(lambda e: e.affine_select(oneh16[:, :], oneh16[:, :], [[-16, 8]], ALU.is_ge, 0.0, base=0, channel_multiplier=1))
    C_(lambda e: e.affine_select(oneh16[:, :], oneh16[:, :], [[16, 8]], ALU.is_ge, 0.0, base=15, channel_multiplier=-1))
    P.op('dve', lambda e: e.tensor_reduce(mpar[:, 0:1], oneh16[:, 0:8:2], AX.X, ALU.add), reads=[t_const], writes=[t_const])
    P.op('dve', lambda e: e.tensor_reduce(mpar[:, 1:2], oneh16[:, 1:8:2], AX.X, ALU.add), reads=[t_const], writes=[t_const])
    P.op('dve', lambda e: e.tensor_copy(ident_b[:, :], ident_f[:, :]), reads=[t_const], writes=[t_const])

    t_par = Tok("params")

    def fm_load(name, src, nch):
        t = sb(name, [128, DEPTH, nch], F32)
        for l in range(DEPTH):
            P.dma('sp', t[:, l, :], src[l].rearrange("(k p) -> p k", p=128), writes=[t_par],
                  allow_slow_non_contiguous=True)
        return t

    gpm = fm_load("gpm", g_pre_mix, 8)
    gpo = fm_load("gpo", g_post_mix, 8)
    gpl = fm_load("gpl", g_pre_mlp, 8)
    gpol = fm_load("gpol", g_post_mlp, 8)
    gn = fm_load("gn", b_gn_g, 4)
    lbr = fm_load("lbr", b_lb, 4)
    cb = fm_load("cb", c_b_dw, 2)
    clg = fm_load("clg", c_ln_g, 2)
    clb = fm_load("clb", c_ln_b, 2)
    cw = sb("cw", [128, DEPTH, 2, 31], F32)
    for l in range(DEPTH):
        for cc in range(2):
            P.dma('sp', cw[:, l, cc, :], c_w_dw[l][:, cc * 128:(cc + 1) * 128].rearrange("j p -> p j"),
                  writes=[t_par], allow_slow_non_contiguous=True)
    lb1 = sb("lb1", [128, 4], F32)
    oml1 = sb("oml1", [128, 4], F32)
    P.op('dve', lambda e: e.tensor_tensor(lb1[:, :], lbr[:, 1, :], lbr[:, 0, :], ALU.subtract), reads=[t_par], writes=[t_par])
    P.op('act', lambda e: e.activation(lb1[:, :], lb1[:, :], AF.Sigmoid), reads=[t_par], writes=[t_par])
    P.op('dve', lambda e: e.tensor_scalar(oml1[:, :], lb1[:, :], -1.0, 1.0, ALU.mult, ALU.add), reads=[t_par], writes=[t_par])
    lng = sb("lng", [128, 256], F32)
    lnb = sb("lnb", [128, 256], F32)
    t_ln = Tok("ln")

    NS = 1 + NSEQ_S
    modT = sb("modT", [128, DEPTH, 48, NS], F32)
    t_mod = Tok("modT")
    t2 = sb("t2", [128, D], F32)
    xn = sb("xn", [128, D], BF16)
    c_sb = t2[0:NS, :]
    c_bf = xn[0:NS, :]
    cT = sb("cT", [128, 8, NS], BF16)
    badaT = sb("badaT", [128, DEPTH, 48], F32)
    t_c = Tok("c")
    t_cT = Tok("cT")
    P.dma('sp', c_sb, cin[:, :], writes=[t_c])
    for l in range(DEPTH):
        P.dma('sp', badaT[:, l, :], b_ada[l].rearrange("(j p) -> p j", p=128), writes=[t_par],
              allow_slow_non_contiguous=True)
    P.op('act', lambda e: e.activation(c_bf, c_sb, AF.Silu), reads=[t_c], writes=[t_c])
    pst, pstok = psum()
    pst_b = pst.bitcast(BF16)
    for k in range(8):
        P.op('pe', lambda e, k=k: e.transpose(pst_b[0:128, k * 32:k * 32 + NS], xn[0:NS, k * 128:(k + 1) * 128],
                                              ident_b[0:NS, 0:NS]),
             reads=[t_c, t_const], writes=[pstok])
    P.op('dve', lambda e: e.tensor_copy(cT[:, :, :], pst_b[:, 0:256].rearrange("p (k s) -> p k s", s=32)[:, :, 0:NS]),
         reads=[pstok], writes=[t_cT])

    def load_w_in_out(l, r):
        wv = w_in[l].rearrange("(k p) n -> p k n", p=128)
        for k in range(8):
            P.dma('pool', W[r][:, k * 3072:(k + 1) * 3072], wv[:, k, :], writes=wtoks(r, k * 3072, (k + 1) * 3072),
                  max_dma_last_dim=4096)
        wo = w_out[l].rearrange("(k p) n -> p k n", p=128)
        for k in range(8):
            c0 = 24576 + k * 1024
            P.dma('pool', W[r][:, c0:c0 + 1024], wo[:, k, :], writes=wtoks(r, c0, c0 + 1024))

    def fold_gn(l, r):
        for hp in range(4):
            c0 = 24576 + (2 + hp) * 1024
            P.op('dve', lambda e, c0=c0, hp=hp: e.tensor_scalar(W[r][:, c0:c0 + 1024], W[r][:, c0:c0 + 1024],
                                                                gn[:, l, hp:hp + 1], None, ALU.mult),
                 reads=wtoks(r, c0, c0 + 1024) + [t_par], writes=wtoks(r, c0, c0 + 1024))

    def load_w_up(l, r, ks=range(8)):
        wv = w_up[l].rearrange("(k p) n -> p k n", p=128)
        for k in ks:
            for hf in range(2):
                c0 = k * 4096 + hf * 2048
                P.dma('pool', W[r][:, c0:c0 + 2048], wv[:, k, hf * 2048:(hf + 1) * 2048], writes=[Wt[r][k]],
                      max_dma_last_dim=4096)

    def load_w_down(l, r, js=range(32)):
        wv = w_down[l].rearrange("(j p) n -> p j n", p=128)
        for j in js:
            P.dma('pool', W[r][:, j * 1024:(j + 1) * 1024], wv[:, j, :], writes=[Wt[r][j // 4]])

    def mod_pieces(l, pieces=range(12)):
        wv = w_ada[l].rearrange("(k p) n -> p k n", p=128)
        for piece in pieces:
            ri = piece % 3
            buf = W[1][:, ri * 4096:(ri + 1) * 4096].rearrange("p (k n) -> p k n", k=8)
            tk = Wt[1][ri]
            P.dma('pool', buf[:, :, :], wv[:, :, piece * 512:(piece + 1) * 512], writes=[tk])
            pt, ptk = psum()
            for jj in range(4):
                col = jj * NS
                for k in range(8):
                    P.op('pe', lambda e, pt=pt, buf=buf, jj=jj, k=k, col=col: e.matmul(
                        pt[:, col:col + NS], buf[:, k, jj * 128:(jj + 1) * 128], cT[:, k, :],
                        start=(k == 0), stop=(k == 7)), reads=[tk, t_cT], writes=[ptk])
            P.op('dve', lambda e, pt=pt, piece=piece: e.tensor_tensor(
                modT[:, l, piece * 4:(piece + 1) * 4, :],
                pt[:, 0:4 * NS].rearrange("p (j s) -> p j s", s=NS),
                badaT[:, l, piece * 4:(piece + 1) * 4].unsqueeze(2).to_broadcast([128, 4, NS]),
                ALU.add), reads=[ptk, t_par], writes=[t_mod])
            yield

    for _ in mod_pieces(0, range(4)):
        pass
    load_w_in_out(0, 0)

    def _bg_pieces():
        yield from mod_pieces(0, range(4, 12))
        yield from mod_pieces(1)
    mod1_gen = _bg_pieces()
    mod1_state = {'done': False}

    def mod1_bg():
        if mod1_state['done']:
            return
        try:
            next(mod1_gen)
            next(mod1_gen)
        except StopIteration:
            mod1_state['done'] = True
            load_w_up(0, 1, range(6))

    xscr = nc.dram_tensor("xscr", [SEQ + TS, D], F32).ap()
    t_x = [Tok("x%d" % i) for i in range(17)]

    cur = {'first': False, 'last': False}

    def x_src(l, ph, ti):
        if cur['first']:
            return xp[ti * 128:(ti + 1) * 128, :] if ti < 16 else xs[:, :]
        return xscr[ti * 128:(ti + 1) * 128, :] if ti < 16 else xscr[SEQ:SEQ + TS, :]

    def x_dst(l, ph, ti):
        if cur['last']:
            return yp[ti * 128:(ti + 1) * 128, :] if ti < 16 else ys[:, :]
        return xscr[ti * 128:(ti + 1) * 128, :] if ti < 16 else xscr[SEQ:SEQ + TS, :]

    xb = [sb("xb%d" % i, [128, 1, D], F32) for i in range(2)]
    t_xb = [Tok("xb0"), Tok("xb1")]
    t_xn = Tok("xn")
    t_t2 = Tok("t2")
    junk = t2[:, 0:512].bitcast(BF16)
    stat = sb("stat", [128, 16], F32)
    t_stat = Tok("stat")
    t_stat_glob = t_stat
    t_stat_glob2 = Tok("stat_warm")
    t_stat_pre = Tok("stat_pre")
    t_stat_lnv = Tok("stat_lnv")
    hT = [sb("hT%d" % i, [128, 8, 128], BF16) for i in range(2)]
    t_hT = [Tok("hT0"), Tok("hT1")]
    gsT = sb("gsT", [128, 8, NS], F32)
    shT = sb("shT", [128, 8, NS], F32)
    t_gs = Tok("gs")
    t_ggT = Tok("ggT")
    ggT = sb("ggT", [128, 8, NS], F32)
    gg = sb("gg", [128, D], F32)
    t_gg = Tok("gg")

    def setup_mod(l, ph, part=3):
        base = 3 * ph
        gpre = gpm if ph == 0 else gpl
        gpost = gpo if ph == 0 else gpol
        if part & 1:
            P.op('dve', lambda e: e.tensor_scalar(gsT[:, :, :], modT[:, l, (base + 1) * 8:(base + 2) * 8, :], 1.0, None, ALU.add),
                 reads=[t_mod], writes=[t_gs])
            P.op('dve', lambda e: e.tensor_tensor(gsT[:, :, :], gsT[:, :, :], gpre[:, l, :].unsqueeze(2).to_broadcast([128, 8, NS]), ALU.mult),
                 reads=[t_gs, t_par], writes=[t_gs])
            P.op('dve', lambda e: e.tensor_copy(shT[:, :, :], modT[:, l, base * 8:(base + 1) * 8, :]), reads=[t_mod], writes=[t_gs])
        if part & 2:
            P.op('dve', lambda e: e.tensor_tensor(ggT[:, :, :], modT[:, l, (base + 2) * 8:(base + 3) * 8, :],
                                                  gpost[:, l, :].unsqueeze(2).to_broadcast([128, 8, NS]), ALU.mult),
                 reads=[t_mod, t_par], writes=[t_ggT])

    def build_gate(sample):
        M = TS if sample else 128
        rep = t2[:, :].rearrange("p (k m) -> p k m", k=8)
        if sample:
            P.op('dve', lambda e: e.tensor_copy(
                rep[:, :, 0:TS].rearrange("p k (n t) -> p k n t", t=LS),
                ggT[:, :, 1:NS].unsqueeze(3).to_broadcast([128, 8, NSEQ_S, LS])), reads=[t_ggT], writes=[t_t2])
        else:
            P.op('dve', lambda e: e.tensor_copy(rep[:, :, :], ggT[:, :, 0:1].to_broadcast([128, 8, 128])),
                 reads=[t_ggT], writes=[t_t2])
        for half in range(2):
            pt, ptk = psum()
            for kk in range(4):
                k = half * 4 + kk
                P.op('pe', lambda e, pt=pt, k=k, kk=kk: e.transpose(pt[0:M, kk * 128:(kk + 1) * 128], rep[:, k, 0:M], ident_f[:, :]),
                     reads=[t_t2, t_const], writes=[ptk])
            P.op('act', lambda e, pt=pt, half=half: e.copy(gg[0:M, half * 512:(half + 1) * 512], pt[0:M, :]),
                 reads=[ptk], writes=[t_gg])

    def load_x(l, ph, ti, par, j):
        npart = TS if ti == 16 else 128
        P.dma('sp', xb[par][0:npart, j, :], x_src(l, ph, ti), reads=[t_x[ti]], writes=[t_xb[par]] + ([T_["xxT"]] if par == 2 else []))

    def rstd_from(ssum_ap, out_ap, n, np_, tk=None):
        t_stat = tk if tk is not None else t_stat_glob
        P.op('act', lambda e: e.activation(out_ap, ssum_ap, AF.Ln, bias=EPS, scale=1.0 / n), reads=[t_stat], writes=[t_stat])
        P.op('act', lambda e: e.activation(out_ap, out_ap, AF.Exp, scale=-0.5), reads=[t_stat], writes=[t_stat])

    def prenorm(par, j, hpar, col0, sample, part=3):
        np_ = TS if sample else 128
        xt = xb[par][0:np_, j, :]
        if part & 1:
            prenorm_a(par, j, sample)
        if part & 2:
            prenorm_b(hpar, col0, sample, evac_dve=True)

    def prenorm_a(par, j, sample):
        np_ = TS if sample else 128
        xt = xb[par][0:np_, j, :]
        P.op('act', lambda e: e.activation(xn[0:np_, :], xt, AF.Square, accum_out=stat[0:np_, 0:1]),
             reads=[t_xb[par]], writes=[t_xn, t_stat_pre])
        rstd_from(stat[0:np_, 0:1], stat[0:np_, 1:2], D, np_, t_stat_pre)
        P.op('dve', lambda e: e.tensor_scalar(xn[0:np_, :], xt, stat[0:np_, 1:2], None, ALU.mult),
             reads=[t_xb[par], t_stat_pre], writes=[t_xn])

    def prenorm_b(hpar, col0, sample, evac_dve=False):
        np_ = TS if sample else 128
        pt, ptk = psum()
        ptb = pt.bitcast(BF16)
        for k in range(8):
            P.op('pe', lambda e, k=k: e.transpose(ptb[:, k * 128:k * 128 + np_], xn[0:np_, k * 128:(k + 1) * 128],
                                                  ident_b[0:np_, 0:np_]), reads=[t_xn, t_const], writes=[ptk])
        if not sample and evac_dve:
            for k in range(8):
                P.op('dve', lambda e, k=k: e.tensor_scalar(hT[hpar][:, k, col0:col0 + 128], ptb[:, k * 128:(k + 1) * 128],
                                                           gsT[:, k, 0:1], shT[:, k, 0:1], ALU.mult, ALU.add),
                     reads=[ptk, t_gs], writes=[t_hT[hpar]])
        elif not sample:
            for k in range(8):
                P.op('act', lambda e, k=k: e.activation(hT[hpar][:, k, col0:col0 + 128], ptb[:, k * 128:(k + 1) * 128],
                                                        AF.Identity, bias=shT[:, k, 0:1], scale=gsT[:, k, 0:1]),
                     reads=[ptk, t_gs], writes=[t_hT[hpar]])
        else:
            hv = hT[hpar][:, :, col0:col0 + TS].rearrange("p k (n t) -> p k n t", t=LS)
            pv = ptb[:, :].rearrange("p (k m) -> p k m", k=8)[:, :, 0:TS].rearrange("p k (n t) -> p k n t", t=LS)
            tmpv = t2[:, 0:512].rearrange("p (k n t) -> p k n t", k=8, t=LS)
            P.op('dve', lambda e: e.tensor_tensor(tmpv, pv, gsT[:, :, 1:NS].unsqueeze(3).to_broadcast([128, 8, NSEQ_S, LS]), ALU.mult),
                 reads=[ptk, t_gs], writes=[t_t2])
            P.op('dve', lambda e: e.tensor_tensor(hv, tmpv, shT[:, :, 1:NS].unsqueeze(3).to_broadcast([128, 8, NSEQ_S, LS]), ALU.add),
                 reads=[t_t2, t_gs], writes=[t_hT[hpar]])

    def postnorm_store(l, ph, ti, par, j, ybanks, sample):
        np_ = TS if sample else 128
        for h2 in range(2):
            pt, ptk = ybanks[h2]
            P.op('act', lambda e, pt=pt, h2=h2: e.activation(junk[0:np_, h2 * 512:(h2 + 1) * 512], pt[0:np_, :], AF.Square,
                                                             accum_out=stat[0:np_, 4 + h2:5 + h2]),
                 reads=[ptk], writes=[t_t2, t_stat])
        P.op('dve', lambda e: e.tensor_tensor(stat[0:np_, 6:7], stat[0:np_, 4:5], stat[0:np_, 5:6], ALU.add), reads=[t_stat], writes=[t_stat])
        rstd_from(stat[0:np_, 6:7], stat[0:np_, 7:8], D, np_)
        for h2 in range(2):
            pt, ptk = ybanks[h2]
            P.op('dve', lambda e, pt=pt, h2=h2: e.tensor_tensor(t2[0:np_, h2 * 512:(h2 + 1) * 512], pt[0:np_, :],
                                                                gg[0:np_, h2 * 512:(h2 + 1) * 512], ALU.mult),
                 reads=[ptk, t_gg], writes=[t_t2])
        xt = xb[par][0:np_, j, :]
        P.op('dve', lambda e: e.scalar_tensor_tensor(xt, t2[0:np_, :], stat[0:np_, 7:8], xt, ALU.mult, ALU.add),
             reads=[t_t2, t_stat, t_xb[par]], writes=[t_xb[par]])
        P.dma('sp', x_dst(l, ph, ti), xt, reads=[t_xb[par]], writes=[t_x[ti]])

    arena = sb("arena", [128, 4160], F32)
    ffT = arena[:, 0:2048].bitcast(BF16).rearrange("p (j t) -> p j t", j=32)
    t_S0b = Tok("S0b")
    t_ff = t_S0b
    rbuf = [arena[:, 2048:2304].bitcast(BF16), arena[:, 2304:2560].bitcast(BF16)]
    t_rb = [Tok("rb0"), Tok("rb1")]

    def phase_mlp(l, ru, rd, after_last_up=None):
        setup_mod(l, 1)
        wup = W[ru][:, :].rearrange("p (k n) -> p k n", k=8)
        wdn = W[rd][:, :].rearrange("p (j n) -> p j n", j=32)
        blocks = [(ti, 1, ti == 16) for ti in range(17)]
        build_gate(False)

        def up(bi, jfs):
            ti0, nt, sample = blocks[bi]
            par = bi % 2
            T = TS if sample else 128
            for jg in jfs:
                pt, ptk = psum()
                for jj in range(4):
                    jf = jg * 4 + jj
                    for k in range(8):
                        P.op('pe', lambda e, pt=pt, k=k, jf=jf, jj=jj, T=T, par=par: e.matmul(
                            pt[:, jj * 128:jj * 128 + T], wup[:, k, jf * 128:(jf + 1) * 128], hT[par][:, k, 0:T], start=(k == 0), stop=(k == 7)),
                            reads=[Wt[ru][k], t_hT[par]], writes=[ptk])
                rp = jg % 2
                pv_ = pt[:, :].rearrange("p (j t) -> p j t", j=4)[:, :, 0:T]
                rv_ = rbuf[rp][:, :].rearrange("p (j t) -> p j t", j=4)[:, :, 0:T]
                P.op('act', lambda e, pv_=pv_, rv_=rv_: e.activation(rv_, pv_, AF.Relu), reads=[ptk], writes=[t_rb[rp], T_["S0f"]])
                P.op('dve', lambda e, jg=jg, T=T, rv_=rv_: e.tensor_tensor(ffT[:, jg * 4:(jg + 1) * 4, 0:T], rv_, rv_, ALU.mult),
                     reads=[t_rb[rp]], writes=[t_ff])

        nb = len(blocks)
        load_x(l, 1, 0, 0, 0)
        load_x(l, 1, 1, 1, 0)
        prenorm(0, 0, 0, 0, False)
        for bi, (ti0, nt, sample) in enumerate(blocks):
            par = bi % 2
            xpar = bi % 3
            if bi + 2 < nb:
                load_x(l, 1, bi + 2, (bi + 2) % 3, 0)
            if bi + 1 < nb:
                prenorm_a((bi + 1) % 3, 0, blocks[bi + 1][2])
            up(bi, range(0, 4))
            if bi + 1 < nb:
                prenorm_b((bi + 1) % 2, 0, blocks[bi + 1][2], evac_dve=True)
            up(bi, range(4, 8))
            if bi + 1 == nb and after_last_up is not None:
                after_last_up()
            if sample:
                build_gate(True)
            np_ = TS if sample else 128
            yb = [psum(), psum()]
            for h2 in range(2):
                pt, ptk = yb[h2]
                for jf in range(32):
                    P.op('pe', lambda e, pt=pt, jf=jf, h2=h2, np_=np_: e.matmul(
                        pt[0:np_, :], ffT[:, jf, 0:np_], wdn[:, jf, h2 * 512:(h2 + 1) * 512],
                        start=(jf == 0), stop=(jf == 31)), reads=[t_ff, Wt[rd][jf // 4]], writes=[ptk])
            postnorm_store(l, 1, ti0, xpar, 0, yb, sample)

    LN8 = float(np.log(8.0))
    sigc_ = sb("sigc", [128, 2, 128], F32)
    sigc = sigc_
    uT = sb("uT", [128, 2, 128], BF16)
    vg = sb("vg", [128, 256], F32)
    vln = sb("vln", [128, 256], BF16)
    sqT = sb("sqT", [128, 4, 128], BF16)
    fT = sb("fT", [128, 4, 128], F32)
    lfT = sb("lfT", [128, 4, 128], F32)
    bT = sb("bT", [128, 4, 128], F32)
    r1 = sb("r1", [128, 4, 128], F32)
    t1 = r1
    osq = sqT
    acc2 = sigc_
    gT = sb("gT", [128, 4, 16], F32)
    qinT = sb("qinT", [128, 4, 128], BF16)
    kinT = sb("kinT", [128, 4, 128], BF16)
    kdT = sb("kdT", [128, 4, 128], BF16)
    qinTz = sb("qinTz", [128, 4, 2, 128], BF16)
    sgT = sb("sgT", [128, 4, 128], BF16)
    v_tok = sb("v_tok", [128, 512], BF16)
    kd_tok = sb("kd_tok", [128, 512], BF16)
    PT = sb("PT", [128, 2, 4, 128], BF16)
    S = sb("S", [128, 4, 64], F32)
    Sb = sb("Sb", [128, 8, 4, 64], BF16)
    kdmp = sb("kdmp", [128, 2, 512], BF16)
    ycatT = sb("ycatT", [128, 8, 128], BF16)
    xgT = [sb("xgT%d" % i, [128, 2, 30 + 128], BF16) for i in range(2)]
    Sg = sb("Sg", [128, 4, 64], F32)
    acc = sb("acc", [128, 2, 128], F32)
    cst = sb("cst", [128, 3, 128], F32)
    wsT = sb("wsT", [128, 4, 128], BF16)
    wsTf = bT
    bs_row = arena[0:1, 2048:2560].rearrange("p (h t) -> p h t", h=4)
    wsTs = sb("wsTs", [64, 4, 64], BF16)
    wsTsf = r1[0:64, :, 0:64]
    bs_row_s4 = arena[0:1, 2560:2816].rearrange("p (c a b) -> p c a b", c=2, a=2)
    xg_tok = acc[:, :, :].rearrange("p c t -> p (c t)")[0:64, :]
    sig_tok = cst[:, 0:2, :].rearrange("p c t -> p (c t)")[0:64, :]
    S0b = arena[:, 0:2048].bitcast(BF16).rearrange("p (n h v) -> p n h v", n=16, v=64)
    S0f = arena[:, 2048:3072].rearrange("p (n h v) -> p n h v", n=4, v=64)
    xxT = arena[:, 3072:4160].rearrange("p (c n j) -> p c n j", c=2, j=34)
    kdm = kdmp[0:64, :, :]
    scv_t = t2[0:120, :].rearrange("p (g c) -> p g c", g=4)
    tmpc = lfT[:, :, :].rearrange("p h t -> p (h t)")[:, 0:496].rearrange("p (n j) -> p n j", j=31)
    T_ = {n: Tok(n) for n in ["uT", "vg", "vln", "sqT", "fT", "lfT", "bT", "gT", "qinT", "kinT", "kdT", "qinTz", "sgT",
                              "v_tok", "kd_tok", "PT", "S", "Sb", "osq", "r1", "t1", "ycatT", "xg0", "xg1", "sigc", "acc",
                              "acc2", "cst", "ws", "wss", "xg_tok", "S0f", "S0b", "kdm", "xxT", "scv_t", "tmpc"]}
    T_["S0b"] = t_S0b
    for _n in ("ycat_a", "ycat_b", "ycat_c"):
        T_[_n] = Tok(_n)
    T_["Sg"] = Tok("Sg")
    T_["S0"] = T_["S"]
    T_["S1"] = Tok("S2")
    T_["t1"] = T_["r1"]
    T_["osq"] = T_["sqT"]
    T_["acc2"] = T_["sigc"]
    T_["scv_t"] = t_t2
    T_["tmpc"] = T_["lfT"]

    def prep_layer_mixer(l, rw):
        Dg = W[1 - rw][:, 24576:32768].rearrange("p (i c) -> p i c", c=128)
        dgtk = [Wt[1 - rw][6], Wt[1 - rw][7]]
        P.op('dve', lambda e: e.tensor_tensor(Dg[:, 0:62, :], ident_b[:, :].unsqueeze(1).to_broadcast([128, 62, 128]),
                                              cw[:, l, :, :].rearrange("p c j -> p (c j)").unsqueeze(2).to_broadcast([128, 62, 128]), ALU.mult),
             reads=[t_const, t_par], writes=dgtk)

    def prep_layer_mixer_b(l, rw):
        P.dma('sp', lng[:, :], a_ln_g[l:l + 1, :].to_broadcast([128, 256]), writes=[t_ln])
        P.dma('sp', lnb[:, :], a_ln_b[l:l + 1, :].to_broadcast([128, 256]), writes=[t_ln])
        P.dma('sp', wsTf[:, :, :], a_w_s[l].rearrange("h t s -> t h s"), writes=[T_["ws"], T_["bT"]])
        pt, ptk = psum()
        for h in range(4):
            P.op('pe', lambda e, h=h: e.transpose(pt[:, h * 128:(h + 1) * 128], wsTf[:, h, :], ident_f[:, :]),
                 reads=[T_["ws"], T_["bT"], t_const], writes=[ptk])
        P.op('dve', lambda e: e.tensor_copy(wsTf[:, :, :].rearrange("s h t -> s (h t)"), pt[:, :]), reads=[ptk], writes=[T_["ws"], T_["bT"]])
        P.op('pool', lambda e: e.affine_select(wsT[:, :, :], wsTf[:, :, :], [[0, 4], [1, 128]], ALU.is_ge, 0.0, base=0,
                                               channel_multiplier=-1), reads=[T_["ws"], T_["bT"]], writes=[T_["ws"]])
        P.dma('sp', bs_row, a_b_s[l:l + 1, :, :], writes=[T_["ws"], T_["S0f"]])
        P.op('pool', lambda e: e.memset(wsTsf, 0.0), writes=[T_["wss"], T_["r1"]])
        for n in range(NSEQ_S):
            P.dma('sp', r1[4 * n:4 * n + 4, :, 4 * n:4 * n + 4], wsTf[0:4, :, 0:4], reads=[T_["ws"], T_["bT"]],
                  writes=[T_["wss"], T_["r1"]])

    def prep_layer_mixer_c(l, rw):
        P.op('dve', lambda e: e.tensor_tensor(wsTs[:, :, :], wsTsf, mask_s[:, :].unsqueeze(1).to_broadcast([64, 4, 64]), ALU.mult),
             reads=[T_["wss"], T_["r1"], t_const], writes=[T_["wss"]])


    _a = arena
    FS = [(uT, vg, vln, sqT, fT, sgT, v_tok),
          (_a[:, 0:128].bitcast(BF16).rearrange("p (c t) -> p c t", c=2),
           _a[:, 128:384],
           _a[:, 384:512].bitcast(BF16),
           _a[:, 512:768].bitcast(BF16).rearrange("p (c t) -> p c t", c=4),
           _a[:, 768:1280].rearrange("p (c t) -> p c t", c=4),
           _a[:, 1280:1536].bitcast(BF16).rearrange("p (c t) -> p c t", c=4),
           _a[:, 1536:1792].bitcast(BF16))]
    FTOK = [{k: T_[k] for k in ["uT", "vg", "vln", "sqT", "fT", "sgT", "v_tok"]},
            {k: Tok(k + "_1") for k in ["uT", "vg", "vln", "sqT", "fT", "sgT", "v_tok"]}]
    xb.append(arena[:, 3072:4096].rearrange("p (j d) -> p j d", j=1))
    t_xb.append(Tok("xb2"))

    def mixer_tile(l, rw, ti, sample, last_prompt):
        Dg = W[1 - rw][:, 24576:32768].rearrange("p (i c) -> p i c", c=128)
        dgtk = [Wt[1 - rw][6], Wt[1 - rw][7]]
        par = ti % 2
        xpar = ti % 3
        T = TS if sample else 128
        C = LS if sample else 16
        nch = T // C
        win = W[rw][:, 0:24576].rearrange("p (k n) -> p k n", k=8)
        wout = W[rw][:, 24576:32768].rearrange("p (k n) -> p k n", k=8)
        wtk = [Wt[rw][i] for i in range(6)]
        wotk = [Wt[rw][6], Wt[rw][7]]
        hpar = par
        h_ = hT[hpar]
        th = t_hT[hpar]
        mask = mask_s if sample else mask_p
        mres = mres_s if sample else mres_p
        xg = xgT[par]
        txg = T_["xg%d" % par]

        def fm_group(c0, nchunk):
            pt, ptk = psum()
            for jc in range(nchunk):
                for k in range(8):
                    P.op('pe', lambda e, pt=pt, jc=jc, k=k: e.matmul(pt[:, jc * 128:jc * 128 + T], win[:, k, c0 + jc * 128:c0 + (jc + 1) * 128],
                                                                     h_[:, k, 0:T], start=(k == 0), stop=(k == 7)),
                         reads=wtk + [th], writes=[ptk])
            return pt, ptk

        def tm_group(c0, ncol, t0=0, tn=None):
            tn = T if tn is None else tn
            pt, ptk = psum()
            for k in range(8):
                P.op('pe', lambda e, pt=pt, k=k: e.matmul(pt[0:tn, 0:ncol], h_[:, k, t0:t0 + tn], win[:, k, c0:c0 + ncol],
                                                          start=(k == 0), stop=(k == 7)), reads=wtk + [th], writes=[ptk])
            return pt, ptk

        def v3(pt, n):
            return pt[:, 0:n * 128].rearrange("p (c t) -> p c t", t=128)[:, :, 0:T]

        uT, vg, vln, sqT, fT, sgT, v_tok = FS[par]
        osq = sqT
        TT = dict(T_)
        TT.update(FTOK[par])
        TT["osq"] = TT["sqT"]
        prenorm(xpar, 0, par, 0, sample)
        yield
        pq, pqk = fm_group(512, 4)
        P.op('act', lambda e: e.activation(sqT[:, :, 0:T], v3(pq, 4), AF.Silu), reads=[pqk], writes=[TT["sqT"]])
        pg, pgk = fm_group(2048, 4)
        P.op('act', lambda e: e.activation(sgT[:, :, 0:T], v3(pg, 4), AF.Silu), reads=[pgk], writes=[TT["sgT"]])
        pu, puk = fm_group(0, 2)
        P.op('act', lambda e: e.activation(uT[:, :, 0:T], v3(pu, 2), AF.Gelu), reads=[puk], writes=[TT["uT"]])
        pv, pvk = tm_group(256, 256)
        P.op('act', lambda e: e.activation(vg[0:T, :], pv[0:T, 0:256], AF.Gelu), reads=[pvk], writes=[TT["vg"]])
        pf, pfk = fm_group(1024, 4)
        P.op('act', lambda e: e.activation(fT[:, :, 0:T], v3(pf, 4), AF.Sigmoid), reads=[pfk], writes=[TT["fT"]])
        pc, pck = fm_group(2560, 4)
        P.op('act', lambda e: e.activation(sigc[:, :, 0:T], v3(pc, 4)[:, 2:4, :], AF.Sigmoid), reads=[pck], writes=[TT["sigc"]])
        pi_, pik = tm_group(1536, 512)
        P.op('act', lambda e: e.copy(v_tok[0:T, :], pi_[0:T, :]), reads=[pik], writes=[TT["v_tok"]])
        if sample or last_prompt:
            t0, tn = (0, TS) if sample else (96, 32)
            pz, pzk = tm_group(2560, 512, t0, tn)
            P.op('act', lambda e: e.activation(sig_tok[0:tn, :], pz[0:tn, 256:512], AF.Sigmoid), reads=[pzk], writes=[TT["cst"]])
        P.op('act', lambda e: e.activation(stat[:, 2:3], ones_f[:, 0:1], AF.Ln), reads=[t_const], writes=[t_stat_glob2])
        yield
        if sample:
            xgdst = xxT[:, :, :, 30:34]
            P.op('dve', lambda e: e.tensor_tensor(xgdst, v3(pc, 4)[:, 0:2, :].rearrange("p c (n t) -> p c n t", t=LS),
                                                  sigc[:, :, 0:T].rearrange("p c (n t) -> p c n t", t=LS), ALU.mult),
                 reads=[pck, TT["sigc"]], writes=[TT["xxT"], t_xb[2]])
        else:
            P.op('dve', lambda e: e.tensor_tensor(xg[:, :, 30:30 + T], v3(pc, 4)[:, 0:2, :], sigc[:, :, 0:T], ALU.mult),
                 reads=[pck, TT["sigc"]], writes=[txg])
        if sample or last_prompt:
            P.op('dve', lambda e: e.tensor_tensor(xg_tok[0:tn, :], pz[0:tn, 0:256], sig_tok[0:tn, :], ALU.mult),
                 reads=[pzk, TT["cst"]], writes=[TT["acc"]])
            if sample:
                for t in range(LS):
                    P.dma('sp', cs_o[l, :, 26 + t, :], xg_tok[t:TS:LS, :], reads=[TT["acc"]])
                P.dma('sp', cs_o[l, :, 0:26, :], scv[l, :, 4:30, :])
            else:
                P.dma('sp', cp_o[l, :, :], xg_tok[2:32, :], reads=[TT["acc"]])
        if not sample:
            nx = xgT[1 - par]
            P.op('pool', lambda e: e.tensor_copy(nx[:, :, 0:30], xg[:, :, T:T + 30]), reads=[txg], writes=[TT["xg%d" % (1 - par)]])
            pcv, pcvk = ps[:, 7, :], ps_tok[7]
            for cc in range(2):
                for j in range(31):
                    P.op('pe', lambda e, cc=cc, j=j: e.matmul(pcv[:, cc * 128:cc * 128 + T], Dg[:, cc * 31 + j, :], xg[:, cc, j:j + T],
                                                              start=(j == 0), stop=(j == 30)), reads=[txg] + dgtk, writes=[pcvk])
        yield
        P.begin_chain()
        P.op('dve', lambda e: e.bn_stats(stat[0:T, 8:14], vg[0:T, :]), reads=[TT["vg"]], writes=[t_stat_lnv])
        P.op('dve', lambda e: e.bn_aggr(stat[0:T, 14:16], stat[0:T, 8:14]), reads=[t_stat_lnv], writes=[t_stat_lnv])
        P.op('act', lambda e: e.activation(stat[0:T, 15:16], stat[0:T, 15:16], AF.Ln, bias=EPS, scale=1.0), reads=[t_stat_lnv], writes=[t_stat_lnv])
        P.op('act', lambda e: e.activation(stat[0:T, 15:16], stat[0:T, 15:16], AF.Exp, scale=-0.5), reads=[t_stat_lnv], writes=[t_stat_lnv])
        P.op('dve', lambda e: e.tensor_scalar(vg[0:T, :], vg[0:T, :], stat[0:T, 14:15], stat[0:T, 15:16], ALU.subtract, ALU.mult),
             reads=[TT["vg"], t_stat_lnv], writes=[TT["vg"]])
        P.op('dve', lambda e: e.tensor_tensor(vg[0:T, :], vg[0:T, :], lng[0:T, :], ALU.mult), reads=[TT["vg"], t_ln], writes=[TT["vg"]])
        P.op('dve', lambda e: e.tensor_tensor(vg[0:T, :], vg[0:T, :], lnb[0:T, :], ALU.add), reads=[TT["vg"], t_ln], writes=[TT["vg"]])
        P.op('act', lambda e: e.copy(vln[0:T, :], vg[0:T, :]), reads=[TT["vg"]], writes=[TT["vln"]])
        if sample:
            P.dma('sp', gv_o[l].rearrange("n t c -> (n t) c"), vg[0:T, :], reads=[TT["vg"]])
        P.end_chain()
        P.begin_chain()
        if l == 1:
            for hp in range(4):
                P.op('dve', lambda e, hp=hp: e.tensor_scalar(fT[:, hp, 0:T], fT[:, hp, 0:T], oml1[:, hp:hp + 1], lb1[:, hp:hp + 1], ALU.mult, ALU.add),
                     reads=[TT["fT"], t_par], writes=[TT["fT"]])
        P.op('act', lambda e: e.activation(lfT[:, :, 0:T], fT[:, :, 0:T], AF.Ln), reads=[TT["fT"]], writes=[TT["lfT"]])
        P.op('dve', lambda e: e.tensor_scalar(fT[:, :, 0:T], fT[:, :, 0:T], -1.0, 1.0, ALU.mult, ALU.add), reads=[TT["fT"], TT["lfT"]], writes=[TT["fT"]])
        for hp in range(4):
            P.op('dve', lambda e, hp=hp: e.tensor_tensor_scan(bT[:, hp, 0:T], mres[:, 0:T], lfT[:, hp, 0:T], 0.0, ALU.mult, ALU.add),
                 reads=[TT["lfT"], t_const], writes=[TT["bT"]])
        bview = bT[:, :, 0:T].rearrange("p h (c i) -> p h c i", i=C)[:, :, :, C - 1]
        P.op('act', lambda e: e.activation(gT[:, :, 0:nch], bview, AF.Exp), reads=[TT["bT"]], writes=[TT["gT"]])
        P.op('act', lambda e: e.activation(lfT[:, :, 0:T], bT[:, :, 0:T], AF.Exp, bias=-LN8), reads=[TT["bT"], TT["lfT"]], writes=[TT["lfT"]])
        P.op('dve', lambda e: e.tensor_tensor(qinT[:, :, 0:T], sqT[:, :, 0:T], lfT[:, :, 0:T], ALU.mult), reads=[TT["sqT"], TT["lfT"]], writes=[TT["qinT"]])
        P.op('act', lambda e: e.activation(bT[:, :, 0:T], bT[:, :, 0:T], AF.Exp, scale=-1.0), reads=[TT["bT"], TT["gT"], TT["lfT"]], writes=[TT["bT"]])
        P.op('dve', lambda e: e.tensor_tensor(kinT[:, :, 0:T], fT[:, :, 0:T], bT[:, :, 0:T], ALU.mult), reads=[TT["fT"], TT["bT"]], writes=[TT["kinT"]])
        P.op('dve', lambda e: e.tensor_tensor(kdT[:, :, 0:T].rearrange("p h (c i) -> p h c i", i=C),
                                              kinT[:, :, 0:T].rearrange("p h (c i) -> p h c i", i=C),
                                              gT[:, :, 0:nch].unsqueeze(3).to_broadcast([128, 4, nch, C]), ALU.mult),
             reads=[TT["kinT"], TT["gT"]], writes=[TT["kdT"]])
        P.end_chain()
        P.begin_chain()
        if sample:
            for g4 in range(4):
                P.dma('sp', scv_t[:, g4, :], scv[l, 4 * g4:4 * g4 + 4].rearrange("n j c -> (n j) c"), writes=[TT["scv_t"]])
            for cc in range(2):
                pt, ptk = psum()
                for g4 in range(4):
                    P.op('pe', lambda e, pt=pt, g4=g4, cc=cc: e.transpose(pt[:, g4 * 120:(g4 + 1) * 120], scv_t[:, g4, cc * 128:(cc + 1) * 128], ident_f[0:120, 0:120]),
                         reads=[TT["scv_t"], t_const], writes=[ptk])
                P.op('act', lambda e, pt=pt, cc=cc: e.copy(xxT[:, cc, :, 0:30], pt[:, 0:480].rearrange("p (n j) -> p n j", j=30)),
                     reads=[ptk], writes=[TT["xxT"], t_xb[2]])
            for cc in range(2):
                for t in range(LS):
                    P.op('dve', lambda e, cc=cc, t=t: e.tensor_tensor(tmpc[:, :, :], xxT[:, cc, :, t:t + 31],
                                                                      cw[:, l, cc, :].unsqueeze(1).to_broadcast([128, 16, 31]), ALU.mult),
                         reads=[TT["xxT"], t_par], writes=[TT["tmpc"]])
                    P.op('dve', lambda e, cc=cc, t=t: e.tensor_reduce(acc[:, cc, t:TS:LS], tmpc[:, :, :], AX.X, ALU.add),
                         reads=[TT["tmpc"]], writes=[TT["acc"]])
                P.op('dve', lambda e, cc=cc: e.tensor_scalar(acc[:, cc, 0:T], acc[:, cc, 0:T], cb[:, l, cc:cc + 1], None, ALU.add),
                     reads=[TT["acc"], t_par], writes=[TT["acc"]])
        else:
            for cc in range(2):
                P.op('dve', lambda e, cc=cc: e.tensor_scalar(acc[:, cc, 0:T], pcv[:, cc * 128:cc * 128 + T], cb[:, l, cc:cc + 1], None, ALU.add),
                     reads=[pcvk, t_par], writes=[TT["acc"]])
        P.op('act', lambda e: e.activation(acc2[:, :, 0:T], acc[:, :, 0:T], AF.Square), reads=[TT["acc"]], writes=[TT["acc2"]])
        pl_, plk = psum()
        for cc in range(2):
            P.op('pe', lambda e, cc=cc: e.matmul(pl_[:, 0:T], ones_f[:, :], acc[:, cc, 0:T], start=(cc == 0), stop=(cc == 1)),
                 reads=[TT["acc"], t_const], writes=[plk])
        for cc in range(2):
            P.op('pe', lambda e, cc=cc: e.matmul(pl_[:, 128:128 + T], ones_f[:, :], acc2[:, cc, 0:T], start=(cc == 0), stop=(cc == 1)),
                 reads=[TT["acc2"], t_const], writes=[plk])
        P.op('dve', lambda e: e.tensor_scalar(cst[:, 0, 0:T], pl_[:, 0:T], 1.0 / 256, None, ALU.mult), reads=[plk], writes=[TT["cst"]])
        P.op('dve', lambda e: e.tensor_tensor(cst[:, 1, 0:T], cst[:, 0, 0:T], cst[:, 0, 0:T], ALU.mult), reads=[TT["cst"]], writes=[TT["cst"]])
        P.op('dve', lambda e: e.scalar_tensor_tensor(cst[:, 1, 0:T], pl_[:, 128:128 + T], 1.0 / 256, cst[:, 1, 0:T], ALU.mult, ALU.subtract),
             reads=[plk, TT["cst"]], writes=[TT["cst"]])
        P.op('act', lambda e: e.activation(cst[:, 1, 0:T], cst[:, 1, 0:T], AF.Ln, bias=EPS, scale=1.0), reads=[TT["cst"]], writes=[TT["cst"]])
        P.op('act', lambda e: e.activation(cst[:, 1, 0:T], cst[:, 1, 0:T], AF.Exp, scale=-0.5), reads=[TT["cst"]], writes=[TT["cst"]])
        P.op('dve', lambda e: e.tensor_tensor(acc[:, :, 0:T], acc[:, :, 0:T], cst[:, 0:1, 0:T].to_broadcast([128, 2, T]), ALU.subtract),
             reads=[TT["acc"], TT["cst"]], writes=[TT["acc"]])
        P.op('dve', lambda e: e.tensor_tensor(acc[:, :, 0:T], acc[:, :, 0:T], cst[:, 1:2, 0:T].to_broadcast([128, 2, T]), ALU.mult),
             reads=[TT["acc"], TT["cst"]], writes=[TT["acc"]])
        P.end_chain()
        yield
        for hl in range(2):
            P.op('dve', lambda e, hl=hl: e.tensor_scalar(qinTz[:, :, hl, 0:T], qinT[:, :, 0:T], mhl[:, hl:hl + 1], None, ALU.mult),
                 reads=[TT["qinT"], t_const], writes=[TT["qinTz"]])
        pm, pmk = psum()
        wsv = wsTs if sample else wsT
        twsv = TT["wss"] if sample else TT["ws"]
        tbsr = TT["S0f"]
        for h in range(4):
            c2, hl = h // 2, h % 2
            o_ap = pm[hl * 64:(hl + 1) * 64, c2 * 128:c2 * 128 + T]
            P.op('pe', lambda e, o_ap=o_ap, h=h: e.matmul(o_ap, vln[0:T, h * 64:(h + 1) * 64], wsv[0:T, h, 0:T], start=True, stop=False),
                 reads=[TT["vln"], twsv], writes=[pmk])
            P.op('pe', lambda e, o_ap=o_ap, h=h: e.matmul(o_ap, ones_f[0:1, 0:64], (bs_row_s4[0:1, h // 2, h % 2, 0:T] if sample else bs_row[0:1, h, 0:T]), start=False, stop=True),
                 reads=[twsv, tbsr, t_const], writes=[pmk])
        P.op('dve', lambda e: e.tensor_tensor(ycatT[:, 0:2, 0:T], uT[:, :, 0:T], v3(pm, 2), ALU.mult), reads=[pmk, TT["uT"]], writes=[TT["ycat_a"]])

        pk, pkk = psum()
        pkb = pk.bitcast(BF16)
        for hp in range(4):
            P.op('pe', lambda e, hp=hp: e.transpose(pkb[0:T, hp * 128:(hp + 1) * 128], kdT[:, hp, 0:T], ident_b[:, :]),
                 reads=[TT["kdT"], t_const], writes=[pkk])
        P.op('act', lambda e: e.copy(kd_tok[0:T, :], pkb[0:T, 0:512]), reads=[pkk], writes=[TT["kd_tok"]])
        sc = [psum(), psum()]
        for h in range(8):
            hp, hl = h // 2, h % 2
            pt, ptk = sc[hl]
            P.op('pe', lambda e, pt=pt, hp=hp, hl=hl: e.matmul(pt[0:T, hp * 128:hp * 128 + T], kinT[hl * 64:(hl + 1) * 64, hp, 0:T],
                                                               qinT[hl * 64:(hl + 1) * 64, hp, 0:T], start=True, stop=True),
                 reads=[TT["kinT"], TT["qinT"]], writes=[ptk])
        for hl in range(2):
            pt, ptk = sc[hl]
            P.op('dve', lambda e, pt=pt, hl=hl: e.tensor_tensor(PT[0:T, hl, :, 0:T], v3(pt, 4)[0:T], mask[0:T, 0:T].unsqueeze(1).to_broadcast([T, 4, T]), ALU.mult),
                 reads=[ptk, t_const], writes=[TT["PT"]])
        if not sample:
            for q in range(2):
                P.op('dve', lambda e, q=q: e.tensor_scalar(kdmp[:, q, :], kd_tok[:, :], mpar[:, q:q + 1], None, ALU.mult),
                     reads=[TT["kd_tok"], t_const], writes=[TT["kdm"]])
            ubanks = []
            for c in range(8):
                if c % 2 == 0:
                    ubanks.append(psum(hold=True))
                pu_, puk_ = ubanks[c // 2]
                q = c % 2
                b32 = 32 * (c // 2)
                for h in range(8):
                    hp, hl = h // 2, h % 2
                    P.op('pe', lambda e, pu_=pu_, hp=hp, hl=hl, h=h, q=q, b32=b32: e.matmul(
                        pu_[hl * 64:(hl + 1) * 64, q * 256 + hp * 64:q * 256 + (hp + 1) * 64], kdmp[b32:b32 + 32, q, h * 64:(h + 1) * 64],
                        v_tok[b32:b32 + 32, h * 64:(h + 1) * 64], start=True, stop=True, tile_position=(b32, 64 * hl)),
                        reads=[TT["kdm"], TT["v_tok"]], writes=[puk_])
        if sample:
            s0bufs = [S0f, xb[0][:, 0, :].rearrange("p (n h v) -> p n h v", n=4, v=64)]
            s0toks = [TT["S0f"], t_xb[0]]

            def s0_load(g4):
                for hl in range(2):
                    P.dma('sp', s0bufs[g4 % 2][hl * 64:(hl + 1) * 64, :, :, :],
                          sh[l, 4 * g4:4 * g4 + 4].rearrange("n (hp hl) k v -> hl k n hp v", hl=2)[hl], writes=[s0toks[g4 % 2]])
            s0_load(0)
            s0_load(1)
            for g4 in range(4):
                sbuf_, stok_ = s0bufs[g4 % 2], s0toks[g4 % 2]
                P.op('act', lambda e, g4=g4, sbuf_=sbuf_: e.copy(S0b[:, 4 * g4:4 * g4 + 4, :, :], sbuf_[:, :, :, :]), reads=[stok_],
                     writes=[TT["S0b"]] + list(FTOK[1].values()))
                for pr in range(2):
                    pu_, puk_ = psum()
                    for q in range(2):
                        n = 4 * g4 + 2 * pr + q
                        P.op('dve', lambda e, n=n, q=q: e.tensor_scalar(kdm[:, q, :], kd_tok[0:TS, :], oneh[:, n:n + 1], None, ALU.mult),
                             reads=[TT["kd_tok"], t_const], writes=[TT["kdm"]])
                        for h in range(8):
                            hp, hl = h // 2, h % 2
                            P.op('pe', lambda e, pu_=pu_, q=q, hp=hp, hl=hl, h=h: e.matmul(
                                pu_[hl * 64:(hl + 1) * 64, q * 256 + hp * 64:q * 256 + (hp + 1) * 64], kdm[:, q, h * 64:(h + 1) * 64],
                                v_tok[0:TS, h * 64:(h + 1) * 64], start=True, stop=True), reads=[TT["kdm"], TT["v_tok"]], writes=[puk_])
                    n0 = 2 * pr
                    sv = sbuf_[:, n0:n0 + 2, :, :]
                    gb = gT[:, :, 4 * g4 + n0:4 * g4 + n0 + 2].rearrange("p h n -> p n h").unsqueeze(3).to_broadcast([128, 2, 4, 64])
                    P.op('dve', lambda e, sv=sv, gb=gb: e.tensor_tensor(sv, sv, gb, ALU.mult), reads=[stok_, TT["gT"], TT["S0b"]], writes=[stok_])
                    P.op('dve', lambda e, sv=sv, pu_=pu_: e.tensor_tensor(sv, sv, pu_[:, 0:512].rearrange("p (n h v) -> p n h v", n=2, v=64), ALU.add),
                         reads=[stok_, puk_], writes=[stok_])
                for hl in range(2):
                    P.dma('sp', hs_o[l, 4 * g4:4 * g4 + 4].rearrange("n (hp hl) k v -> hl k n hp v", hl=2)[hl],
                          sbuf_[hl * 64:(hl + 1) * 64, :, :, :], reads=[stok_])
                if g4 + 2 < 4:
                    s0_load(g4 + 2)
        for cc in range(2):
            P.op('act', lambda e, cc=cc: e.activation(ycatT[:, 6 + cc, 0:T], acc[:, cc, 0:T], AF.Silu, bias=clb[:, l, cc:cc + 1], scale=clg[:, l, cc:cc + 1]),
                 reads=[TT["acc"], t_par], writes=[TT["ycat_c"]])

        yield
        if not sample:
            for c in range(8):
                pu_, puk_ = ubanks[c // 2]
                q = c % 2
                P.op('dve', lambda e, c=c: e.tensor_tensor(Sg[:, :, :], S[:, :, :], gT[:, :, c:c + 1].to_broadcast([128, 4, 64]), ALU.mult),
                     reads=[TT["S"], TT["gT"]], writes=[TT["Sg"]])
                P.op('dve', lambda e, c=c: e.tensor_copy(Sb[:, c, :, :], S[:, :, :]), reads=[TT["S"]], writes=[TT["Sb"]])
                P.op('dve', lambda e, pu_=pu_, q=q: e.tensor_tensor(S[:, :, :], Sg[:, :, :], pu_[:, q * 256:(q + 1) * 256].rearrange("p (h v) -> p h v", v=64), ALU.add),
                     reads=[TT["Sg"], puk_], writes=[TT["S"]])
                if c % 2 == 1:
                    psum_release(puk_)
        yield
        po, pok = psum()
        for h in range(8):
            hp, hl = h // 2, h % 2
            o_ap = po[hl * 64:(hl + 1) * 64, hp * 128:hp * 128 + T]
            P.op('pe', lambda e, o_ap=o_ap, h=h, hp=hp, hl=hl: e.matmul(o_ap, v_tok[0:T, h * 64:(h + 1) * 64], PT[0:T, hl, hp, 0:T], start=True, stop=False),
                 reads=[TT["v_tok"], TT["PT"]], writes=[pok])
            for c in range(nch):
                st_ap = S0b[:, c, hp, :] if sample else Sb[:, c, hp, :]
                P.op('pe', lambda e, st_ap=st_ap, hp=hp, hl=hl, c=c: e.matmul(
                    po[hl * 64:(hl + 1) * 64, hp * 128 + c * C:hp * 128 + (c + 1) * C], st_ap, qinTz[:, hp, hl, c * C:(c + 1) * C],
                    start=False, stop=(c == nch - 1)), reads=[TT["S0b"] if sample else TT["Sb"], TT["qinTz"]], writes=[pok])
        yield
        P.op('act', lambda e: e.activation(osq[:, :, 0:T], v3(po, 4), AF.Square), reads=[pok], writes=[TT["osq"]])
        pss, pssk = psum()
        if sample:
            for hp in range(4):
                P.op('pe', lambda e, hp=hp: e.matmul(pss[:, hp * 128:hp * 128 + T], bones_b[:, :], osq[:, hp, 0:T], start=True, stop=True),
                     reads=[TT["osq"], t_const], writes=[pssk])
        else:
            P.op('pe', lambda e: e.matmul(pss[:, 0:512], bones_b[:, :], osq[:, :, :].rearrange("p c t -> p (c t)"), start=True, stop=True),
                 reads=[TT["osq"], t_const], writes=[pssk])
        P.op('act', lambda e: e.activation(r1[:, :, 0:T], v3(pss, 4), AF.Ln, bias=EPS, scale=1.0 / 64), reads=[pssk], writes=[TT["r1"]])
        P.op('act', lambda e: e.activation(r1[:, :, 0:T], r1[:, :, 0:T], AF.Exp, scale=-0.5), reads=[TT["r1"]], writes=[TT["r1"]])
        P.op('dve', lambda e: e.tensor_tensor(t1[:, :, 0:T], v3(po, 4), r1[:, :, 0:T], ALU.mult), reads=[pok, TT["r1"]], writes=[TT["t1"]])
        P.op('dve', lambda e: e.tensor_tensor(ycatT[:, 2:6, 0:T], t1[:, :, 0:T], sgT[:, :, 0:T], ALU.mult), reads=[TT["t1"], TT["sgT"]], writes=[TT["ycat_b"]])
        if last_prompt:
            for hl in range(2):
                P.dma('sp', hp_o[l].rearrange("(hp hl) k v -> hl k hp v", hl=2)[hl], S[hl * 64:(hl + 1) * 64, :, :], reads=[TT["S"]])

        yb = [psum(), psum()]
        korder = [0, 1, 6, 7, 2, 3, 4, 5]
        ytok = {0: "ycat_a", 1: "ycat_a", 6: "ycat_c", 7: "ycat_c", 2: "ycat_b", 3: "ycat_b", 4: "ycat_b", 5: "ycat_b"}
        for kk in range(0, 8, 4):
            for h2 in range(2):
                pt, ptk = yb[h2]
                for k in korder[kk:kk + 4]:
                    P.op('pe', lambda e, pt=pt, k=k, h2=h2: e.matmul(pt[0:T, :], ycatT[:, k, 0:T], wout[:, k, h2 * 512:(h2 + 1) * 512],
                                                                      start=(k == korder[0]), stop=(k == korder[-1])),
                         reads=[TT[ytok[k]]] + wotk, writes=[ptk])
        postnorm_store(l, 0, ti, xpar, 0, yb, sample)
        yield


    def phase_mix(l, rw, hooks=None, bg=None):
        setup_mod(l, 0, part=1)
        gens = [mixer_tile(l, rw, ti, ti == 16, ti == 15) for ti in range(17)]
        load_x(l, 0, 0, 0, 0)
        load_x(l, 0, 1, 1, 0)
        P.op('pool', lambda e: e.memset(S[:, :, :], 0.0), writes=[T_["S"]])
        P.op('pool', lambda e: e.memset(xgT[0][:, :, 0:30], 0.0), writes=[T_["xg0"]])
        next(gens[0])
        prep_layer_mixer(l, rw)
        next(gens[0])
        next(gens[0])
        fold_gn(l, rw)
        prep_layer_mixer_b(l, rw)
        if bg is not None:
            bg()
        setup_mod(l, 0, part=2)
        build_gate(False)
        P.begin_chain()
        next(gens[1])
        P.end_chain()
        next(gens[0])
        P.merge_chains()
        for ti in range(17):
            nxt = ti + 1 < 17
            if bg is not None:
                bg()
            if ti + 2 < 17:
                load_x(l, 0, ti + 2, (ti + 2) % 3, 0)
            if ti == 16:
                build_gate(True)
                for h in range(4):
                    P.dma('sp', bs_row_s4[0:1, h // 2, h % 2, :].rearrange("p (n t) -> p n t", t=LS),
                          a_b_s[l:l + 1, h, 0:4].unsqueeze(1).to_broadcast([1, NSEQ_S, LS]), writes=[T_["S0f"]])
            if ti == 0:
                prep_layer_mixer_c(l, rw)
            next(gens[ti])
            if nxt:
                next(gens[ti + 1])
                if ti + 1 == 16 and hooks is not None:
                    hooks[1]()
            next(gens[ti])
            if nxt:
                next(gens[ti + 1])
                if ti + 1 == 15 and hooks is not None:
                    hooks[0]()
            next(gens[ti])
            P.begin_chain()
            next(gens[ti])
            P.end_chain()
            if ti + 2 < 17:
                P.begin_chain()
                next(gens[ti + 2])
                P.end_chain()
            if nxt:
                next(gens[ti + 1])
            P.merge_chains()

    if debug == 'mlp':
        while not mod1_state['done']:
            mod1_bg()
        load_w_up(0, 1, [6, 7])
        load_w_down(0, 0)
        cur['first'] = True
        cur['last'] = True
        phase_mlp(0, 1, 0)
    elif debug == 'mix':
        cur['first'] = True
        cur['last'] = True
        while not mod1_state['done']:
            mod1_bg()
        phase_mix(0, 0)
    else:
        cur['first'] = True
        phase_mix(0, 0, hooks=(lambda: load_w_up(0, 1, [6, 7]), lambda: load_w_down(0, 0, range(24))), bg=mod1_bg)
        while not mod1_state['done']:
            mod1_bg()
        cur['first'] = False
        load_w_down(0, 0, range(24, 32))
        phase_mlp(0, 1, 0, after_last_up=lambda: load_w_in_out(1, 1))
        load_w_up(1, 0, range(6))
        phase_mix(1, 1, hooks=(lambda: load_w_up(1, 0, [6, 7]), lambda: load_w_down(1, 1, range(24))))
        load_w_down(1, 1, range(24, 32))
        cur['last'] = True
        phase_mlp(1, 0, 1)

    P.finalize()
    with nc.Block() as block:
        @block.tensor
        def _(e):
            P.emit_engine('pe', e)

        @block.scalar
        def _(e):
            P.emit_engine('act', e)

        @block.vector
        def _(e):
            P.emit_engine('dve', e)

        @block.gpsimd
        def _(e):
            P.emit_engine('pool', e)

        @block.sync
        def _(e):
            P.emit_engine('sp', e)
            P.final_waits('sp', e)
    es.close()
    return nc


def make_in_maps(inputs):
    f = lambda a: np.ascontiguousarray(np.asarray(a, dtype=np.float32))
    maps = []
    for i in range(NCORES):
        m = {}
        m["xp"] = f(inputs["x_prompt"][i])
        m["xs"] = f(inputs["x_sample"][NSEQ_S * i:NSEQ_S * (i + 1)].reshape(TS, D))
        m["sh"] = f(inputs["state_hgrn"][:, NSEQ_S * i:NSEQ_S * (i + 1)])
        m["scv"] = f(inputs["state_conv"][:, NSEQ_S * i:NSEQ_S * (i + 1)])
        m["c"] = f(np.concatenate([inputs["c_prompt"][i:i + 1], inputs["c_sample"][NSEQ_S * i:NSEQ_S * (i + 1)]], 0))
        for k in ["w_ada", "b_ada", "g_pre_mix", "g_post_mix", "g_pre_mlp", "g_post_mlp", "w_in", "a_ln_g",
                  "a_ln_b", "a_w_s", "a_b_s", "b_lb", "b_gn_g", "c_w_dw", "c_b_dw", "c_ln_g", "c_ln_b",
                  "w_out", "w_up", "w_down"]:
            m[k] = f(inputs[k])
        maps.append(m)
    return maps


def kernel(**inputs):
    nc = build()
    maps = make_in_maps(inputs)
    res = run_bass_kernel_spmd(nc, maps, core_ids=list(range(NCORES)))
    r = res.results
    y_prompt = np.stack([r[i]["yp"] for i in range(NCORES)], 0)
    y_sample = np.concatenate([r[i]["ys"].reshape(NSEQ_S, LS, D) for i in range(NCORES)], 0)
    hgrn_prompt = np.stack([r[i]["hp"] for i in range(NCORES)], 1)
    hgrn_sample = np.concatenate([r[i]["hs"] for i in range(NCORES)], 1)
    conv_prompt = np.stack([r[i]["cp"] for i in range(NCORES)], 1)
    conv_sample = np.concatenate([r[i]["cs"] for i in range(NCORES)], 1)
    gmlp_v = np.concatenate([r[i]["gv"] for i in range(NCORES)], 1)
    return (y_prompt, y_sample, hgrn_prompt, hgrn_sample, conv_prompt, conv_sample, gmlp_v)
```

```python
import numpy as np
from contextlib import ExitStack
import concourse.bass as bass
import concourse.mybir as mybir
from concourse.bass_utils import run_bass_kernel_spmd

F32 = mybir.dt.float32
BF16 = mybir.dt.bfloat16
AF = mybir.ActivationFunctionType
ALU = mybir.AluOpType
AX = mybir.AxisListType

NCORES = 8
D = 1024
SEQ = 2048
NSEQ_S = 16
LS = 4
TS = NSEQ_S * LS
DEPTH = 2
NIN = 3072
DFF = 4096
EPS = 1e-6
ENGS = ['pe', 'act', 'dve', 'pool', 'sp']
STRICT_SAME_ENGINE = True


class Tok:
    __slots__ = ('name', 'w', 'wl', 'r')

    def __init__(self, name=''):
        self.name = name
        self.w = None
        self.wl = []
        self.r = {}


class Op:
    __slots__ = ('eng', 'fn', 'deps', 'idx', 'sig', 'cnt', 'dma', 'dsem', 'dval')


class Prog:
    def __init__(self):
        self.ops = {e: [] for e in ENGS}
        self.pools = {}
        self.sems = {}
        self.chains = []
        self.cur_chain = None

    def set_dma_pool(self, queue, sems):
        self.pools[queue] = {'slots': [[s, 0, None] for s in sems], 'i': 0}

    def _dep(self, op, d, kind):
        if d is op or d is None:
            return
        if kind == 'waw' and d.dma and op.dma:
            return
        if (not d.dma) and d.eng == op.eng:
            if op.eng == 'pe':
                return
            if kind != 'raw' and not op.dma and not STRICT_SAME_ENGINE:
                return
        op.deps.append(d)

    def begin_chain(self):
        self.cur_chain = []

    def end_chain(self):
        self.chains.append(self.cur_chain)
        self.cur_chain = None

    def merge_chains(self):
        chains, self.chains = self.chains, []
        while any(chains):
            for c in chains:
                if c:
                    a = c.pop(0)
                    self.op(*a[0], **a[1])

    def op(self, eng, fn, reads=(), writes=(), dma=False):
        if getattr(self, 'cur_chain', None) is not None:
            self.cur_chain.append(((eng, fn), dict(reads=list(reads), writes=list(writes), dma=dma)))
            return None
        o = Op()
        o.eng = eng
        o.fn = fn
        o.idx = len(self.ops[eng])
        o.deps = []
        o.sig = False
        o.dma = dma
        o.cnt = 0
        o.dsem = None
        o.dval = 0
        for t in reads:
            self._dep(o, t.w, 'raw')
            for d in t.wl:
                self._dep(o, d, 'raw')
        for t in writes:
            self._dep(o, t.w, 'waw')
            for d in t.wl:
                self._dep(o, d, 'waw')
            for r in t.r.values():
                self._dep(o, r, 'war')
        for t in reads:
            t.r[id(o) if dma else eng] = o
        for t in writes:
            if dma:
                t.wl.append(o)
            else:
                t.w = o
                t.wl = []
            t.r = {}
        if dma:
            pool = self.pools[eng]
            slot = pool['slots'][pool['i']]
            pool['i'] = (pool['i'] + 1) % len(pool['slots'])
            if slot[2] is not None:
                o.deps.append(slot[2])
            slot[1] += 16
            slot[2] = o
            o.dsem = slot[0]
            o.dval = slot[1]
        self.ops[eng].append(o)
        return o

    def dma(self, queue, out, in_, reads=(), writes=(), **kw):
        return self.op(queue, lambda e: e.dma_start(out=out, in_=in_, **kw), reads, writes, dma=True)

    def finalize(self):
        for e in ENGS:
            for o in self.ops[e]:
                for d in o.deps:
                    if not d.dma:
                        d.sig = True
        for e in ENGS:
            c = 0
            for o in self.ops[e]:
                if o.sig and not o.dma:
                    c += 1
                o.cnt = c

    def emit_engine(self, eng, e):
        seen = {}
        for o in self.ops[eng]:
            waits = {}
            for d in o.deps:
                if d.dma:
                    sem, val = d.dsem, d.dval
                else:
                    sem, val = self.sems[d.eng], d.cnt
                k = id(sem)
                if k not in waits or waits[k][1] < val:
                    waits[k] = (sem, val)
            for k, (sem, val) in waits.items():
                if seen.get(k, 0) >= val:
                    continue
                e.wait_ge(sem, val)
                seen[k] = val
            ins = o.fn(e)
            if o.dma:
                ins.then_inc(o.dsem, 16)
            elif o.sig:
                ins.then_inc(self.sems[eng], 1)

    def final_waits(self, eng, e):
        for q, pool in self.pools.items():
            for sem, val, last in pool['slots']:
                if val > 0:
                    e.wait_ge(sem, val)


def AP(t, off, pat):
    return bass.AP(t.tensor, t.offset + off, pat)


def build(debug=None):
    nc = bass.Bass("TRN2", target_bir_lowering=False, dynamic_dma_scratch_size=4096)
    P = Prog()
    es = ExitStack()

    def din(name, shape):
        return nc.dram_tensor(name, list(shape), F32, kind="ExternalInput").ap()

    def dout(name, shape):
        return nc.dram_tensor(name, list(shape), F32, kind="ExternalOutput").ap()

    xp = din("xp", [SEQ, D])
    xs = din("xs", [TS, D])
    sh = din("sh", [DEPTH, NSEQ_S, 8, 64, 64])
    scv = din("scv", [DEPTH, NSEQ_S, 30, 256])
    cin = din("c", [1 + NSEQ_S, D])
    w_ada = din("w_ada", [DEPTH, D, 6 * D])
    b_ada = din("b_ada", [DEPTH, 6 * D])
    g_pre_mix = din("g_pre_mix", [DEPTH, D])
    g_post_mix = din("g_post_mix", [DEPTH, D])
    g_pre_mlp = din("g_pre_mlp", [DEPTH, D])
    g_post_mlp = din("g_post_mlp", [DEPTH, D])
    w_in = din("w_in", [DEPTH, D, NIN])
    a_ln_g = din("a_ln_g", [DEPTH, 256])
    a_ln_b = din("a_ln_b", [DEPTH, 256])
    a_w_s = din("a_w_s", [DEPTH, 4, 128, 128])
    a_b_s = din("a_b_s", [DEPTH, 4, 128])
    b_lb = din("b_lb", [DEPTH, 512])
    b_gn_g = din("b_gn_g", [DEPTH, 512])
    c_w_dw = din("c_w_dw", [DEPTH, 31, 256])
    c_b_dw = din("c_b_dw", [DEPTH, 256])
    c_ln_g = din("c_ln_g", [DEPTH, 256])
    c_ln_b = din("c_ln_b", [DEPTH, 256])
    w_out = din("w_out", [DEPTH, D, D])
    w_up = din("w_up", [DEPTH, D, DFF])
    w_down = din("w_down", [DEPTH, DFF, D])

    yp = dout("yp", [SEQ, D])
    ys = dout("ys", [TS, D])
    hp_o = dout("hp", [DEPTH, 8, 64, 64])
    hs_o = dout("hs", [DEPTH, NSEQ_S, 8, 64, 64])
    cp_o = dout("cp", [DEPTH, 30, 256])
    cs_o = dout("cs", [DEPTH, NSEQ_S, 30, 256])
    gv_o = dout("gv", [DEPTH, NSEQ_S, LS, 256])
    dbg = None
    if debug:
        dbg = dout("dbg", [128, 2048])

    def sb(name, shape, dt=F32):
        return es.enter_context(nc.sbuf_tensor(name, list(shape), dt))

    def sem(name):
        return es.enter_context(nc.semaphore(name))

    for e in ENGS:
        P.sems[e] = sem("s_" + e)
    P.set_dma_pool('sp', [sem("dsp%d" % i) for i in range(16)])
    P.set_dma_pool('pool', [sem("dpl%d" % i) for i in range(16)])
    P.set_dma_pool('act', [sem("dac%d" % i) for i in range(4)])

    ps = es.enter_context(nc.psum_tensor("ps", [128, 8, 512], F32))
    ps_tok = [Tok("ps%d" % i) for i in range(8)]
    ps_rr = [0]

    ps_held = set()

    def psum(hold=False):
        for _ in range(8):
            i = ps_rr[0]
            ps_rr[0] = (i + 1) % 7
            if i not in ps_held:
                break
        else:
            raise RuntimeError("no free PSUM bank")
        if hold:
            ps_held.add(i)
        return ps[:, i, :], ps_tok[i]

    def psum_release(tok):
        ps_held.discard(ps_tok.index(tok))


    W = [sb("W0", [128, 32768], BF16), sb("W1", [128, 32768], BF16)]
    Wt = [[Tok("W%d_%d" % (r, i)) for i in range(8)] for r in range(2)]

    def wtoks(r, c0, c1):
        return [Wt[r][i] for i in range(c0 // 4096, (c1 - 1) // 4096 + 1)]

    ident_f = sb("ident_f", [128, 128], F32)
    ident_b = sb("ident_b", [128, 128], BF16)
    ones_f = sb("ones_f", [128, 128], F32)
    bones_b = sb("bones_b", [128, 128], BF16)
    mask_p = sb("mask_p", [128, 128], F32)
    mask_s = sb("mask_s", [64, 64], F32)
    mres_p = sb("mres_p", [128, 128], F32)
    mres_s = sb("mres_s", [128, 64], F32)
    mhl = sb("mhl", [128, 2], F32)
    oneh = sb("oneh", [64, 16], F32)
    oneh16 = sb("oneh16", [128, 8], F32)
    mpar = sb("mpar", [128, 2], F32)
    t_const = Tok("const")
    C_ = lambda fn: P.op('pool', fn, reads=[t_const], writes=[t_const])
    C_(lambda e: e.memset(ident_f[:, :], 1.0))
    C_(lambda e: e.affine_select(ident_f[:, :], ident_f[:, :], [[-1, 128]], ALU.is_equal, 0.0, base=0, channel_multiplier=1))
    C_(lambda e: e.memset(ones_f[:, :], 1.0))
    C_(lambda e: e.memset(bones_b[:, :], 0.0))
    C_(lambda e: e.memset(bones_b[0:64, 0:64], 1.0))
    C_(lambda e: e.memset(bones_b[64:128, 64:128], 1.0))
    C_(lambda e: e.memset(mask_p[:, :], 1.0))
    C_(lambda e: e.affine_select(mask_p[:, :], mask_p[:, :], [[1, 128]], ALU.is_ge, 0.0, base=0, channel_multiplier=-1))
    C_(lambda e: e.affine_select(mask_p[:, :], mask_p[:, :], [[-16, 8], [0, 16]], ALU.is_ge, 0.0, base=0, channel_multiplier=1))
    C_(lambda e: e.memset(mask_s[:, :], 1.0))
    C_(lambda e: e.affine_select(mask_s[:, :], mask_s[:, :], [[1, 64]], ALU.is_ge, 0.0, base=0, channel_multiplier=-1))
    C_(lambda e: e.affine_select(mask_s[:, :], mask_s[:, :], [[-4, 16], [0, 4]], ALU.is_ge, 0.0, base=0, channel_multiplier=1))
    C_(lambda e: e.memset(mres_p[:, :], 1.0))
    C_(lambda e: e.memset(mres_p[:, :].rearrange("p (c i) -> p c i", i=16)[:, :, 0:1], 0.0))
    C_(lambda e: e.memset(mres_s[:, :], 1.0))
    C_(lambda e: e.memset(mres_s[:, :].rearrange("p (c i) -> p c i", i=4)[:, :, 0:1], 0.0))
    C_(lambda e: e.memset(mhl[:, :], 0.0))
    C_(lambda e: e.memset(mhl[0:64, 0:1], 1.0))
    C_(lambda e: e.memset(mhl[64:128, 1:2], 1.0))
    C_(lambda e: e.memset(oneh[:, :], 1.0))
    C_(lambda e: e.affine_select(oneh[:, :], oneh[:, :], [[-4, 16]], ALU.is_ge, 0.0, base=0, channel_multiplier=1))
    C_(lambda e: e.affine_select(oneh[:, :], oneh[:, :], [[4, 16]], ALU.is_ge, 0.0, base=3, channel_multiplier=-1))
    C_(lambda e: e.memset(oneh16[:, :], 1.0))
    C_(lambda e: e.affine_select(oneh16[:, :], oneh16[:, :], [[-16, 8]], ALU.is_ge, 0.0, base=0, channel_multiplier=1))
    C_(lambda e: e.affine_select(oneh16[:, :], oneh16[:, :], [[16, 8]], ALU.is_ge, 0.0, base=15, channel_multiplier=-1))
    P.op('dve', lambda e: e.tensor_reduce(mpar[:, 0:1], oneh16[:, 0:8:2], AX.X, ALU.add), reads=[t_const], writes=[t_const])
    P.op('dve', lambda e: e.tensor_reduce(mpar[:, 1:2], oneh16[:, 1:8:2], AX.X, ALU.add), reads=[t_const], writes=[t_const])
    P.op('dve', lambda e: e.tensor_copy(ident_b[:, :], ident_f[:, :]), reads=[t_const], writes=[t_const])

    t_par = Tok("params")

    def fm_load(name, src, nch):
        t = sb(name, [128, DEPTH, nch], F32)
        for l in range(DEPTH):
            P.dma('sp', t[:, l, :], src[l].rearrange("(k p) -> p k", p=128), writes=[t_par],
                  allow_slow_non_contiguous=True)
        return t

    gpm = fm_load("gpm", g_pre_mix, 8)
    gpo = fm_load("gpo", g_post_mix, 8)
    gpl = fm_load("gpl", g_pre_mlp, 8)
    gpol = fm_load("gpol", g_post_mlp, 8)
    gn = fm_load("gn", b_gn_g, 4)
    lbr = fm_load("lbr", b_lb, 4)
    cb = fm_load("cb", c_b_dw, 2)
    clg = fm_load("clg", c_ln_g, 2)
    clb = fm_load("clb", c_ln_b, 2)
    cw = sb("cw", [128, DEPTH, 2, 31], F32)
    for l in range(DEPTH):
        for cc in range(2):
            P.dma('sp', cw[:, l, cc, :], c_w_dw[l][:, cc * 128:(cc + 1) * 128].rearrange("j p -> p j"),
                  writes=[t_par], allow_slow_non_contiguous=True)
    lb1 = sb("lb1", [128, 4], F32)
    oml1 = sb("oml1", [128, 4], F32)
    P.op('dve', lambda e: e.tensor_tensor(lb1[:, :], lbr[:, 1, :], lbr[:, 0, :], ALU.subtract), reads=[t_par], writes=[t_par])
    P.op('act', lambda e: e.activation(lb1[:, :], lb1[:, :], AF.Sigmoid), reads=[t_par], writes=[t_par])
    P.op('dve', lambda e: e.tensor_scalar(oml1[:, :], lb1[:, :], -1.0, 1.0, ALU.mult, ALU.add), reads=[t_par], writes=[t_par])
    lng = sb("lng", [128, 256], F32)
    lnb = sb("lnb", [128, 256], F32)
    t_ln = Tok("ln")

    NS = 1 + NSEQ_S
    modT = sb("modT", [128, DEPTH, 48, NS], F32)
    t_mod = Tok("modT")
    t2 = sb("t2", [128, D], F32)
    xn = sb("xn", [128, D], BF16)
    c_sb = t2[0:NS, :]
    c_bf = xn[0:NS, :]
    cT = sb("cT", [128, 8, NS], BF16)
    badaT = sb("badaT", [128, DEPTH, 48], F32)
    t_c = Tok("c")
    t_cT = Tok("cT")
    P.dma('sp', c_sb, cin[:, :], writes=[t_c])
    for l in range(DEPTH):
        P.dma('sp', badaT[:, l, :], b_ada[l].rearrange("(j p) -> p j", p=128), writes=[t_par],
              allow_slow_non_contiguous=True)
    P.op('act', lambda e: e.activation(c_bf, c_sb, AF.Silu), reads=[t_c], writes=[t_c])
    pst, pstok = psum()
    pst_b = pst.bitcast(BF16)
    for k in range(8):
        P.op('pe', lambda e, k=k: e.transpose(pst_b[0:128, k * 32:k * 32 + NS], xn[0:NS, k * 128:(k + 1) * 128],
                                              ident_b[0:NS, 0:NS]),
             reads=[t_c, t_const], writes=[pstok])
    P.op('dve', lambda e: e.tensor_copy(cT[:, :, :], pst_b[:, 0:256].rearrange("p (k s) -> p k s", s=32)[:, :, 0:NS]),
         reads=[pstok], writes=[t_cT])

    def load_w_in_out(l, r):
        wv = w_in[l].rearrange("(k p) n -> p k n", p=128)
        for k in range(8):
            P.dma('pool', W[r][:, k * 3072:(k + 1) * 3072], wv[:, k, :], writes=wtoks(r, k * 3072, (k + 1) * 3072),
                  max_dma_last_dim=4096)
        wo = w_out[l].rearrange("(k p) n -> p k n", p=128)
        for k in range(8):
            c0 = 24576 + k * 1024
            P.dma('pool', W[r][:, c0:c0 + 1024], wo[:, k, :], writes=wtoks(r, c0, c0 + 1024))

    def fold_gn(l, r):
        for hp in range(4):
            c0 = 24576 + (2 + hp) * 1024
            P.op('dve', lambda e, c0=c0, hp=hp: e.tensor_scalar(W[r][:, c0:c0 + 1024], W[r][:, c0:c0 + 1024],
                                                                gn[:, l, hp:hp + 1], None, ALU.mult),
                 reads=wtoks(r, c0, c0 + 1024) + [t_par], writes=wtoks(r, c0, c0 + 1024))

    def load_w_up(l, r, ks=range(8)):
        wv = w_up[l].rearrange("(k p) n -> p k n", p=128)
        for k in ks:
            for hf in range(2):
                c0 = k * 4096 + hf * 2048
                P.dma('pool', W[r][:, c0:c0 + 2048], wv[:, k, hf * 2048:(hf + 1) * 2048], writes=[Wt[r][k]],
                      max_dma_last_dim=4096)

    def load_w_down(l, r, js=range(32)):
        wv = w_down[l].rearrange("(j p) n -> p j n", p=128)
        for j in js:
            P.dma('pool', W[r][:, j * 1024:(j + 1) * 1024], wv[:, j, :], writes=[Wt[r][j // 4]])

    def mod_pieces(l, pieces=range(12)):
        wv = w_ada[l].rearrange("(k p) n -> p k n", p=128)
        for piece in pieces:
            ri = piece % 3
            buf = W[1][:, ri * 4096:(ri + 1) * 4096].rearrange("p (k n) -> p k n", k=8)
            tk = Wt[1][ri]
            P.dma('pool', buf[:, :, :], wv[:, :, piece * 512:(piece + 1) * 512], writes=[tk])
            pt, ptk = psum()
            for jj in range(4):
                col = jj * NS
                for k in range(8):
                    P.op('pe', lambda e, pt=pt, buf=buf, jj=jj, k=k, col=col: e.matmul(
                        pt[:, col:col + NS], buf[:, k, jj * 128:(jj + 1) * 128], cT[:, k, :],
                        start=(k == 0), stop=(k == 7)), reads=[tk, t_cT], writes=[ptk])
            P.op('dve', lambda e, pt=pt, piece=piece: e.tensor_tensor(
                modT[:, l, piece * 4:(piece + 1) * 4, :],
                pt[:, 0:4 * NS].rearrange("p (j s) -> p j s", s=NS),
                badaT[:, l, piece * 4:(piece + 1) * 4].unsqueeze(2).to_broadcast([128, 4, NS]),
                ALU.add), reads=[ptk, t_par], writes=[t_mod])
            yield

    for _ in mod_pieces(0, range(4)):
        pass
    load_w_in_out(0, 0)

    def _bg_pieces():
        yield from mod_pieces(0, range(4, 12))
        yield from mod_pieces(1)
    mod1_gen = _bg_pieces()
    mod1_state = {'done': False}

    def mod1_bg():
        if mod1_state['done']:
            return
        try:
            next(mod1_gen)
            next(mod1_gen)
        except StopIteration:
            mod1_state['done'] = True
            load_w_up(0, 1, range(6))

    xscr = nc.dram_tensor("xscr", [SEQ + TS, D], F32).ap()
    t_x = [Tok("x%d" % i) for i in range(17)]

    cur = {'first': False, 'last': False}

    def x_src(l, ph, ti):
        if cur['first']:
            return xp[ti * 128:(ti + 1) * 128, :] if ti < 16 else xs[:, :]
        return xscr[ti * 128:(ti + 1) * 128, :] if ti < 16 else xscr[SEQ:SEQ + TS, :]

    def x_dst(l, ph, ti):
        if cur['last']:
            return yp[ti * 128:(ti + 1) * 128, :] if ti < 16 else ys[:, :]
        return xscr[ti * 128:(ti + 1) * 128, :] if ti < 16 else xscr[SEQ:SEQ + TS, :]

    xb = [sb("xb%d" % i, [128, 1, D], F32) for i in range(2)]
    t_xb = [Tok("xb0"), Tok("xb1")]
    t_xn = Tok("xn")
    t_t2 = Tok("t2")
    junk = t2[:, 0:512].bitcast(BF16)
    stat = sb("stat", [128, 16], F32)
    t_stat = Tok("stat")
    t_stat_glob = t_stat
    t_stat_glob2 = Tok("stat_warm")
    t_stat_pre = Tok("stat_pre")
    t_stat_lnv = Tok("stat_lnv")
    hT = [sb("hT%d" % i, [128, 8, 128], BF16) for i in range(2)]
    t_hT = [Tok("hT0"), Tok("hT1")]
    gsT = sb("gsT", [128, 8, NS], F32)
    shT = sb("shT", [128, 8, NS], F32)
    t_gs = Tok("gs")
    t_ggT = Tok("ggT")
    ggT = sb("ggT", [128, 8, NS], F32)
    gg = sb("gg", [128, D], F32)
    t_gg = Tok("gg")

    def setup_mod(l, ph, part=3):
        base = 3 * ph
        gpre = gpm if ph == 0 else gpl
        gpost = gpo if ph == 0 else gpol
        if part & 1:
            P.op('dve', lambda e: e.tensor_scalar(gsT[:, :, :], modT[:, l, (base + 1) * 8:(base + 2) * 8, :], 1.0, None, ALU.add),
                 reads=[t_mod], writes=[t_gs])
            P.op('dve', lambda e: e.tensor_tensor(gsT[:, :, :], gsT[:, :, :], gpre[:, l, :].unsqueeze(2).to_broadcast([128, 8, NS]), ALU.mult),
                 reads=[t_gs, t_par], writes=[t_gs])
            P.op('dve', lambda e: e.tensor_copy(shT[:, :, :], modT[:, l, base * 8:(base + 1) * 8, :]), reads=[t_mod], writes=[t_gs])
        if part & 2:
            P.op('dve', lambda e: e.tensor_tensor(ggT[:, :, :], modT[:, l, (base + 2) * 8:(base + 3) * 8, :],
                                                  gpost[:, l, :].unsqueeze(2).to_broadcast([128, 8, NS]), ALU.mult),
                 reads=[t_mod, t_par], writes=[t_ggT])

    def build_gate(sample):
        M = TS if sample else 128
        rep = t2[:, :].rearrange("p (k m) -> p k m", k=8)
        if sample:
            P.op('dve', lambda e: e.tensor_copy(
                rep[:, :, 0:TS].rearrange("p k (n t) -> p k n t", t=LS),
                ggT[:, :, 1:NS].unsqueeze(3).to_broadcast([128, 8, NSEQ_S, LS])), reads=[t_ggT], writes=[t_t2])
        else:
            P.op('dve', lambda e: e.tensor_copy(rep[:, :, :], ggT[:, :, 0:1].to_broadcast([128, 8, 128])),
                 reads=[t_ggT], writes=[t_t2])
        for half in range(2):
            pt, ptk = psum()
            for kk in range(4):
                k = half * 4 + kk
                P.op('pe', lambda e, pt=pt, k=k, kk=kk: e.transpose(pt[0:M, kk * 128:(kk + 1) * 128], rep[:, k, 0:M], ident_f[:, :]),
                     reads=[t_t2, t_const], writes=[ptk])
            P.op('act', lambda e, pt=pt, half=half: e.copy(gg[0:M, half * 512:(half + 1) * 512], pt[0:M, :]),
                 reads=[ptk], writes=[t_gg])

    def load_x(l, ph, ti, par, j):
        npart = TS if ti == 16 else 128
        P.dma('sp', xb[par][0:npart, j, :], x_src(l, ph, ti), reads=[t_x[ti]], writes=[t_xb[par]] + ([T_["xxT"]] if par == 2 else []))

    def rstd_from(ssum_ap, out_ap, n, np_, tk=None):
        t_stat = tk if tk is not None else t_stat_glob
        P.op('act', lambda e: e.activation(out_ap, ssum_ap, AF.Ln, bias=EPS, scale=1.0 / n), reads=[t_stat], writes=[t_stat])
        P.op('act', lambda e: e.activation(out_ap, out_ap, AF.Exp, scale=-0.5), reads=[t_stat], writes=[t_stat])

    def prenorm(par, j, hpar, col0, sample, part=3):
        np_ = TS if sample else 128
        xt = xb[par][0:np_, j, :]
        if part & 1:
            prenorm_a(par, j, sample)
        if part & 2:
            prenorm_b(hpar, col0, sample, evac_dve=True)

    def prenorm_a(par, j, sample):
        np_ = TS if sample else 128
        xt = xb[par][0:np_, j, :]
        P.op('act', lambda e: e.activation(xn[0:np_, :], xt, AF.Square, accum_out=stat[0:np_, 0:1]),
             reads=[t_xb[par]], writes=[t_xn, t_stat_pre])
        rstd_from(stat[0:np_, 0:1], stat[0:np_, 1:2], D, np_, t_stat_pre)
        P.op('dve', lambda e: e.tensor_scalar(xn[0:np_, :], xt, stat[0:np_, 1:2], None, ALU.mult),
             reads=[t_xb[par], t_stat_pre], writes=[t_xn])

    def prenorm_b(hpar, col0, sample, evac_dve=False):
        np_ = TS if sample else 128
        pt, ptk = psum()
        ptb = pt.bitcast(BF16)
        for k in range(8):
            P.op('pe', lambda e, k=k: e.transpose(ptb[:, k * 128:k * 128 + np_], xn[0:np_, k * 128:(k + 1) * 128],
                                                  ident_b[0:np_, 0:np_]), reads=[t_xn, t_const], writes=[ptk])
        if not sample and evac_dve:
            for k in range(8):
                P.op('dve', lambda e, k=k: e.tensor_scalar(hT[hpar][:, k, col0:col0 + 128], ptb[:, k * 128:(k + 1) * 128],
                                                           gsT[:, k, 0:1], shT[:, k, 0:1], ALU.mult, ALU.add),
                     reads=[ptk, t_gs], writes=[t_hT[hpar]])
        elif not sample:
            for k in range(8):
                P.op('act', lambda e, k=k: e.activation(hT[hpar][:, k, col0:col0 + 128], ptb[:, k * 128:(k + 1) * 128],
                                                        AF.Identity, bias=shT[:, k, 0:1], scale=gsT[:, k, 0:1]),
                     reads=[ptk, t_gs], writes=[t_hT[hpar]])
        else:
            hv = hT[hpar][:, :, col0:col0 + TS].rearrange("p k (n t) -> p k n t", t=LS)
            pv = ptb[:, :].rearrange("p (k m) -> p k m", k=8)[:, :, 0:TS].rearrange("p k (n t) -> p k n t", t=LS)
            tmpv = t2[:, 0:512].rearrange("p (k n t) -> p k n t", k=8, t=LS)
            P.op('dve', lambda e: e.tensor_tensor(tmpv, pv, gsT[:, :, 1:NS].unsqueeze(3).to_broadcast([128, 8, NSEQ_S, LS]), ALU.mult),
                 reads=[ptk, t_gs], writes=[t_t2])
            P.op('dve', lambda e: e.tensor_tensor(hv, tmpv, shT[:, :, 1:NS].unsqueeze(3).to_broadcast([128, 8, NSEQ_S, LS]), ALU.add),
                 reads=[t_t2, t_gs], writes=[t_hT[hpar]])

    def postnorm_store(l, ph, ti, par, j, ybanks, sample):
        np_ = TS if sample else 128
        for h2 in range(2):
            pt, ptk = ybanks[h2]
            P.op('act', lambda e, pt=pt, h2=h2: e.activation(junk[0:np_, h2 * 512:(h2 + 1) * 512], pt[0:np_, :], AF.Square,
                                                             accum_out=stat[0:np_, 4 + h2:5 + h2]),
                 reads=[ptk], writes=[t_t2, t_stat])
        P.op('dve', lambda e: e.tensor_tensor(stat[0:np_, 6:7], stat[0:np_, 4:5], stat[0:np_, 5:6], ALU.add), reads=[t_stat], writes=[t_stat])
        rstd_from(stat[0:np_, 6:7], stat[0:np_, 7:8], D, np_)
        for h2 in range(2):
            pt, ptk = ybanks[h2]
            P.op('dve', lambda e, pt=pt, h2=h2: e.tensor_tensor(t2[0:np_, h2 * 512:(h2 + 1) * 512], pt[0:np_, :],
                                                                gg[0:np_, h2 * 512:(h2 + 1) * 512], ALU.mult),
                 reads=[ptk, t_gg], writes=[t_t2])
        xt = xb[par][0:np_, j, :]
        P.op('dve', lambda e: e.scalar_tensor_tensor(xt, t2[0:np_, :], stat[0:np_, 7:8], xt, ALU.mult, ALU.add),
             reads=[t_t2, t_stat, t_xb[par]], writes=[t_xb[par]])
        P.dma('sp', x_dst(l, ph, ti), xt, reads=[t_xb[par]], writes=[t_x[ti]])

    arena = sb("arena", [128, 4160], F32)
    ffT = arena[:, 0:2048].bitcast(BF16).rearrange("p (j t) -> p j t", j=32)
    t_S0b = Tok("S0b")
    t_ff = t_S0b
    rbuf = [arena[:, 2048:2304].bitcast(BF16), arena[:, 2304:2560].bitcast(BF16)]
    t_rb = [Tok("rb0"), Tok("rb1")]

    def phase_mlp(l, ru, rd, after_last_up=None):
        setup_mod(l, 1)
        wup = W[ru][:, :].rearrange("p (k n) -> p k n", k=8)
        wdn = W[rd][:, :].rearrange("p (j n) -> p j n", j=32)
        blocks = [(ti, 1, ti == 16) for ti in range(17)]
        build_gate(False)

        def up(bi, jfs):
            ti0, nt, sample = blocks[bi]
            par = bi % 2
            T = TS if sample else 128
            for jg in jfs:
                pt, ptk = psum()
                for jj in range(4):
                    jf = jg * 4 + jj
                    for k in range(8):
                        P.op('pe', lambda e, pt=pt, k=k, jf=jf, jj=jj, T=T, par=par: e.matmul(
                            pt[:, jj * 128:jj * 128 + T], wup[:, k, jf * 128:(jf + 1) * 128], hT[par][:, k, 0:T], start=(k == 0), stop=(k == 7)),
                            reads=[Wt[ru][k], t_hT[par]], writes=[ptk])
                rp = jg % 2
                pv_ = pt[:, :].rearrange("p (j t) -> p j t", j=4)[:, :, 0:T]
                rv_ = rbuf[rp][:, :].rearrange("p (j t) -> p j t", j=4)[:, :, 0:T]
                P.op('act', lambda e, pv_=pv_, rv_=rv_: e.activation(rv_, pv_, AF.Relu), reads=[ptk], writes=[t_rb[rp], T_["S0f"]])
                P.op('dve', lambda e, jg=jg, T=T, rv_=rv_: e.tensor_tensor(ffT[:, jg * 4:(jg + 1) * 4, 0:T], rv_, rv_, ALU.mult),
                     reads=[t_rb[rp]], writes=[t_ff])

        nb = len(blocks)
        load_x(l, 1, 0, 0, 0)
        load_x(l, 1, 1, 1, 0)
        prenorm(0, 0, 0, 0, False)
        for bi, (ti0, nt, sample) in enumerate(blocks):
            par = bi % 2
            xpar = bi % 3
            if bi + 2 < nb:
                load_x(l, 1, bi + 2, (bi + 2) % 3, 0)
            if bi + 1 < nb:
                prenorm_a((bi + 1) % 3, 0, blocks[bi + 1][2])
            up(bi, range(0, 4))
            if bi + 1 < nb:
                prenorm_b((bi + 1) % 2, 0, blocks[bi + 1][2], evac_dve=True)
            up(bi, range(4, 8))
            if bi + 1 == nb and after_last_up is not None:
                after_last_up()
            if sample:
                build_gate(True)
            np_ = TS if sample else 128
            yb = [psum(), psum()]
            for h2 in range(2):
                pt, ptk = yb[h2]
                for jf in range(32):
                    P.op('pe', lambda e, pt=pt, jf=jf, h2=h2, np_=np_: e.matmul(
                        pt[0:np_, :], ffT[:, jf, 0:np_], wdn[:, jf, h2 * 512:(h2 + 1) * 512],
                        start=(jf == 0), stop=(jf == 31)), reads=[t_ff, Wt[rd][jf // 4]], writes=[ptk])
            postnorm_store(l, 1, ti0, xpar, 0, yb, sample)

    LN8 = float(np.log(8.0))
    sigc_ = sb("sigc", [128, 2, 128], F32)
    sigc = sigc_
    uT = sb("uT", [128, 2, 128], BF16)
    vg = sb("vg", [128, 256], F32)
    vln = sb("vln", [128, 256], BF16)
    sqT = sb("sqT", [128, 4, 128], BF16)
    fT = sb("fT", [128, 4, 128], F32)
    lfT = sb("lfT", [128, 4, 128], F32)
    bT = sb("bT", [128, 4, 128], F32)
    r1 = sb("r1", [128, 4, 128], F32)
    t1 = r1
    osq = sqT
    acc2 = sigc_
    gT = sb("gT", [128, 4, 16], F32)
    qinT = sb("qinT", [128, 4, 128], BF16)
    kinT = sb("kinT", [128, 4, 128], BF16)
    kdT = sb("kdT", [128, 4, 128], BF16)
    qinTz = sb("qinTz", [128, 4, 2, 128], BF16)
    sgT = sb("sgT", [128, 4, 128], BF16)
    v_tok = sb("v_tok", [128, 512], BF16)
    kd_tok = sb("kd_tok", [128, 512], BF16)
    PT = sb("PT", [128, 2, 4, 128], BF16)
    S = sb("S", [128, 4, 64], F32)
    Sb = sb("Sb", [128, 8, 4, 64], BF16)
    kdmp = sb("kdmp", [128, 2, 512], BF16)
    ycatT = sb("ycatT", [128, 8, 128], BF16)
    xgT = [sb("xgT%d" % i, [128, 2, 30 + 128], BF16) for i in range(2)]
    Sg = sb("Sg", [128, 4, 64], F32)
    acc = sb("acc", [128, 2, 128], F32)
    cst = sb("cst", [128, 3, 128], F32)
    wsT = sb("wsT", [128, 4, 128], BF16)
    wsTf = bT
    bs_row = arena[0:1, 2048:2560].rearrange("p (h t) -> p h t", h=4)
    wsTs = sb("wsTs", [64, 4, 64], BF16)
    corner = sb("corner", [4, 4, 4], F32)
    wsTsf = r1[0:64, :, 0:64]
    bs_row_s4 = arena[0:1, 2560:2816].rearrange("p (c a b) -> p c a b", c=2, a=2)
    xg_tok = acc[:, :, :].rearrange("p c t -> p (c t)")[0:64, :]
    sig_tok = cst[:, 0:2, :].rearrange("p c t -> p (c t)")[0:64, :]
    S0b = arena[:, 0:2048].bitcast(BF16).rearrange("p (n h v) -> p n h v", n=16, v=64)
    S0f = arena[:, 2048:3072].rearrange("p (n h v) -> p n h v", n=4, v=64)
    xxT = arena[:, 3072:4160].rearrange("p (c n j) -> p c n j", c=2, j=34)
    kdm = kdmp[0:64, :, :]
    scv_t = t2[0:120, :].rearrange("p (g c) -> p g c", g=4)
    tmpc = lfT[:, :, :].rearrange("p h t -> p (h t)")[:, 0:496].rearrange("p (n j) -> p n j", j=31)
    T_ = {n: Tok(n) for n in ["uT", "vg", "vln", "sqT", "fT", "lfT", "bT", "gT", "qinT", "kinT", "kdT", "qinTz", "sgT",
                              "v_tok", "kd_tok", "PT", "S", "Sb", "osq", "r1", "t1", "ycatT", "xg0", "xg1", "sigc", "acc",
                              "acc2", "cst", "ws", "wss", "xg_tok", "S0f", "S0b", "kdm", "xxT", "scv_t", "tmpc"]}
    T_["S0b"] = t_S0b
    T_["corner"] = Tok("corner")
    for _n in ("ycat_a", "ycat_b", "ycat_c"):
        T_[_n] = Tok(_n)
    T_["Sg"] = Tok("Sg")
    T_["S0"] = T_["S"]
    T_["S1"] = Tok("S2")
    T_["t1"] = T_["r1"]
    T_["osq"] = T_["sqT"]
    T_["acc2"] = T_["sigc"]
    T_["scv_t"] = t_t2
    T_["tmpc"] = T_["lfT"]

    def prep_layer_mixer(l, rw):
        Dg = W[1 - rw][:, 24576:32768].rearrange("p (i c) -> p i c", c=128)
        dgtk = [Wt[1 - rw][6], Wt[1 - rw][7]]
        P.op('dve', lambda e: e.tensor_tensor(Dg[:, 0:62, :], ident_b[:, :].unsqueeze(1).to_broadcast([128, 62, 128]),
                                              cw[:, l, :, :].rearrange("p c j -> p (c j)").unsqueeze(2).to_broadcast([128, 62, 128]), ALU.mult),
             reads=[t_const, t_par], writes=dgtk)

    def prep_layer_mixer_b(l, rw):
        P.dma('sp', lng[:, :], a_ln_g[l:l + 1, :].to_broadcast([128, 256]), writes=[t_ln])
        P.dma('sp', lnb[:, :], a_ln_b[l:l + 1, :].to_broadcast([128, 256]), writes=[t_ln])
        P.dma('sp', wsTf[:, :, :], a_w_s[l].rearrange("h t s -> t h s"), writes=[T_["ws"], T_["bT"]])
        pt, ptk = psum()
        for h in range(4):
            P.op('pe', lambda e, h=h: e.transpose(pt[:, h * 128:(h + 1) * 128], wsTf[:, h, :], ident_f[:, :]),
                 reads=[T_["ws"], T_["bT"], t_const], writes=[ptk])
        P.op('dve', lambda e: e.tensor_copy(wsTf[:, :, :].rearrange("s h t -> s (h t)"), pt[:, :]), reads=[ptk], writes=[T_["ws"], T_["bT"]])
        P.op('pool', lambda e: e.affine_select(wsT[:, :, :], wsTf[:, :, :], [[0, 4], [1, 128]], ALU.is_ge, 0.0, base=0,
                                               channel_multiplier=-1), reads=[T_["ws"], T_["bT"]], writes=[T_["ws"]])
        P.dma('sp', bs_row, a_b_s[l:l + 1, :, :], writes=[T_["ws"], T_["S0f"]])
        P.op('pool', lambda e: e.memset(wsTsf, 0.0), writes=[T_["wss"], T_["r1"]])
        P.op('dve', lambda e: e.tensor_copy(corner[:, :, :], wsTf[0:4, :, 0:4]), reads=[T_["ws"], T_["bT"]], writes=[T_["corner"]])
        for n in range(NSEQ_S):
            P.dma('sp', r1[4 * n:4 * n + 4, :, 4 * n:4 * n + 4], corner[:, :, :], reads=[T_["corner"]],
                  writes=[T_["wss"], T_["r1"]])

    def prep_layer_mixer_c(l, rw):
        P.op('dve', lambda e: e.tensor_tensor(wsTs[:, :, :], wsTsf, mask_s[:, :].unsqueeze(1).to_broadcast([64, 4, 64]), ALU.mult),
             reads=[T_["wss"], T_["r1"], t_const], writes=[T_["wss"]])


    _a = arena
    FS = [(uT, vg, vln, sqT, fT, sgT, v_tok),
          (_a[:, 0:128].bitcast(BF16).rearrange("p (c t) -> p c t", c=2),
           _a[:, 128:384],
           _a[:, 384:512].bitcast(BF16),
           _a[:, 512:768].bitcast(BF16).rearrange("p (c t) -> p c t", c=4),
           _a[:, 768:1280].rearrange("p (c t) -> p c t", c=4),
           _a[:, 1280:1536].bitcast(BF16).rearrange("p (c t) -> p c t", c=4),
           _a[:, 1536:1792].bitcast(BF16))]
    FTOK = [{k: T_[k] for k in ["uT", "vg", "vln", "sqT", "fT", "sgT", "v_tok"]},
            {k: Tok(k + "_1") for k in ["uT", "vg", "vln", "sqT", "fT", "sgT", "v_tok"]}]
    xb.append(arena[:, 3072:4096].rearrange("p (j d) -> p j d", j=1))
    t_xb.append(Tok("xb2"))

    def mixer_tile(l, rw, ti, sample, last_prompt):
        Dg = W[1 - rw][:, 24576:32768].rearrange("p (i c) -> p i c", c=128)
        dgtk = [Wt[1 - rw][6], Wt[1 - rw][7]]
        par = ti % 2
        xpar = ti % 3
        T = TS if sample else 128
        C = LS if sample else 16
        nch = T // C
        win = W[rw][:, 0:24576].rearrange("p (k n) -> p k n", k=8)
        wout = W[rw][:, 24576:32768].rearrange("p (k n) -> p k n", k=8)
        wtk = [Wt[rw][i] for i in range(6)]
        wotk = [Wt[rw][6], Wt[rw][7]]
        hpar = par
        h_ = hT[hpar]
        th = t_hT[hpar]
        mask = mask_s if sample else mask_p
        mres = mres_s if sample else mres_p
        xg = xgT[par]
        txg = T_["xg%d" % par]

        def fm_group(c0, nchunk):
            pt, ptk = psum()
            for jc in range(nchunk):
                for k in range(8):
                    P.op('pe', lambda e, pt=pt, jc=jc, k=k: e.matmul(pt[:, jc * 128:jc * 128 + T], win[:, k, c0 + jc * 128:c0 + (jc + 1) * 128],
                                                                     h_[:, k, 0:T], start=(k == 0), stop=(k == 7)),
                         reads=wtk + [th], writes=[ptk])
            return pt, ptk

        def tm_group(c0, ncol, t0=0, tn=None):
            tn = T if tn is None else tn
            pt, ptk = psum()
            for k in range(8):
                P.op('pe', lambda e, pt=pt, k=k: e.matmul(pt[0:tn, 0:ncol], h_[:, k, t0:t0 + tn], win[:, k, c0:c0 + ncol],
                                                          start=(k == 0), stop=(k == 7)), reads=wtk + [th], writes=[ptk])
            return pt, ptk

        def v3(pt, n):
            return pt[:, 0:n * 128].rearrange("p (c t) -> p c t", t=128)[:, :, 0:T]

        uT, vg, vln, sqT, fT, sgT, v_tok = FS[par]
        osq = sqT
        TT = dict(T_)
        TT.update(FTOK[par])
        TT["osq"] = TT["sqT"]
        prenorm(xpar, 0, par, 0, sample)
        yield
        pq, pqk = fm_group(512, 4)
        P.op('act', lambda e: e.activation(sqT[:, :, 0:T], v3(pq, 4), AF.Silu), reads=[pqk], writes=[TT["sqT"]])
        pg, pgk = fm_group(2048, 4)
        P.op('act', lambda e: e.activation(sgT[:, :, 0:T], v3(pg, 4), AF.Silu), reads=[pgk], writes=[TT["sgT"]])
        pu, puk = fm_group(0, 2)
        P.op('act', lambda e: e.activation(uT[:, :, 0:T], v3(pu, 2), AF.Gelu), reads=[puk], writes=[TT["uT"]])
        pv, pvk = tm_group(256, 256)
        P.op('act', lambda e: e.activation(vg[0:T, :], pv[0:T, 0:256], AF.Gelu), reads=[pvk], writes=[TT["vg"]])
        pf, pfk = fm_group(1024, 4)
        P.op('act', lambda e: e.activation(fT[:, :, 0:T], v3(pf, 4), AF.Sigmoid), reads=[pfk], writes=[TT["fT"]])
        pc, pck = fm_group(2560, 4)
        P.op('act', lambda e: e.activation(sigc[:, :, 0:T], v3(pc, 4)[:, 2:4, :], AF.Sigmoid), reads=[pck], writes=[TT["sigc"]])
        pi_, pik = tm_group(1536, 512)
        P.op('act', lambda e: e.copy(v_tok[0:T, :], pi_[0:T, :]), reads=[pik], writes=[TT["v_tok"]])
        if sample or last_prompt:
            t0, tn = (0, TS) if sample else (96, 32)
            pz, pzk = tm_group(2560, 512, t0, tn)
            P.op('act', lambda e: e.activation(sig_tok[0:tn, :], pz[0:tn, 256:512], AF.Sigmoid), reads=[pzk], writes=[TT["cst"]])
        P.op('act', lambda e: e.activation(stat[:, 2:3], ones_f[:, 0:1], AF.Ln), reads=[t_const], writes=[t_stat_glob2])
        yield
        if sample:
            xgdst = xxT[:, :, :, 30:34]
            P.op('dve', lambda e: e.tensor_tensor(xgdst, v3(pc, 4)[:, 0:2, :].rearrange("p c (n t) -> p c n t", t=LS),
                                                  sigc[:, :, 0:T].rearrange("p c (n t) -> p c n t", t=LS), ALU.mult),
                 reads=[pck, TT["sigc"]], writes=[TT["xxT"], t_xb[2]])
        else:
            P.op('dve', lambda e: e.tensor_tensor(xg[:, :, 30:30 + T], v3(pc, 4)[:, 0:2, :], sigc[:, :, 0:T], ALU.mult),
                 reads=[pck, TT["sigc"]], writes=[txg])
        if sample or last_prompt:
            P.op('dve', lambda e: e.tensor_tensor(xg_tok[0:tn, :], pz[0:tn, 0:256], sig_tok[0:tn, :], ALU.mult),
                 reads=[pzk, TT["cst"]], writes=[TT["acc"]])
            if sample:
                for t in range(LS):
                    P.dma('sp', cs_o[l, :, 26 + t, :], xg_tok[t:TS:LS, :], reads=[TT["acc"]])
                P.dma('sp', cs_o[l, :, 0:26, :], scv[l, :, 4:30, :])
            else:
                P.dma('sp', cp_o[l, :, :], xg_tok[2:32, :], reads=[TT["acc"]])
        if not sample:
            nx = xgT[1 - par]
            P.op('pool', lambda e: e.tensor_copy(nx[:, :, 0:30], xg[:, :, T:T + 30]), reads=[txg], writes=[TT["xg%d" % (1 - par)]])
            pcv, pcvk = ps[:, 7, :], ps_tok[7]
            for cc in range(2):
                for j in range(31):
                    P.op('pe', lambda e, cc=cc, j=j: e.matmul(pcv[:, cc * 128:cc * 128 + T], Dg[:, cc * 31 + j, :], xg[:, cc, j:j + T],
                                                              start=(j == 0), stop=(j == 30)), reads=[txg] + dgtk, writes=[pcvk])
        yield
        P.begin_chain()
        P.op('dve', lambda e: e.bn_stats(stat[0:T, 8:14], vg[0:T, :]), reads=[TT["vg"]], writes=[t_stat_lnv])
        P.op('dve', lambda e: e.bn_aggr(stat[0:T, 14:16], stat[0:T, 8:14]), reads=[t_stat_lnv], writes=[t_stat_lnv])
        P.op('act', lambda e: e.activation(stat[0:T, 15:16], stat[0:T, 15:16], AF.Ln, bias=EPS, scale=1.0), reads=[t_stat_lnv], writes=[t_stat_lnv])
        P.op('act', lambda e: e.activation(stat[0:T, 15:16], stat[0:T, 15:16], AF.Exp, scale=-0.5), reads=[t_stat_lnv], writes=[t_stat_lnv])
        P.op('dve', lambda e: e.tensor_scalar(vg[0:T, :], vg[0:T, :], stat[0:T, 14:15], stat[0:T, 15:16], ALU.subtract, ALU.mult),
             reads=[TT["vg"], t_stat_lnv], writes=[TT["vg"]])
        P.op('dve', lambda e: e.tensor_tensor(vg[0:T, :], vg[0:T, :], lng[0:T, :], ALU.mult), reads=[TT["vg"], t_ln], writes=[TT["vg"]])
        P.op('dve', lambda e: e.tensor_tensor(vg[0:T, :], vg[0:T, :], lnb[0:T, :], ALU.add), reads=[TT["vg"], t_ln], writes=[TT["vg"]])
        P.op('act', lambda e: e.copy(vln[0:T, :], vg[0:T, :]), reads=[TT["vg"]], writes=[TT["vln"]])
        if sample:
            P.dma('sp', gv_o[l].rearrange("n t c -> (n t) c"), vg[0:T, :], reads=[TT["vg"]])
        P.end_chain()
        P.begin_chain()
        if l == 1:
            for hp in range(4):
                P.op('dve', lambda e, hp=hp: e.tensor_scalar(fT[:, hp, 0:T], fT[:, hp, 0:T], oml1[:, hp:hp + 1], lb1[:, hp:hp + 1], ALU.mult, ALU.add),
                     reads=[TT["fT"], t_par], writes=[TT["fT"]])
        P.op('act', lambda e: e.activation(lfT[:, :, 0:T], fT[:, :, 0:T], AF.Ln), reads=[TT["fT"]], writes=[TT["lfT"]])
        P.op('dve', lambda e: e.tensor_scalar(fT[:, :, 0:T], fT[:, :, 0:T], -1.0, 1.0, ALU.mult, ALU.add), reads=[TT["fT"], TT["lfT"]], writes=[TT["fT"]])
        for hp in range(4):
            P.op('dve', lambda e, hp=hp: e.tensor_tensor_scan(bT[:, hp, 0:T], mres[:, 0:T], lfT[:, hp, 0:T], 0.0, ALU.mult, ALU.add),
                 reads=[TT["lfT"], t_const], writes=[TT["bT"]])
        bview = bT[:, :, 0:T].rearrange("p h (c i) -> p h c i", i=C)[:, :, :, C - 1]
        P.op('act', lambda e: e.activation(gT[:, :, 0:nch], bview, AF.Exp), reads=[TT["bT"]], writes=[TT["gT"]])
        P.op('act', lambda e: e.activation(lfT[:, :, 0:T], bT[:, :, 0:T], AF.Exp, bias=-LN8), reads=[TT["bT"], TT["lfT"]], writes=[TT["lfT"]])
        P.op('dve', lambda e: e.tensor_tensor(qinT[:, :, 0:T], sqT[:, :, 0:T], lfT[:, :, 0:T], ALU.mult), reads=[TT["sqT"], TT["lfT"]], writes=[TT["qinT"]])
        P.op('act', lambda e: e.activation(bT[:, :, 0:T], bT[:, :, 0:T], AF.Exp, scale=-1.0), reads=[TT["bT"], TT["gT"], TT["lfT"]], writes=[TT["bT"]])
        P.op('dve', lambda e: e.tensor_tensor(kinT[:, :, 0:T], fT[:, :, 0:T], bT[:, :, 0:T], ALU.mult), reads=[TT["fT"], TT["bT"]], writes=[TT["kinT"]])
        P.op('dve', lambda e: e.tensor_tensor(kdT[:, :, 0:T].rearrange("p h (c i) -> p h c i", i=C),
                                              kinT[:, :, 0:T].rearrange("p h (c i) -> p h c i", i=C),
                                              gT[:, :, 0:nch].unsqueeze(3).to_broadcast([128, 4, nch, C]), ALU.mult),
             reads=[TT["kinT"], TT["gT"]], writes=[TT["kdT"]])
        P.end_chain()
        P.begin_chain()
        if sample:
            for g4 in range(4):
                P.dma('sp', scv_t[:, g4, :], scv[l, 4 * g4:4 * g4 + 4].rearrange("n j c -> (n j) c"), writes=[TT["scv_t"]])
            for cc in range(2):
                pt, ptk = psum()
                for g4 in range(4):
                    P.op('pe', lambda e, pt=pt, g4=g4, cc=cc: e.transpose(pt[:, g4 * 120:(g4 + 1) * 120], scv_t[:, g4, cc * 128:(cc + 1) * 128], ident_f[0:120, 0:120]),
                         reads=[TT["scv_t"], t_const], writes=[ptk])
                P.op('act', lambda e, pt=pt, cc=cc: e.copy(xxT[:, cc, :, 0:30], pt[:, 0:480].rearrange("p (n j) -> p n j", j=30)),
                     reads=[ptk], writes=[TT["xxT"], t_xb[2]])
            for cc in range(2):
                for t in range(LS):
                    P.op('dve', lambda e, cc=cc, t=t: e.tensor_tensor(tmpc[:, :, :], xxT[:, cc, :, t:t + 31],
                                                                      cw[:, l, cc, :].unsqueeze(1).to_broadcast([128, 16, 31]), ALU.mult),
                         reads=[TT["xxT"], t_par], writes=[TT["tmpc"]])
                    P.op('dve', lambda e, cc=cc, t=t: e.tensor_reduce(acc[:, cc, t:TS:LS], tmpc[:, :, :], AX.X, ALU.add),
                         reads=[TT["tmpc"]], writes=[TT["acc"]])
                P.op('dve', lambda e, cc=cc: e.tensor_scalar(acc[:, cc, 0:T], acc[:, cc, 0:T], cb[:, l, cc:cc + 1], None, ALU.add),
                     reads=[TT["acc"], t_par], writes=[TT["acc"]])
        else:
            for cc in range(2):
                P.op('dve', lambda e, cc=cc: e.tensor_scalar(acc[:, cc, 0:T], pcv[:, cc * 128:cc * 128 + T], cb[:, l, cc:cc + 1], None, ALU.add),
                     reads=[pcvk, t_par], writes=[TT["acc"]])
        P.op('act', lambda e: e.activation(acc2[:, :, 0:T], acc[:, :, 0:T], AF.Square), reads=[TT["acc"]], writes=[TT["acc2"]])
        pl_, plk = psum()
        for cc in range(2):
            P.op('pe', lambda e, cc=cc: e.matmul(pl_[:, 0:T], ones_f[:, :], acc[:, cc, 0:T], start=(cc == 0), stop=(cc == 1)),
                 reads=[TT["acc"], t_const], writes=[plk])
        for cc in range(2):
            P.op('pe', lambda e, cc=cc: e.matmul(pl_[:, 128:128 + T], ones_f[:, :], acc2[:, cc, 0:T], start=(cc == 0), stop=(cc == 1)),
                 reads=[TT["acc2"], t_const], writes=[plk])
        P.op('dve', lambda e: e.tensor_scalar(cst[:, 0, 0:T], pl_[:, 0:T], 1.0 / 256, None, ALU.mult), reads=[plk], writes=[TT["cst"]])
        P.op('dve', lambda e: e.tensor_tensor(cst[:, 1, 0:T], cst[:, 0, 0:T], cst[:, 0, 0:T], ALU.mult), reads=[TT["cst"]], writes=[TT["cst"]])
        P.op('dve', lambda e: e.scalar_tensor_tensor(cst[:, 1, 0:T], pl_[:, 128:128 + T], 1.0 / 256, cst[:, 1, 0:T], ALU.mult, ALU.subtract),
             reads=[plk, TT["cst"]], writes=[TT["cst"]])
        P.op('act', lambda e: e.activation(cst[:, 1, 0:T], cst[:, 1, 0:T], AF.Ln, bias=EPS, scale=1.0), reads=[TT["cst"]], writes=[TT["cst"]])
        P.op('act', lambda e: e.activation(cst[:, 1, 0:T], cst[:, 1, 0:T], AF.Exp, scale=-0.5), reads=[TT["cst"]], writes=[TT["cst"]])
        P.op('dve', lambda e: e.tensor_tensor(acc[:, :, 0:T], acc[:, :, 0:T], cst[:, 0:1, 0:T].to_broadcast([128, 2, T]), ALU.subtract),
             reads=[TT["acc"], TT["cst"]], writes=[TT["acc"]])
        P.op('dve', lambda e: e.tensor_tensor(acc[:, :, 0:T], acc[:, :, 0:T], cst[:, 1:2, 0:T].to_broadcast([128, 2, T]), ALU.mult),
             reads=[TT["acc"], TT["cst"]], writes=[TT["acc"]])
        P.end_chain()
        yield
        for hl in range(2):
            P.op('dve', lambda e, hl=hl: e.tensor_scalar(qinTz[:, :, hl, 0:T], qinT[:, :, 0:T], mhl[:, hl:hl + 1], None, ALU.mult),
                 reads=[TT["qinT"], t_const], writes=[TT["qinTz"]])
        pm, pmk = psum()
        wsv = wsTs if sample else wsT
        twsv = TT["wss"] if sample else TT["ws"]
        tbsr = TT["S0f"]
        for h in range(4):
            c2, hl = h // 2, h % 2
            o_ap = pm[hl * 64:(hl + 1) * 64, c2 * 128:c2 * 128 + T]
            P.op('pe', lambda e, o_ap=o_ap, h=h: e.matmul(o_ap, vln[0:T, h * 64:(h + 1) * 64], wsv[0:T, h, 0:T], start=True, stop=False),
                 reads=[TT["vln"], twsv], writes=[pmk])
            P.op('pe', lambda e, o_ap=o_ap, h=h: e.matmul(o_ap, ones_f[0:1, 0:64], (bs_row_s4[0:1, h // 2, h % 2, 0:T] if sample else bs_row[0:1, h, 0:T]), start=False, stop=True),
                 reads=[twsv, tbsr, t_const], writes=[pmk])
        P.op('dve', lambda e: e.tensor_tensor(ycatT[:, 0:2, 0:T], uT[:, :, 0:T], v3(pm, 2), ALU.mult), reads=[pmk, TT["uT"]], writes=[TT["ycat_a"]])

        pk, pkk = psum()
        pkb = pk.bitcast(BF16)
        for hp in range(4):
            P.op('pe', lambda e, hp=hp: e.transpose(pkb[0:T, hp * 128:(hp + 1) * 128], kdT[:, hp, 0:T], ident_b[:, :]),
                 reads=[TT["kdT"], t_const], writes=[pkk])
        P.op('act', lambda e: e.copy(kd_tok[0:T, :], pkb[0:T, 0:512]), reads=[pkk], writes=[TT["kd_tok"]])
        sc = [psum(), psum()]
        for h in range(8):
            hp, hl = h // 2, h % 2
            pt, ptk = sc[hl]
            P.op('pe', lambda e, pt=pt, hp=hp, hl=hl: e.matmul(pt[0:T, hp * 128:hp * 128 + T], kinT[hl * 64:(hl + 1) * 64, hp, 0:T],
                                                               qinT[hl * 64:(hl + 1) * 64, hp, 0:T], start=True, stop=True),
                 reads=[TT["kinT"], TT["qinT"]], writes=[ptk])
        for hl in range(2):
            pt, ptk = sc[hl]
            P.op('dve', lambda e, pt=pt, hl=hl: e.tensor_tensor(PT[0:T, hl, :, 0:T], v3(pt, 4)[0:T], mask[0:T, 0:T].unsqueeze(1).to_broadcast([T, 4, T]), ALU.mult),
                 reads=[ptk, t_const], writes=[TT["PT"]])
        if not sample:
            for q in range(2):
                P.op('dve', lambda e, q=q: e.tensor_scalar(kdmp[:, q, :], kd_tok[:, :], mpar[:, q:q + 1], None, ALU.mult),
                     reads=[TT["kd_tok"], t_const], writes=[TT["kdm"]])
            ubanks = []
            for c in range(8):
                if c % 2 == 0:
                    ubanks.append(psum(hold=True))
                pu_, puk_ = ubanks[c // 2]
                q = c % 2
                b32 = 32 * (c // 2)
                for h in range(8):
                    hp, hl = h // 2, h % 2
                    P.op('pe', lambda e, pu_=pu_, hp=hp, hl=hl, h=h, q=q, b32=b32: e.matmul(
                        pu_[hl * 64:(hl + 1) * 64, q * 256 + hp * 64:q * 256 + (hp + 1) * 64], kdmp[b32:b32 + 32, q, h * 64:(h + 1) * 64],
                        v_tok[b32:b32 + 32, h * 64:(h + 1) * 64], start=True, stop=True, tile_position=(b32, 64 * hl)),
                        reads=[TT["kdm"], TT["v_tok"]], writes=[puk_])
        if sample:
            s0bufs = [S0f, xb[0][:, 0, :].rearrange("p (n h v) -> p n h v", n=4, v=64)]
            s0toks = [TT["S0f"], t_xb[0]]

            def s0_load(g4):
                for hl in range(2):
                    P.dma('sp', s0bufs[g4 % 2][hl * 64:(hl + 1) * 64, :, :, :],
                          sh[l, 4 * g4:4 * g4 + 4].rearrange("n (hp hl) k v -> hl k n hp v", hl=2)[hl], writes=[s0toks[g4 % 2]])
            s0_load(0)
            s0_load(1)
            for g4 in range(4):
                sbuf_, stok_ = s0bufs[g4 % 2], s0toks[g4 % 2]
                P.op('act', lambda e, g4=g4, sbuf_=sbuf_: e.copy(S0b[:, 4 * g4:4 * g4 + 4, :, :], sbuf_[:, :, :, :]), reads=[stok_],
                     writes=[TT["S0b"]] + list(FTOK[1].values()))
                for pr in range(2):
                    pu_, puk_ = psum()
                    for q in range(2):
                        n = 4 * g4 + 2 * pr + q
                        P.op('dve', lambda e, n=n, q=q: e.tensor_scalar(kdm[:, q, :], kd_tok[0:TS, :], oneh[:, n:n + 1], None, ALU.mult),
                             reads=[TT["kd_tok"], t_const], writes=[TT["kdm"]])
                        for h in range(8):
                            hp, hl = h // 2, h % 2
                            P.op('pe', lambda e, pu_=pu_, q=q, hp=hp, hl=hl, h=h: e.matmul(
                                pu_[hl * 64:(hl + 1) * 64, q * 256 + hp * 64:q * 256 + (hp + 1) * 64], kdm[:, q, h * 64:(h + 1) * 64],
                                v_tok[0:TS, h * 64:(h + 1) * 64], start=True, stop=True), reads=[TT["kdm"], TT["v_tok"]], writes=[puk_])
                    n0 = 2 * pr
                    sv = sbuf_[:, n0:n0 + 2, :, :]
                    gb = gT[:, :, 4 * g4 + n0:4 * g4 + n0 + 2].rearrange("p h n -> p n h").unsqueeze(3).to_broadcast([128, 2, 4, 64])
                    P.op('dve', lambda e, sv=sv, gb=gb: e.tensor_tensor(sv, sv, gb, ALU.mult), reads=[stok_, TT["gT"], TT["S0b"]], writes=[stok_])
                    P.op('dve', lambda e, sv=sv, pu_=pu_: e.tensor_tensor(sv, sv, pu_[:, 0:512].rearrange("p (n h v) -> p n h v", n=2, v=64), ALU.add),
                         reads=[stok_, puk_], writes=[stok_])
                for hl in range(2):
                    P.dma('sp', hs_o[l, 4 * g4:4 * g4 + 4].rearrange("n (hp hl) k v -> hl k n hp v", hl=2)[hl],
                          sbuf_[hl * 64:(hl + 1) * 64, :, :, :], reads=[stok_])
                if g4 + 2 < 4:
                    s0_load(g4 + 2)
        for cc in range(2):
            P.op('act', lambda e, cc=cc: e.activation(ycatT[:, 6 + cc, 0:T], acc[:, cc, 0:T], AF.Silu, bias=clb[:, l, cc:cc + 1], scale=clg[:, l, cc:cc + 1]),
                 reads=[TT["acc"], t_par], writes=[TT["ycat_c"]])

        yield
        if not sample:
            for c in range(8):
                pu_, puk_ = ubanks[c // 2]
                q = c % 2
                P.op('dve', lambda e, c=c: e.tensor_tensor(Sg[:, :, :], S[:, :, :], gT[:, :, c:c + 1].to_broadcast([128, 4, 64]), ALU.mult),
                     reads=[TT["S"], TT["gT"]], writes=[TT["Sg"]])
                P.op('dve', lambda e, c=c: e.tensor_copy(Sb[:, c, :, :], S[:, :, :]), reads=[TT["S"]], writes=[TT["Sb"]])
                P.op('dve', lambda e, pu_=pu_, q=q: e.tensor_tensor(S[:, :, :], Sg[:, :, :], pu_[:, q * 256:(q + 1) * 256].rearrange("p (h v) -> p h v", v=64), ALU.add),
                     reads=[TT["Sg"], puk_], writes=[TT["S"]])
                if c % 2 == 1:
                    psum_release(puk_)
        yield
        po, pok = psum()
        for h in range(8):
            hp, hl = h // 2, h % 2
            o_ap = po[hl * 64:(hl + 1) * 64, hp * 128:hp * 128 + T]
            P.op('pe', lambda e, o_ap=o_ap, h=h, hp=hp, hl=hl: e.matmul(o_ap, v_tok[0:T, h * 64:(h + 1) * 64], PT[0:T, hl, hp, 0:T], start=True, stop=False),
                 reads=[TT["v_tok"], TT["PT"]], writes=[pok])
            for c in range(nch):
                st_ap = S0b[:, c, hp, :] if sample else Sb[:, c, hp, :]
                P.op('pe', lambda e, st_ap=st_ap, hp=hp, hl=hl, c=c: e.matmul(
                    po[hl * 64:(hl + 1) * 64, hp * 128 + c * C:hp * 128 + (c + 1) * C], st_ap, qinTz[:, hp, hl, c * C:(c + 1) * C],
                    start=False, stop=(c == nch - 1)), reads=[TT["S0b"] if sample else TT["Sb"], TT["qinTz"]], writes=[pok])
        yield
        P.op('act', lambda e: e.activation(osq[:, :, 0:T], v3(po, 4), AF.Square), reads=[pok], writes=[TT["osq"]])
        pss, pssk = psum()
        if sample:
            for hp in range(4):
                P.op('pe', lambda e, hp=hp: e.matmul(pss[:, hp * 128:hp * 128 + T], bones_b[:, :], osq[:, hp, 0:T], start=True, stop=True),
                     reads=[TT["osq"], t_const], writes=[pssk])
        else:
            P.op('pe', lambda e: e.matmul(pss[:, 0:512], bones_b[:, :], osq[:, :, :].rearrange("p c t -> p (c t)"), start=True, stop=True),
                 reads=[TT["osq"], t_const], writes=[pssk])
        P.op('act', lambda e: e.activation(r1[:, :, 0:T], v3(pss, 4), AF.Ln, bias=EPS, scale=1.0 / 64), reads=[pssk], writes=[TT["r1"]])
        P.op('act', lambda e: e.activation(r1[:, :, 0:T], r1[:, :, 0:T], AF.Exp, scale=-0.5), reads=[TT["r1"]], writes=[TT["r1"]])
        P.op('dve', lambda e: e.tensor_tensor(t1[:, :, 0:T], v3(po, 4), r1[:, :, 0:T], ALU.mult), reads=[pok, TT["r1"]], writes=[TT["t1"]])
        P.op('dve', lambda e: e.tensor_tensor(ycatT[:, 2:6, 0:T], t1[:, :, 0:T], sgT[:, :, 0:T], ALU.mult), reads=[TT["t1"], TT["sgT"]], writes=[TT["ycat_b"]])
        if last_prompt:
            for hl in range(2):
                P.dma('sp', hp_o[l].rearrange("(hp hl) k v -> hl k hp v", hl=2)[hl], S[hl * 64:(hl + 1) * 64, :, :], reads=[TT["S"]])

        yb = [psum(), psum()]
        korder = [0, 1, 6, 7, 2, 3, 4, 5]
        ytok = {0: "ycat_a", 1: "ycat_a", 6: "ycat_c", 7: "ycat_c", 2: "ycat_b", 3: "ycat_b", 4: "ycat_b", 5: "ycat_b"}
        for kk in range(0, 8, 4):
            for h2 in range(2):
                pt, ptk = yb[h2]
                for k in korder[kk:kk + 4]:
                    P.op('pe', lambda e, pt=pt, k=k, h2=h2: e.matmul(pt[0:T, :], ycatT[:, k, 0:T], wout[:, k, h2 * 512:(h2 + 1) * 512],
                                                                      start=(k == korder[0]), stop=(k == korder[-1])),
                         reads=[TT[ytok[k]]] + wotk, writes=[ptk])
        postnorm_store(l, 0, ti, xpar, 0, yb, sample)
        yield


    def phase_mix(l, rw, hooks=None, bg=None):
        setup_mod(l, 0, part=1)
        gens = [mixer_tile(l, rw, ti, ti == 16, ti == 15) for ti in range(17)]
        load_x(l, 0, 0, 0, 0)
        load_x(l, 0, 1, 1, 0)
        P.op('pool', lambda e: e.memset(S[:, :, :], 0.0), writes=[T_["S"]])
        P.op('pool', lambda e: e.memset(xgT[0][:, :, 0:30], 0.0), writes=[T_["xg0"]])
        next(gens[0])
        prep_layer_mixer(l, rw)
        next(gens[0])
        next(gens[0])
        fold_gn(l, rw)
        prep_layer_mixer_b(l, rw)
        if bg is not None:
            bg()
        setup_mod(l, 0, part=2)
        build_gate(False)
        P.begin_chain()
        next(gens[1])
        P.end_chain()
        next(gens[0])
        P.merge_chains()
        for ti in range(17):
            nxt = ti + 1 < 17
            if bg is not None:
                bg()
            if ti + 2 < 17:
                load_x(l, 0, ti + 2, (ti + 2) % 3, 0)
            if ti == 16:
                build_gate(True)
                for h in range(4):
                    P.dma('sp', bs_row_s4[0:1, h // 2, h % 2, :].rearrange("p (n t) -> p n t", t=LS),
                          a_b_s[l:l + 1, h, 0:4].unsqueeze(1).to_broadcast([1, NSEQ_S, LS]), writes=[T_["S0f"]])
            if ti == 0:
                prep_layer_mixer_c(l, rw)
            next(gens[ti])
            if nxt:
                next(gens[ti + 1])
                if ti + 1 == 16 and hooks is not None:
                    hooks[1]()
            next(gens[ti])
            if nxt:
                next(gens[ti + 1])
                if ti + 1 == 15 and hooks is not None:
                    hooks[0]()
            next(gens[ti])
            P.begin_chain()
            next(gens[ti])
            P.end_chain()
            if ti + 2 < 17:
                P.begin_chain()
                next(gens[ti + 2])
                P.end_chain()
            if nxt:
                next(gens[ti + 1])
            P.merge_chains()

    if debug == 'mlp':
        while not mod1_state['done']:
            mod1_bg()
        load_w_up(0, 1, [6, 7])
        load_w_down(0, 0)
        cur['first'] = True
        cur['last'] = True
        phase_mlp(0, 1, 0)
    elif debug == 'mix':
        cur['first'] = True
        cur['last'] = True
        while not mod1_state['done']:
            mod1_bg()
        phase_mix(0, 0)
    else:
        cur['first'] = True
        phase_mix(0, 0, hooks=(lambda: load_w_up(0, 1, [6, 7]), lambda: load_w_down(0, 0, range(24))), bg=mod1_bg)
        while not mod1_state['done']:
            mod1_bg()
        cur['first'] = False
        load_w_down(0, 0, range(24, 32))
        phase_mlp(0, 1, 0, after_last_up=lambda: load_w_in_out(1, 1))
        load_w_up(1, 0, range(6))
        phase_mix(1, 1, hooks=(lambda: load_w_up(1, 0, [6, 7]), lambda: load_w_down(1, 1, range(24))))
        load_w_down(1, 1, range(24, 32))
        cur['last'] = True
        phase_mlp(1, 0, 1)

    P.finalize()
    with nc.Block() as block:
        @block.tensor
        def _(e):
            P.emit_engine('pe', e)

        @block.scalar
        def _(e):
            P.emit_engine('act', e)

        @block.vector
        def _(e):
            P.emit_engine('dve', e)

        @block.gpsimd
        def _(e):
            P.emit_engine('pool', e)

        @block.sync
        def _(e):
            P.emit_engine('sp', e)
            P.final_waits('sp', e)
    es.close()
    return nc


def make_in_maps(inputs):
    f = lambda a: np.ascontiguousarray(np.asarray(a, dtype=np.float32))
    maps = []
    for i in range(NCORES):
        m = {}
        m["xp"] = f(inputs["x_prompt"][i])
        m["xs"] = f(inputs["x_sample"][NSEQ_S * i:NSEQ_S * (i + 1)].reshape(TS, D))
        m["sh"] = f(inputs["state_hgrn"][:, NSEQ_S * i:NSEQ_S * (i + 1)])
        m["scv"] = f(inputs["state_conv"][:, NSEQ_S * i:NSEQ_S * (i + 1)])
        m["c"] = f(np.concatenate([inputs["c_prompt"][i:i + 1], inputs["c_sample"][NSEQ_S * i:NSEQ_S * (i + 1)]], 0))
        for k in ["w_ada", "b_ada", "g_pre_mix", "g_post_mix", "g_pre_mlp", "g_post_mlp", "w_in", "a_ln_g",
                  "a_ln_b", "a_w_s", "a_b_s", "b_lb", "b_gn_g", "c_w_dw", "c_b_dw", "c_ln_g", "c_ln_b",
                  "w_out", "w_up", "w_down"]:
            m[k] = f(inputs[k])
        maps.append(m)
    return maps


def kernel(**inputs):
    nc = build()
    maps = make_in_maps(inputs)
    res = run_bass_kernel_spmd(nc, maps, core_ids=list(range(NCORES)))
    r = res.results
    y_prompt = np.stack([r[i]["yp"] for i in range(NCORES)], 0)
    y_sample = np.concatenate([r[i]["ys"].reshape(NSEQ_S, LS, D) for i in range(NCORES)], 0)
    hgrn_prompt = np.stack([r[i]["hp"] for i in range(NCORES)], 1)
    hgrn_sample = np.concatenate([r[i]["hs"] for i in range(NCORES)], 1)
    conv_prompt = np.stack([r[i]["cp"] for i in range(NCORES)], 1)
    conv_sample = np.concatenate([r[i]["cs"] for i in range(NCORES)], 1)
    gmlp_v = np.concatenate([r[i]["gv"] for i in range(NCORES)], 1)
    return (y_prompt, y_sample, hgrn_prompt, hgrn_sample, conv_prompt, conv_sample, gmlp_v)
```

```python
import numpy as np
from contextlib import ExitStack
import concourse.bass as bass
import concourse.mybir as mybir
from concourse.bass_utils import run_bass_kernel_spmd

F32 = mybir.dt.float32
BF16 = mybir.dt.bfloat16
AF = mybir.ActivationFunctionType
ALU = mybir.AluOpType
AX = mybir.AxisListType

NCORES = 8
D = 1024
SEQ = 2048
NSEQ_S = 16
LS = 4
TS = NSEQ_S * LS
DEPTH = 2
NIN = 3072
DFF = 4096
EPS = 1e-6
ENGS = ['pe', 'act', 'dve', 'pool', 'sp']
STRICT_SAME_ENGINE = True


class Tok:
    __slots__ = ('name', 'w', 'wl', 'r')

    def __init__(self, name=''):
        self.name = name
        self.w = None
        self.wl = []
        self.r = {}


class Op:
    __slots__ = ('eng', 'fn', 'deps', 'idx', 'sig', 'cnt', 'dma', 'dsem', 'dval')


class Prog:
    def __init__(self):
        self.ops = {e: [] for e in ENGS}
        self.pools = {}
        self.sems = {}
        self.chains = []
        self.cur_chain = None

    def set_dma_pool(self, queue, sems):
        self.pools[queue] = {'slots': [[s, 0, None] for s in sems], 'i': 0}

    def _dep(self, op, d, kind):
        if d is op or d is None:
            return
        if kind == 'waw' and d.dma and op.dma:
            return
        if (not d.dma) and d.eng == op.eng:
            if op.eng == 'pe':
                return
            if kind != 'raw' and not op.dma and not STRICT_SAME_ENGINE:
                return
        op.deps.append(d)

    def begin_chain(self):
        self.cur_chain = []

    def end_chain(self):
        self.chains.append(self.cur_chain)
        self.cur_chain = None

    def merge_chains(self):
        chains, self.chains = self.chains, []
        while any(chains):
            for c in chains:
                if c:
                    a = c.pop(0)
                    self.op(*a[0], **a[1])

    def op(self, eng, fn, reads=(), writes=(), dma=False):
        if getattr(self, 'cur_chain', None) is not None:
            self.cur_chain.append(((eng, fn), dict(reads=list(reads), writes=list(writes), dma=dma)))
            return None
        o = Op()
        o.eng = eng
        o.fn = fn
        o.idx = len(self.ops[eng])
        o.deps = []
        o.sig = False
        o.dma = dma
        o.cnt = 0
        o.dsem = None
        o.dval = 0
        for t in reads:
            self._dep(o, t.w, 'raw')
            for d in t.wl:
                self._dep(o, d, 'raw')
        for t in writes:
            self._dep(o, t.w, 'waw')
            for d in t.wl:
                self._dep(o, d, 'waw')
            for r in t.r.values():
                self._dep(o, r, 'war')
        for t in reads:
            t.r[id(o) if dma else eng] = o
        for t in writes:
            if dma:
                t.wl.append(o)
            else:
                t.w = o
                t.wl = []
            t.r = {}
        if dma:
            pool = self.pools[eng]
            slot = pool['slots'][pool['i']]
            pool['i'] = (pool['i'] + 1) % len(pool['slots'])
            if slot[2] is not None:
                o.deps.append(slot[2])
            slot[1] += 16
            slot[2] = o
            o.dsem = slot[0]
            o.dval = slot[1]
        self.ops[eng].append(o)
        return o

    def dma(self, queue, out, in_, reads=(), writes=(), **kw):
        return self.op(queue, lambda e: e.dma_start(out=out, in_=in_, **kw), reads, writes, dma=True)

    def finalize(self):
        for e in ENGS:
            for o in self.ops[e]:
                for d in o.deps:
                    if not d.dma:
                        d.sig = True
        for e in ENGS:
            c = 0
            for o in self.ops[e]:
                if o.sig and not o.dma:
                    c += 1
                o.cnt = c

    def emit_engine(self, eng, e):
        seen = {}
        for o in self.ops[eng]:
            waits = {}
            for d in o.deps:
                if d.dma:
                    sem, val = d.dsem, d.dval
                else:
                    sem, val = self.sems[d.eng], d.cnt
                k = id(sem)
                if k not in waits or waits[k][1] < val:
                    waits[k] = (sem, val)
            for k, (sem, val) in waits.items():
                if seen.get(k, 0) >= val:
                    continue
                e.wait_ge(sem, val)
                seen[k] = val
            ins = o.fn(e)
            if o.dma:
                ins.then_inc(o.dsem, 16)
            elif o.sig:
                ins.then_inc(self.sems[eng], 1)

    def final_waits(self, eng, e):
        for q, pool in self.pools.items():
            for sem, val, last in pool['slots']:
                if val > 0:
                    e.wait_ge(sem, val)


def AP(t, off, pat):
    return bass.AP(t.tensor, t.offset + off, pat)


def build(debug=None):
    nc = bass.Bass("TRN2", target_bir_lowering=False, dynamic_dma_scratch_size=4096)
    P = Prog()
    es = ExitStack()

    def din(name, shape):
        return nc.dram_tensor(name, list(shape), F32, kind="ExternalInput").ap()

    def dout(name, shape):
        return nc.dram_tensor(name, list(shape), F32, kind="ExternalOutput").ap()

    xp = din("xp", [SEQ, D])
    xs = din("xs", [TS, D])
    sh = din("sh", [DEPTH, NSEQ_S, 8, 64, 64])
    scv = din("scv", [DEPTH, NSEQ_S, 30, 256])
    cin = din("c", [1 + NSEQ_S, D])
    w_ada = din("w_ada", [DEPTH, D, 6 * D])
    b_ada = din("b_ada", [DEPTH, 6 * D])
    g_pre_mix = din("g_pre_mix", [DEPTH, D])
    g_post_mix = din("g_post_mix", [DEPTH, D])
    g_pre_mlp = din("g_pre_mlp", [DEPTH, D])
    g_post_mlp = din("g_post_mlp", [DEPTH, D])
    w_in = din("w_in", [DEPTH, D, NIN])
    a_ln_g = din("a_ln_g", [DEPTH, 256])
    a_ln_b = din("a_ln_b", [DEPTH, 256])
    a_w_s = din("a_w_s", [DEPTH, 4, 128, 128])
    a_b_s = din("a_b_s", [DEPTH, 4, 128])
    b_lb = din("b_lb", [DEPTH, 512])
    b_gn_g = din("b_gn_g", [DEPTH, 512])
    c_w_dw = din("c_w_dw", [DEPTH, 31, 256])
    c_b_dw = din("c_b_dw", [DEPTH, 256])
    c_ln_g = din("c_ln_g", [DEPTH, 256])
    c_ln_b = din("c_ln_b", [DEPTH, 256])
    w_out = din("w_out", [DEPTH, D, D])
    w_up = din("w_up", [DEPTH, D, DFF])
    w_down = din("w_down", [DEPTH, DFF, D])

    yp = dout("yp", [SEQ, D])
    ys = dout("ys", [TS, D])
    hp_o = dout("hp", [DEPTH, 8, 64, 64])
    hs_o = dout("hs", [DEPTH, NSEQ_S, 8, 64, 64])
    cp_o = dout("cp", [DEPTH, 30, 256])
    cs_o = dout("cs", [DEPTH, NSEQ_S, 30, 256])
    gv_o = dout("gv", [DEPTH, NSEQ_S, LS, 256])
    dbg = None
    if debug:
        dbg = dout("dbg", [128, 2048])

    def sb(name, shape, dt=F32):
        return es.enter_context(nc.sbuf_tensor(name, list(shape), dt))

    def sem(name):
        return es.enter_context(nc.semaphore(name))

    for e in ENGS:
        P.sems[e] = sem("s_" + e)
    P.set_dma_pool('sp', [sem("dsp%d" % i) for i in range(16)])
    P.set_dma_pool('pool', [sem("dpl%d" % i) for i in range(16)])
    P.set_dma_pool('act', [sem("dac%d" % i) for i in range(4)])

    ps = es.enter_context(nc.psum_tensor("ps", [128, 8, 512], F32))
    ps_tok = [Tok("ps%d" % i) for i in range(8)]
    ps_rr = [0]

    ps_held = set()

    def psum(hold=False):
        for _ in range(8):
            i = ps_rr[0]
            ps_rr[0] = (i + 1) % 7
            if i not in ps_held:
                break
        else:
            raise RuntimeError("no free PSUM bank")
        if hold:
            ps_held.add(i)
        return ps[:, i, :], ps_tok[i]

    def psum_release(tok):
        ps_held.discard(ps_tok.index(tok))


    W = [sb("W0", [128, 32768], BF16), sb("W1", [128, 32768], BF16)]
    Wt = [[Tok("W%d_%d" % (r, i)) for i in range(8)] for r in range(2)]

    def wtoks(r, c0, c1):
        return [Wt[r][i] for i in range(c0 // 4096, (c1 - 1) // 4096 + 1)]

    ident_f = sb("ident_f", [128, 128], F32)
    ident_b = sb("ident_b", [128, 128], BF16)
    ones_f = sb("ones_f", [128, 128], F32)
    bones_b = sb("bones_b", [128, 128], BF16)
    mask_p = sb("mask_p", [128, 128], F32)
    mask_s = sb("mask_s", [64, 64], F32)
    mres_p = sb("mres_p", [128, 128], F32)
    mres_s = sb("mres_s", [128, 64], F32)
    mhl = sb("mhl", [128, 2], F32)
    oneh = sb("oneh", [64, 16], F32)
    oneh16 = sb("oneh16", [128, 8], F32)
    mpar = sb("mpar", [128, 2], F32)
    t_const = Tok("const")
    C_ = lambda fn: P.op('pool', fn, reads=[t_const], writes=[t_const])
    C_(lambda e: e.memset(ident_f[:, :], 1.0))
    C_(lambda e: e.affine_select(ident_f[:, :], ident_f[:, :], [[-1, 128]], ALU.is_equal, 0.0, base=0, channel_multiplier=1))
    C_(lambda e: e.memset(ones_f[:, :], 1.0))
    C_(lambda e: e.memset(bones_b[:, :], 0.0))
    C_(lambda e: e.memset(bones_b[0:64, 0:64], 1.0))
    C_(lambda e: e.memset(bones_b[64:128, 64:128], 1.0))
    C_(lambda e: e.memset(mask_p[:, :], 1.0))
    C_(lambda e: e.affine_select(mask_p[:, :], mask_p[:, :], [[1, 128]], ALU.is_ge, 0.0, base=0, channel_multiplier=-1))
    C_(lambda e: e.affine_select(mask_p[:, :], mask_p[:, :], [[-16, 8], [0, 16]], ALU.is_ge, 0.0, base=0, channel_multiplier=1))
    C_(lambda e: e.memset(mask_s[:, :], 1.0))
    C_(lambda e: e.affine_select(mask_s[:, :], mask_s[:, :], [[1, 64]], ALU.is_ge, 0.0, base=0, channel_multiplier=-1))
    C_(lambda e: e.affine_select(mask_s[:, :], mask_s[:, :], [[-4, 16], [0, 4]], ALU.is_ge, 0.0, base=0, channel_multiplier=1))
    C_(lambda e: e.memset(mres_p[:, :], 1.0))
    C_(lambda e: e.memset(mres_p[:, :].rearrange("p (c i) -> p c i", i=16)[:, :, 0:1], 0.0))
    C_(lambda e: e.memset(mres_s[:, :], 1.0))
    C_(lambda e: e.memset(mres_s[:, :].rearrange("p (c i) -> p c i", i=4)[:, :, 0:1], 0.0))
    C_(lambda e: e.memset(mhl[:, :], 0.0))
    C_(lambda e: e.memset(mhl[0:64, 0:1], 1.0))
    C_(lambda e: e.memset(mhl[64:128, 1:2], 1.0))
    C_(lambda e: e.memset(oneh[:, :], 1.0))
    C_(lambda e: e.affine_select(oneh[:, :], oneh[:, :], [[-4, 16]], ALU.is_ge, 0.0, base=0, channel_multiplier=1))
    C_(lambda e: e.affine_select(oneh[:, :], oneh[:, :], [[4, 16]], ALU.is_ge, 0.0, base=3, channel_multiplier=-1))
    C_(lambda e: e.memset(oneh16[:, :], 1.0))
    C_(lambda e: e.affine_select(oneh16[:, :], oneh16[:, :], [[-16, 8]], ALU.is_ge, 0.0, base=0, channel_multiplier=1))
    C_(lambda e: e.affine_select(oneh16[:, :], oneh16[:, :], [[16, 8]], ALU.is_ge, 0.0, base=15, channel_multiplier=-1))
    P.op('dve', lambda e: e.tensor_reduce(mpar[:, 0:1], oneh16[:, 0:8:2], AX.X, ALU.add), reads=[t_const], writes=[t_const])
    P.op('dve', lambda e: e.tensor_reduce(mpar[:, 1:2], oneh16[:, 1:8:2], AX.X, ALU.add), reads=[t_const], writes=[t_const])
    P.op('dve', lambda e: e.tensor_copy(ident_b[:, :], ident_f[:, :]), reads=[t_const], writes=[t_const])

    t_par = Tok("params")

    def fm_load(name, src, nch):
        t = sb(name, [128, DEPTH, nch], F32)
        for l in range(DEPTH):
            P.dma('sp', t[:, l, :], src[l].rearrange("(k p) -> p k", p=128), writes=[t_par],
                  allow_slow_non_contiguous=True)
        return t

    gpm = fm_load("gpm", g_pre_mix, 8)
    gpo = fm_load("gpo", g_post_mix, 8)
    gpl = fm_load("gpl", g_pre_mlp, 8)
    gpol = fm_load("gpol", g_post_mlp, 8)
    gn = fm_load("gn", b_gn_g, 4)
    lbr = fm_load("lbr", b_lb, 4)
    cb = fm_load("cb", c_b_dw, 2)
    clg = fm_load("clg", c_ln_g, 2)
    clb = fm_load("clb", c_ln_b, 2)
    cw = sb("cw", [128, DEPTH, 2, 31], F32)
    for l in range(DEPTH):
        for cc in range(2):
            P.dma('sp', cw[:, l, cc, :], c_w_dw[l][:, cc * 128:(cc + 1) * 128].rearrange("j p -> p j"),
                  writes=[t_par], allow_slow_non_contiguous=True)
    lb1 = sb("lb1", [128, 4], F32)
    oml1 = sb("oml1", [128, 4], F32)
    P.op('dve', lambda e: e.tensor_tensor(lb1[:, :], lbr[:, 1, :], lbr[:, 0, :], ALU.subtract), reads=[t_par], writes=[t_par])
    P.op('act', lambda e: e.activation(lb1[:, :], lb1[:, :], AF.Sigmoid), reads=[t_par], writes=[t_par])
    P.op('dve', lambda e: e.tensor_scalar(oml1[:, :], lb1[:, :], -1.0, 1.0, ALU.mult, ALU.add), reads=[t_par], writes=[t_par])
    lng = sb("lng", [128, 256], F32)
    lnb = sb("lnb", [128, 256], F32)
    t_ln = Tok("ln")

    NS = 1 + NSEQ_S
    modT = sb("modT", [128, DEPTH, 48, NS], F32)
    t_mod = Tok("modT")
    t2 = sb("t2", [128, D], F32)
    xn = sb("xn", [128, D], BF16)
    c_sb = t2[0:NS, :]
    c_bf = xn[0:NS, :]
    cT = sb("cT", [128, 8, NS], BF16)
    badaT = sb("badaT", [128, DEPTH, 48], F32)
    t_c = Tok("c")
    t_xn = Tok("xn")
    t_t2 = Tok("t2")
    t_cT = Tok("cT")
    P.dma('sp', c_sb, cin[:, :], writes=[t_c, t_t2])
    for l in range(DEPTH):
        P.dma('sp', badaT[:, l, :], b_ada[l].rearrange("(j p) -> p j", p=128), writes=[t_par],
              allow_slow_non_contiguous=True)
    P.op('act', lambda e: e.activation(c_bf, c_sb, AF.Silu), reads=[t_c, t_t2], writes=[t_c, t_xn])
    pst, pstok = psum()
    pst_b = pst.bitcast(BF16)
    for k in range(8):
        P.op('pe', lambda e, k=k: e.transpose(pst_b[0:128, k * 32:k * 32 + NS], xn[0:NS, k * 128:(k + 1) * 128],
                                              ident_b[0:NS, 0:NS]),
             reads=[t_c, t_xn, t_const], writes=[pstok])
    P.op('dve', lambda e: e.tensor_copy(cT[:, :, :], pst_b[:, 0:256].rearrange("p (k s) -> p k s", s=32)[:, :, 0:NS]),
         reads=[pstok], writes=[t_cT])

    def load_w_in_out(l, r):
        wv = w_in[l].rearrange("(k p) n -> p k n", p=128)
        for k in range(8):
            P.dma('pool', W[r][:, k * 3072:(k + 1) * 3072], wv[:, k, :], writes=wtoks(r, k * 3072, (k + 1) * 3072),
                  max_dma_last_dim=4096)
        wo = w_out[l].rearrange("(k p) n -> p k n", p=128)
        for k in range(8):
            c0 = 24576 + k * 1024
            P.dma('pool', W[r][:, c0:c0 + 1024], wo[:, k, :], writes=wtoks(r, c0, c0 + 1024))

    def fold_gn(l, r):
        for hp in range(4):
            c0 = 24576 + (2 + hp) * 1024
            P.op('dve', lambda e, c0=c0, hp=hp: e.tensor_scalar(W[r][:, c0:c0 + 1024], W[r][:, c0:c0 + 1024],
                                                                gn[:, l, hp:hp + 1], None, ALU.mult),
                 reads=wtoks(r, c0, c0 + 1024) + [t_par], writes=wtoks(r, c0, c0 + 1024))

    def load_w_up(l, r, ks=range(8)):
        wv = w_up[l].rearrange("(k p) n -> p k n", p=128)
        for k in ks:
            for hf in range(2):
                c0 = k * 4096 + hf * 2048
                P.dma('pool', W[r][:, c0:c0 + 2048], wv[:, k, hf * 2048:(hf + 1) * 2048], writes=[Wt[r][k]],
                      max_dma_last_dim=4096)

    def load_w_down(l, r, js=range(32)):
        wv = w_down[l].rearrange("(j p) n -> p j n", p=128)
        for j in js:
            P.dma('pool', W[r][:, j * 1024:(j + 1) * 1024], wv[:, j, :], writes=[Wt[r][j // 4]])

    def mod_pieces(l, pieces=range(12)):
        wv = w_ada[l].rearrange("(k p) n -> p k n", p=128)
        for piece in pieces:
            ri = piece % 3
            buf = W[1][:, ri * 4096:(ri + 1) * 4096].rearrange("p (k n) -> p k n", k=8)
            tk = Wt[1][ri]
            P.dma('pool', buf[:, :, :], wv[:, :, piece * 512:(piece + 1) * 512], writes=[tk])
            pt, ptk = psum()
            for jj in range(4):
                col = jj * NS
                for k in range(8):
                    P.op('pe', lambda e, pt=pt, buf=buf, jj=jj, k=k, col=col: e.matmul(
                        pt[:, col:col + NS], buf[:, k, jj * 128:(jj + 1) * 128], cT[:, k, :],
                        start=(k == 0), stop=(k == 7)), reads=[tk, t_cT], writes=[ptk])
            P.op('dve', lambda e, pt=pt, piece=piece: e.tensor_tensor(
                modT[:, l, piece * 4:(piece + 1) * 4, :],
                pt[:, 0:4 * NS].rearrange("p (j s) -> p j s", s=NS),
                badaT[:, l, piece * 4:(piece + 1) * 4].unsqueeze(2).to_broadcast([128, 4, NS]),
                ALU.add), reads=[ptk, t_par], writes=[t_mod])
            yield

    for _ in mod_pieces(0, range(4)):
        pass
    load_w_in_out(0, 0)

    def _bg_pieces():
        yield from mod_pieces(0, range(4, 12))
        yield from mod_pieces(1)
    mod1_gen = _bg_pieces()
    mod1_state = {'done': False}

    def mod1_bg():
        if mod1_state['done']:
            return
        try:
            next(mod1_gen)
            next(mod1_gen)
        except StopIteration:
            mod1_state['done'] = True
            load_w_up(0, 1, range(6))

    xscr = nc.dram_tensor("xscr", [SEQ + TS, D], F32).ap()
    t_x = [Tok("x%d" % i) for i in range(17)]

    cur = {'first': False, 'last': False}

    def x_src(l, ph, ti):
        if cur['first']:
            return xp[ti * 128:(ti + 1) * 128, :] if ti < 16 else xs[:, :]
        return xscr[ti * 128:(ti + 1) * 128, :] if ti < 16 else xscr[SEQ:SEQ + TS, :]

    def x_dst(l, ph, ti):
        if cur['last']:
            return yp[ti * 128:(ti + 1) * 128, :] if ti < 16 else ys[:, :]
        return xscr[ti * 128:(ti + 1) * 128, :] if ti < 16 else xscr[SEQ:SEQ + TS, :]

    xb = [sb("xb%d" % i, [128, 1, D], F32) for i in range(2)]
    t_xb = [Tok("xb0"), Tok("xb1")]
    junk = t2[:, 0:512].bitcast(BF16)
    stat = sb("stat", [128, 16], F32)
    t_stat = Tok("stat")
    t_stat_glob = t_stat
    t_stat_glob2 = Tok("stat_warm")
    t_stat_pre = Tok("stat_pre")
    t_stat_lnv = Tok("stat_lnv")
    hT = [sb("hT%d" % i, [128, 8, 128], BF16) for i in range(2)]
    t_hT = [Tok("hT0"), Tok("hT1")]
    gsT = sb("gsT", [128, 8, NS], F32)
    shT = sb("shT", [128, 8, NS], F32)
    t_gs = Tok("gs")
    t_ggT = Tok("ggT")
    ggT = sb("ggT", [128, 8, NS], F32)
    gg = sb("gg", [128, D], F32)
    t_gg = Tok("gg")

    def setup_mod(l, ph, part=3):
        base = 3 * ph
        gpre = gpm if ph == 0 else gpl
        gpost = gpo if ph == 0 else gpol
        if part & 1:
            P.op('dve', lambda e: e.tensor_scalar(gsT[:, :, :], modT[:, l, (base + 1) * 8:(base + 2) * 8, :], 1.0, None, ALU.add),
                 reads=[t_mod], writes=[t_gs])
            P.op('dve', lambda e: e.tensor_tensor(gsT[:, :, :], gsT[:, :, :], gpre[:, l, :].unsqueeze(2).to_broadcast([128, 8, NS]), ALU.mult),
                 reads=[t_gs, t_par], writes=[t_gs])
            P.op('dve', lambda e: e.tensor_copy(shT[:, :, :], modT[:, l, base * 8:(base + 1) * 8, :]), reads=[t_mod], writes=[t_gs])
        if part & 2:
            P.op('dve', lambda e: e.tensor_tensor(ggT[:, :, :], modT[:, l, (base + 2) * 8:(base + 3) * 8, :],
                                                  gpost[:, l, :].unsqueeze(2).to_broadcast([128, 8, NS]), ALU.mult),
                 reads=[t_mod, t_par], writes=[t_ggT])

    def build_gate(sample):
        M = TS if sample else 128
        rep = t2[:, :].rearrange("p (k m) -> p k m", k=8)
        if sample:
            P.op('dve', lambda e: e.tensor_copy(
                rep[:, :, 0:TS].rearrange("p k (n t) -> p k n t", t=LS),
                ggT[:, :, 1:NS].unsqueeze(3).to_broadcast([128, 8, NSEQ_S, LS])), reads=[t_ggT], writes=[t_t2])
        else:
            P.op('dve', lambda e: e.tensor_copy(rep[:, :, :], ggT[:, :, 0:1].to_broadcast([128, 8, 128])),
                 reads=[t_ggT], writes=[t_t2])
        for half in range(2):
            pt, ptk = psum()
            for kk in range(4):
                k = half * 4 + kk
                P.op('pe', lambda e, pt=pt, k=k, kk=kk: e.transpose(pt[0:M, kk * 128:(kk + 1) * 128], rep[:, k, 0:M], ident_f[:, :]),
                     reads=[t_t2, t_const], writes=[ptk])
            P.op('act', lambda e, pt=pt, half=half: e.copy(gg[0:M, half * 512:(half + 1) * 512], pt[0:M, :]),
                 reads=[ptk], writes=[t_gg])

    def load_x(l, ph, ti, par, j):
        npart = TS if ti == 16 else 128
        P.dma('sp', xb[par][0:npart, j, :], x_src(l, ph, ti), reads=[t_x[ti]], writes=[t_xb[par]] + ([T_["xxT"]] if par == 2 else []))

    def rstd_from(ssum_ap, out_ap, n, np_, tk=None):
        t_stat = tk if tk is not None else t_stat_glob
        P.op('act', lambda e: e.activation(out_ap, ssum_ap, AF.Ln, bias=EPS, scale=1.0 / n), reads=[t_stat], writes=[t_stat])
        P.op('act', lambda e: e.activation(out_ap, out_ap, AF.Exp, scale=-0.5), reads=[t_stat], writes=[t_stat])

    def prenorm(par, j, hpar, col0, sample, part=3):
        np_ = TS if sample else 128
        xt = xb[par][0:np_, j, :]
        if part & 1:
            prenorm_a(par, j, sample)
        if part & 2:
            prenorm_b(hpar, col0, sample, evac_dve=True)

    def prenorm_a(par, j, sample):
        np_ = TS if sample else 128
        xt = xb[par][0:np_, j, :]
        P.op('act', lambda e: e.activation(xn[0:np_, :], xt, AF.Square, accum_out=stat[0:np_, 0:1]),
             reads=[t_xb[par]], writes=[t_xn, t_stat_pre])
        rstd_from(stat[0:np_, 0:1], stat[0:np_, 1:2], D, np_, t_stat_pre)
        P.op('dve', lambda e: e.tensor_scalar(xn[0:np_, :], xt, stat[0:np_, 1:2], None, ALU.mult),
             reads=[t_xb[par], t_stat_pre], writes=[t_xn])

    def prenorm_b(hpar, col0, sample, evac_dve=False):
        np_ = TS if sample else 128
        pt, ptk = psum()
        ptb = pt.bitcast(BF16)
        for k in range(8):
            P.op('pe', lambda e, k=k: e.transpose(ptb[:, k * 128:k * 128 + np_], xn[0:np_, k * 128:(k + 1) * 128],
                                                  ident_b[0:np_, 0:np_]), reads=[t_xn, t_const], writes=[ptk])
        if not sample and evac_dve:
            for k in range(8):
                P.op('dve', lambda e, k=k: e.tensor_scalar(hT[hpar][:, k, col0:col0 + 128], ptb[:, k * 128:(k + 1) * 128],
                                                           gsT[:, k, 0:1], shT[:, k, 0:1], ALU.mult, ALU.add),
                     reads=[ptk, t_gs], writes=[t_hT[hpar]])
        elif not sample:
            for k in range(8):
                P.op('act', lambda e, k=k: e.activation(hT[hpar][:, k, col0:col0 + 128], ptb[:, k * 128:(k + 1) * 128],
                                                        AF.Identity, bias=shT[:, k, 0:1], scale=gsT[:, k, 0:1]),
                     reads=[ptk, t_gs], writes=[t_hT[hpar]])
        else:
            hv = hT[hpar][:, :, col0:col0 + TS].rearrange("p k (n t) -> p k n t", t=LS)
            pv = ptb[:, :].rearrange("p (k m) -> p k m", k=8)[:, :, 0:TS].rearrange("p k (n t) -> p k n t", t=LS)
            tmpv = t2[:, 0:512].rearrange("p (k n t) -> p k n t", k=8, t=LS)
            P.op('dve', lambda e: e.tensor_tensor(tmpv, pv, gsT[:, :, 1:NS].unsqueeze(3).to_broadcast([128, 8, NSEQ_S, LS]), ALU.mult),
                 reads=[ptk, t_gs], writes=[t_t2])
            P.op('dve', lambda e: e.tensor_tensor(hv, tmpv, shT[:, :, 1:NS].unsqueeze(3).to_broadcast([128, 8, NSEQ_S, LS]), ALU.add),
                 reads=[t_t2, t_gs], writes=[t_hT[hpar]])

    def postnorm_store(l, ph, ti, par, j, ybanks, sample):
        np_ = TS if sample else 128
        for h2 in range(2):
            pt, ptk = ybanks[h2]
            P.op('act', lambda e, pt=pt, h2=h2: e.activation(junk[0:np_, h2 * 512:(h2 + 1) * 512], pt[0:np_, :], AF.Square,
                                                             accum_out=stat[0:np_, 4 + h2:5 + h2]),
                 reads=[ptk], writes=[t_t2, t_stat])
        P.op('dve', lambda e: e.tensor_tensor(stat[0:np_, 6:7], stat[0:np_, 4:5], stat[0:np_, 5:6], ALU.add), reads=[t_stat], writes=[t_stat])
        rstd_from(stat[0:np_, 6:7], stat[0:np_, 7:8], D, np_)
        for h2 in range(2):
            pt, ptk = ybanks[h2]
            P.op('dve', lambda e, pt=pt, h2=h2: e.tensor_tensor(t2[0:np_, h2 * 512:(h2 + 1) * 512], pt[0:np_, :],
                                                                gg[0:np_, h2 * 512:(h2 + 1) * 512], ALU.mult),
                 reads=[ptk, t_gg], writes=[t_t2])
        xt = xb[par][0:np_, j, :]
        P.op('dve', lambda e: e.scalar_tensor_tensor(xt, t2[0:np_, :], stat[0:np_, 7:8], xt, ALU.mult, ALU.add),
             reads=[t_t2, t_stat, t_xb[par]], writes=[t_xb[par]])
        P.dma('sp', x_dst(l, ph, ti), xt, reads=[t_xb[par]], writes=[t_x[ti]])

    arena = sb("arena", [128, 4160], F32)
    ffT = arena[:, 0:2048].bitcast(BF16).rearrange("p (j t) -> p j t", j=32)
    t_S0b = Tok("S0b")
    t_ff = t_S0b
    rbuf = [arena[:, 2048:2304].bitcast(BF16), arena[:, 2304:2560].bitcast(BF16)]
    t_rb = [Tok("rb0"), Tok("rb1")]

    def phase_mlp(l, ru, rd, after_last_up=None):
        setup_mod(l, 1)
        wup = W[ru][:, :].rearrange("p (k n) -> p k n", k=8)
        wdn = W[rd][:, :].rearrange("p (j n) -> p j n", j=32)
        blocks = [(ti, 1, ti == 16) for ti in range(17)]
        build_gate(False)

        def up(bi, jfs):
            ti0, nt, sample = blocks[bi]
            par = bi % 2
            T = TS if sample else 128
            for jg in jfs:
                pt, ptk = psum()
                for jj in range(4):
                    jf = jg * 4 + jj
                    for k in range(8):
                        P.op('pe', lambda e, pt=pt, k=k, jf=jf, jj=jj, T=T, par=par: e.matmul(
                            pt[:, jj * 128:jj * 128 + T], wup[:, k, jf * 128:(jf + 1) * 128], hT[par][:, k, 0:T], start=(k == 0), stop=(k == 7)),
                            reads=[Wt[ru][k], t_hT[par]], writes=[ptk])
                rp = jg % 2
                pv_ = pt[:, :].rearrange("p (j t) -> p j t", j=4)[:, :, 0:T]
                rv_ = rbuf[rp][:, :].rearrange("p (j t) -> p j t", j=4)[:, :, 0:T]
                P.op('act', lambda e, pv_=pv_, rv_=rv_: e.activation(rv_, pv_, AF.Relu), reads=[ptk], writes=[t_rb[rp], T_["S0f"]])
                P.op('dve', lambda e, jg=jg, T=T, rv_=rv_: e.tensor_tensor(ffT[:, jg * 4:(jg + 1) * 4, 0:T], rv_, rv_, ALU.mult),
                     reads=[t_rb[rp]], writes=[t_ff])

        nb = len(blocks)
        load_x(l, 1, 0, 0, 0)
        load_x(l, 1, 1, 1, 0)
        prenorm(0, 0, 0, 0, False)
        for bi, (ti0, nt, sample) in enumerate(blocks):
            par = bi % 2
            xpar = bi % 3
            if bi + 2 < nb:
                load_x(l, 1, bi + 2, (bi + 2) % 3, 0)
            if bi + 1 < nb:
                prenorm_a((bi + 1) % 3, 0, blocks[bi + 1][2])
            up(bi, range(0, 4))
            if bi + 1 < nb:
                prenorm_b((bi + 1) % 2, 0, blocks[bi + 1][2], evac_dve=True)
            up(bi, range(4, 8))
            if bi + 1 == nb and after_last_up is not None:
                after_last_up()
            if sample:
                build_gate(True)
            np_ = TS if sample else 128
            yb = [psum(), psum()]
            for h2 in range(2):
                pt, ptk = yb[h2]
                for jf in range(32):
                    P.op('pe', lambda e, pt=pt, jf=jf, h2=h2, np_=np_: e.matmul(
                        pt[0:np_, :], ffT[:, jf, 0:np_], wdn[:, jf, h2 * 512:(h2 + 1) * 512],
                        start=(jf == 0), stop=(jf == 31)), reads=[t_ff, Wt[rd][jf // 4]], writes=[ptk])
            postnorm_store(l, 1, ti0, xpar, 0, yb, sample)

    LN8 = float(np.log(8.0))
    sigc_ = sb("sigc", [128, 2, 128], F32)
    sigc = sigc_
    uT = sb("uT", [128, 2, 128], BF16)
    vg = sb("vg", [128, 256], F32)
    vln = sb("vln", [128, 256], BF16)
    sqT = sb("sqT", [128, 4, 128], BF16)
    fT = sb("fT", [128, 4, 128], F32)
    lfT = sb("lfT", [128, 4, 128], F32)
    bT = sb("bT", [128, 4, 128], F32)
    r1 = sb("r1", [128, 4, 128], F32)
    t1 = r1
    osq = sqT
    acc2 = sigc_
    gT = sb("gT", [128, 4, 16], F32)
    qinT = sb("qinT", [128, 4, 128], BF16)
    kinT = sb("kinT", [128, 4, 128], BF16)
    kdT = sb("kdT", [128, 4, 128], BF16)
    qinTz = sb("qinTz", [128, 4, 2, 128], BF16)
    sgT = sb("sgT", [128, 4, 128], BF16)
    v_tok = sb("v_tok", [128, 512], BF16)
    kd_tok = sb("kd_tok", [128, 512], BF16)
    PT = sb("PT", [128, 2, 4, 128], BF16)
    S = sb("S", [128, 4, 64], F32)
    Sb = sb("Sb", [128, 8, 4, 64], BF16)
    kdmp = sb("kdmp", [128, 2, 512], BF16)
    ycatT = sb("ycatT", [128, 8, 128], BF16)
    xgT = [sb("xgT%d" % i, [128, 2, 30 + 128], BF16) for i in range(2)]
    Sg = sb("Sg", [128, 4, 64], F32)
    acc = sb("acc", [128, 2, 128], F32)
    cst = sb("cst", [128, 3, 128], F32)
    wsT = sb("wsT", [128, 4, 128], BF16)
    wsTf = bT
    bs_row = arena[0:1, 2048:2560].rearrange("p (h t) -> p h t", h=4)
    wsTs = sb("wsTs", [64, 4, 64], BF16)
    corner = sb("corner", [4, 4, 4], F32)
    wsTsf = r1[0:64, :, 0:64]
    bs_row_s4 = arena[0:1, 2560:2816].rearrange("p (c a b) -> p c a b", c=2, a=2)
    xg_tok = acc[:, :, :].rearrange("p c t -> p (c t)")[0:64, :]
    sig_tok = cst[:, 0:2, :].rearrange("p c t -> p (c t)")[0:64, :]
    S0b = arena[:, 0:2048].bitcast(BF16).rearrange("p (n h v) -> p n h v", n=16, v=64)
    S0f = arena[:, 2048:3072].rearrange("p (n h v) -> p n h v", n=4, v=64)
    xxT = arena[:, 3072:4160].rearrange("p (c n j) -> p c n j", c=2, j=34)
    kdm = kdmp[0:64, :, :]
    scv_t = t2[0:120, :].rearrange("p (g c) -> p g c", g=4)
    tmpc = lfT[:, :, :].rearrange("p h t -> p (h t)")[:, 0:496].rearrange("p (n j) -> p n j", j=31)
    T_ = {n: Tok(n) for n in ["uT", "vg", "vln", "sqT", "fT", "lfT", "bT", "gT", "qinT", "kinT", "kdT", "qinTz", "sgT",
                              "v_tok", "kd_tok", "PT", "S", "Sb", "osq", "r1", "t1", "ycatT", "xg0", "xg1", "sigc", "acc",
                              "acc2", "cst", "ws", "wss", "xg_tok", "S0f", "S0b", "kdm", "xxT", "scv_t", "tmpc"]}
    T_["S0b"] = t_S0b
    T_["corner"] = Tok("corner")
    for _n in ("ycat_a", "ycat_b", "ycat_c"):
        T_[_n] = Tok(_n)
    T_["Sg"] = Tok("Sg")
    T_["S0"] = T_["S"]
    T_["S1"] = Tok("S2")
    T_["t1"] = T_["r1"]
    T_["osq"] = T_["sqT"]
    T_["acc2"] = T_["sigc"]
    T_["scv_t"] = t_t2
    T_["tmpc"] = T_["lfT"]

    def prep_layer_mixer(l, rw):
        Dg = W[1 - rw][:, 24576:32768].rearrange("p (i c) -> p i c", c=128)
        dgtk = [Wt[1 - rw][6], Wt[1 - rw][7]]
        P.op('dve', lambda e: e.tensor_tensor(Dg[:, 0:62, :], ident_b[:, :].unsqueeze(1).to_broadcast([128, 62, 128]),
                                              cw[:, l, :, :].rearrange("p c j -> p (c j)").unsqueeze(2).to_broadcast([128, 62, 128]), ALU.mult),
             reads=[t_const, t_par], writes=dgtk)

    def prep_layer_mixer_b(l, rw):
        P.dma('sp', lng[:, :], a_ln_g[l:l + 1, :].to_broadcast([128, 256]), writes=[t_ln])
        P.dma('sp', lnb[:, :], a_ln_b[l:l + 1, :].to_broadcast([128, 256]), writes=[t_ln])
        P.dma('sp', wsTf[:, :, :], a_w_s[l].rearrange("h t s -> t h s"), writes=[T_["ws"], T_["bT"]])
        pt, ptk = psum()
        for h in range(4):
            P.op('pe', lambda e, h=h: e.transpose(pt[:, h * 128:(h + 1) * 128], wsTf[:, h, :], ident_f[:, :]),
                 reads=[T_["ws"], T_["bT"], t_const], writes=[ptk])
        P.op('dve', lambda e: e.tensor_copy(wsTf[:, :, :].rearrange("s h t -> s (h t)"), pt[:, :]), reads=[ptk], writes=[T_["ws"], T_["bT"]])
        P.op('pool', lambda e: e.affine_select(wsT[:, :, :], wsTf[:, :, :], [[0, 4], [1, 128]], ALU.is_ge, 0.0, base=0,
                                               channel_multiplier=-1), reads=[T_["ws"], T_["bT"]], writes=[T_["ws"]])
        P.dma('sp', bs_row, a_b_s[l:l + 1, :, :], writes=[T_["ws"], T_["S0f"]])
        P.op('pool', lambda e: e.memset(wsTsf, 0.0), writes=[T_["wss"], T_["r1"]])
        P.op('dve', lambda e: e.tensor_copy(corner[:, :, :], wsTf[0:4, :, 0:4]), reads=[T_["ws"], T_["bT"]], writes=[T_["corner"]])
        for n in range(NSEQ_S):
            P.dma('sp', r1[4 * n:4 * n + 4, :, 4 * n:4 * n + 4], corner[:, :, :], reads=[T_["corner"]],
                  writes=[T_["wss"], T_["r1"]])

    def prep_layer_mixer_c(l, rw):
        P.op('dve', lambda e: e.tensor_tensor(wsTs[:, :, :], wsTsf, mask_s[:, :].unsqueeze(1).to_broadcast([64, 4, 64]), ALU.mult),
             reads=[T_["wss"], T_["r1"], t_const], writes=[T_["wss"]])


    _a = arena
    FS = [(uT, vg, vln, sqT, fT, sgT, v_tok),
          (_a[:, 0:128].bitcast(BF16).rearrange("p (c t) -> p c t", c=2),
           _a[:, 128:384],
           _a[:, 384:512].bitcast(BF16),
           _a[:, 512:768].bitcast(BF16).rearrange("p (c t) -> p c t", c=4),
           _a[:, 768:1280].rearrange("p (c t) -> p c t", c=4),
           _a[:, 1280:1536].bitcast(BF16).rearrange("p (c t) -> p c t", c=4),
           _a[:, 1536:1792].bitcast(BF16))]
    FTOK = [{k: T_[k] for k in ["uT", "vg", "vln", "sqT", "fT", "sgT", "v_tok"]},
            {k: Tok(k + "_1") for k in ["uT", "vg", "vln", "sqT", "fT", "sgT", "v_tok"]}]
    xb.append(arena[:, 3072:4096].rearrange("p (j d) -> p j d", j=1))
    t_xb.append(Tok("xb2"))

    def mixer_tile(l, rw, ti, sample, last_prompt):
        Dg = W[1 - rw][:, 24576:32768].rearrange("p (i c) -> p i c", c=128)
        dgtk = [Wt[1 - rw][6], Wt[1 - rw][7]]
        par = ti % 2
        xpar = ti % 3
        T = TS if sample else 128
        C = LS if sample else 16
        nch = T // C
        win = W[rw][:, 0:24576].rearrange("p (k n) -> p k n", k=8)
        wout = W[rw][:, 24576:32768].rearrange("p (k n) -> p k n", k=8)
        wtk = [Wt[rw][i] for i in range(6)]
        wotk = [Wt[rw][6], Wt[rw][7]]
        hpar = par
        h_ = hT[hpar]
        th = t_hT[hpar]
        mask = mask_s if sample else mask_p
        mres = mres_s if sample else mres_p
        xg = xgT[par]
        txg = T_["xg%d" % par]

        def fm_group(c0, nchunk):
            pt, ptk = psum()
            for jc in range(nchunk):
                for k in range(8):
                    P.op('pe', lambda e, pt=pt, jc=jc, k=k: e.matmul(pt[:, jc * 128:jc * 128 + T], win[:, k, c0 + jc * 128:c0 + (jc + 1) * 128],
                                                                     h_[:, k, 0:T], start=(k == 0), stop=(k == 7)),
                         reads=wtk + [th], writes=[ptk])
            return pt, ptk

        def tm_group(c0, ncol, t0=0, tn=None):
            tn = T if tn is None else tn
            pt, ptk = psum()
            for k in range(8):
                P.op('pe', lambda e, pt=pt, k=k: e.matmul(pt[0:tn, 0:ncol], h_[:, k, t0:t0 + tn], win[:, k, c0:c0 + ncol],
                                                          start=(k == 0), stop=(k == 7)), reads=wtk + [th], writes=[ptk])
            return pt, ptk

        def v3(pt, n):
            return pt[:, 0:n * 128].rearrange("p (c t) -> p c t", t=128)[:, :, 0:T]

        uT, vg, vln, sqT, fT, sgT, v_tok = FS[par]
        osq = sqT
        TT = dict(T_)
        TT.update(FTOK[par])
        TT["osq"] = TT["sqT"]
        prenorm(xpar, 0, par, 0, sample)
        yield
        pq, pqk = fm_group(512, 4)
        P.op('act', lambda e: e.activation(sqT[:, :, 0:T], v3(pq, 4), AF.Silu), reads=[pqk], writes=[TT["sqT"]])
        pg, pgk = fm_group(2048, 4)
        P.op('act', lambda e: e.activation(sgT[:, :, 0:T], v3(pg, 4), AF.Silu), reads=[pgk], writes=[TT["sgT"]])
        pu, puk = fm_group(0, 2)
        P.op('act', lambda e: e.activation(uT[:, :, 0:T], v3(pu, 2), AF.Gelu), reads=[puk], writes=[TT["uT"]])
        pv, pvk = tm_group(256, 256)
        P.op('act', lambda e: e.activation(vg[0:T, :], pv[0:T, 0:256], AF.Gelu), reads=[pvk], writes=[TT["vg"]])
        pf, pfk = fm_group(1024, 4)
        P.op('act', lambda e: e.activation(fT[:, :, 0:T], v3(pf, 4), AF.Sigmoid), reads=[pfk], writes=[TT["fT"]])
        pc, pck = fm_group(2560, 4)
        P.op('act', lambda e: e.activation(sigc[:, :, 0:T], v3(pc, 4)[:, 2:4, :], AF.Sigmoid), reads=[pck], writes=[TT["sigc"]])
        pi_, pik = tm_group(1536, 512)
        P.op('act', lambda e: e.copy(v_tok[0:T, :], pi_[0:T, :]), reads=[pik], writes=[TT["v_tok"]])
        if sample or last_prompt:
            t0, tn = (0, TS) if sample else (96, 32)
            pz, pzk = tm_group(2560, 512, t0, tn)
            P.op('act', lambda e: e.activation(sig_tok[0:tn, :], pz[0:tn, 256:512], AF.Sigmoid), reads=[pzk], writes=[TT["cst"]])
        P.op('act', lambda e: e.activation(stat[:, 2:3], ones_f[:, 0:1], AF.Ln), reads=[t_const], writes=[t_stat_glob2])
        yield
        if sample:
            xgdst = xxT[:, :, :, 30:34]
            P.op('dve', lambda e: e.tensor_tensor(xgdst, v3(pc, 4)[:, 0:2, :].rearrange("p c (n t) -> p c n t", t=LS),
                                                  sigc[:, :, 0:T].rearrange("p c (n t) -> p c n t", t=LS), ALU.mult),
                 reads=[pck, TT["sigc"]], writes=[TT["xxT"], t_xb[2]])
        else:
            P.op('dve', lambda e: e.tensor_tensor(xg[:, :, 30:30 + T], v3(pc, 4)[:, 0:2, :], sigc[:, :, 0:T], ALU.mult),
                 reads=[pck, TT["sigc"]], writes=[txg])
        if sample or last_prompt:
            P.op('dve', lambda e: e.tensor_tensor(xg_tok[0:tn, :], pz[0:tn, 0:256], sig_tok[0:tn, :], ALU.mult),
                 reads=[pzk, TT["cst"]], writes=[TT["acc"]])
            if sample:
                for t in range(LS):
                    P.dma('sp', cs_o[l, :, 26 + t, :], xg_tok[t:TS:LS, :], reads=[TT["acc"]])
                P.dma('sp', cs_o[l, :, 0:26, :], scv[l, :, 4:30, :])
            else:
                P.dma('sp', cp_o[l, :, :], xg_tok[2:32, :], reads=[TT["acc"]])
        if not sample:
            nx = xgT[1 - par]
            P.op('pool', lambda e: e.tensor_copy(nx[:, :, 0:30], xg[:, :, T:T + 30]), reads=[txg], writes=[TT["xg%d" % (1 - par)]])
            pcv, pcvk = ps[:, 7, :], ps_tok[7]
            for cc in range(2):
                for j in range(31):
                    P.op('pe', lambda e, cc=cc, j=j: e.matmul(pcv[:, cc * 128:cc * 128 + T], Dg[:, cc * 31 + j, :], xg[:, cc, j:j + T],
                                                              start=(j == 0), stop=(j == 30)), reads=[txg] + dgtk, writes=[pcvk])
        yield
        P.begin_chain()
        P.op('dve', lambda e: e.bn_stats(stat[0:T, 8:14], vg[0:T, :]), reads=[TT["vg"]], writes=[t_stat_lnv])
        P.op('dve', lambda e: e.bn_aggr(stat[0:T, 14:16], stat[0:T, 8:14]), reads=[t_stat_lnv], writes=[t_stat_lnv])
        P.op('act', lambda e: e.activation(stat[0:T, 15:16], stat[0:T, 15:16], AF.Ln, bias=EPS, scale=1.0), reads=[t_stat_lnv], writes=[t_stat_lnv])
        P.op('act', lambda e: e.activation(stat[0:T, 15:16], stat[0:T, 15:16], AF.Exp, scale=-0.5), reads=[t_stat_lnv], writes=[t_stat_lnv])
        P.op('dve', lambda e: e.tensor_scalar(vg[0:T, :], vg[0:T, :], stat[0:T, 14:15], stat[0:T, 15:16], ALU.subtract, ALU.mult),
             reads=[TT["vg"], t_stat_lnv], writes=[TT["vg"]])
        P.op('dve', lambda e: e.tensor_tensor(vg[0:T, :], vg[0:T, :], lng[0:T, :], ALU.mult), reads=[TT["vg"], t_ln], writes=[TT["vg"]])
        P.op('dve', lambda e: e.tensor_tensor(vg[0:T, :], vg[0:T, :], lnb[0:T, :], ALU.add), reads=[TT["vg"], t_ln], writes=[TT["vg"]])
        P.op('act', lambda e: e.copy(vln[0:T, :], vg[0:T, :]), reads=[TT["vg"]], writes=[TT["vln"]])
        if sample:
            P.dma('sp', gv_o[l].rearrange("n t c -> (n t) c"), vg[0:T, :], reads=[TT["vg"]])
        P.end_chain()
        P.begin_chain()
        if l == 1:
            for hp in range(4):
                P.op('dve', lambda e, hp=hp: e.tensor_scalar(fT[:, hp, 0:T], fT[:, hp, 0:T], oml1[:, hp:hp + 1], lb1[:, hp:hp + 1], ALU.mult, ALU.add),
                     reads=[TT["fT"], t_par], writes=[TT["fT"]])
        P.op('act', lambda e: e.activation(lfT[:, :, 0:T], fT[:, :, 0:T], AF.Ln), reads=[TT["fT"]], writes=[TT["lfT"]])
        P.op('dve', lambda e: e.tensor_scalar(fT[:, :, 0:T], fT[:, :, 0:T], -1.0, 1.0, ALU.mult, ALU.add), reads=[TT["fT"], TT["lfT"]], writes=[TT["fT"]])
        for hp in range(4):
            P.op('dve', lambda e, hp=hp: e.tensor_tensor_scan(bT[:, hp, 0:T], mres[:, 0:T], lfT[:, hp, 0:T], 0.0, ALU.mult, ALU.add),
                 reads=[TT["lfT"], t_const], writes=[TT["bT"]])
        bview = bT[:, :, 0:T].rearrange("p h (c i) -> p h c i", i=C)[:, :, :, C - 1]
        P.op('act', lambda e: e.activation(gT[:, :, 0:nch], bview, AF.Exp), reads=[TT["bT"]], writes=[TT["gT"]])
        P.op('act', lambda e: e.activation(lfT[:, :, 0:T], bT[:, :, 0:T], AF.Exp, bias=-LN8), reads=[TT["bT"], TT["lfT"]], writes=[TT["lfT"]])
        P.op('dve', lambda e: e.tensor_tensor(qinT[:, :, 0:T], sqT[:, :, 0:T], lfT[:, :, 0:T], ALU.mult), reads=[TT["sqT"], TT["lfT"]], writes=[TT["qinT"]])
        P.op('act', lambda e: e.activation(bT[:, :, 0:T], bT[:, :, 0:T], AF.Exp, scale=-1.0), reads=[TT["bT"], TT["gT"], TT["lfT"]], writes=[TT["bT"]])
        P.op('dve', lambda e: e.tensor_tensor(kinT[:, :, 0:T], fT[:, :, 0:T], bT[:, :, 0:T], ALU.mult), reads=[TT["fT"], TT["bT"]], writes=[TT["kinT"]])
        P.op('dve', lambda e: e.tensor_tensor(kdT[:, :, 0:T].rearrange("p h (c i) -> p h c i", i=C),
                                              kinT[:, :, 0:T].rearrange("p h (c i) -> p h c i", i=C),
                                              gT[:, :, 0:nch].unsqueeze(3).to_broadcast([128, 4, nch, C]), ALU.mult),
             reads=[TT["kinT"], TT["gT"]], writes=[TT["kdT"]])
        P.end_chain()
        P.begin_chain()
        if sample:
            for g4 in range(4):
                P.dma('sp', scv_t[:, g4, :], scv[l, 4 * g4:4 * g4 + 4].rearrange("n j c -> (n j) c"), writes=[TT["scv_t"]])
            for cc in range(2):
                pt, ptk = psum()
                for g4 in range(4):
                    P.op('pe', lambda e, pt=pt, g4=g4, cc=cc: e.transpose(pt[:, g4 * 120:(g4 + 1) * 120], scv_t[:, g4, cc * 128:(cc + 1) * 128], ident_f[0:120, 0:120]),
                         reads=[TT["scv_t"], t_const], writes=[ptk])
                P.op('act', lambda e, pt=pt, cc=cc: e.copy(xxT[:, cc, :, 0:30], pt[:, 0:480].rearrange("p (n j) -> p n j", j=30)),
                     reads=[ptk], writes=[TT["xxT"], t_xb[2]])
            for cc in range(2):
                for t in range(LS):
                    P.op('dve', lambda e, cc=cc, t=t: e.tensor_tensor(tmpc[:, :, :], xxT[:, cc, :, t:t + 31],
                                                                      cw[:, l, cc, :].unsqueeze(1).to_broadcast([128, 16, 31]), ALU.mult),
                         reads=[TT["xxT"], t_par], writes=[TT["tmpc"]])
                    P.op('dve', lambda e, cc=cc, t=t: e.tensor_reduce(acc[:, cc, t:TS:LS], tmpc[:, :, :], AX.X, ALU.add),
                         reads=[TT["tmpc"]], writes=[TT["acc"]])
                P.op('dve', lambda e, cc=cc: e.tensor_scalar(acc[:, cc, 0:T], acc[:, cc, 0:T], cb[:, l, cc:cc + 1], None, ALU.add),
                     reads=[TT["acc"], t_par], writes=[TT["acc"]])
        else:
            for cc in range(2):
                P.op('dve', lambda e, cc=cc: e.tensor_scalar(acc[:, cc, 0:T], pcv[:, cc * 128:cc * 128 + T], cb[:, l, cc:cc + 1], None, ALU.add),
                     reads=[pcvk, t_par], writes=[TT["acc"]])
        P.op('act', lambda e: e.activation(acc2[:, :, 0:T], acc[:, :, 0:T], AF.Square), reads=[TT["acc"]], writes=[TT["acc2"]])
        pl_, plk = psum()
        for cc in range(2):
            P.op('pe', lambda e, cc=cc: e.matmul(pl_[:, 0:T], ones_f[:, :], acc[:, cc, 0:T], start=(cc == 0), stop=(cc == 1)),
                 reads=[TT["acc"], t_const], writes=[plk])
        for cc in range(2):
            P.op('pe', lambda e, cc=cc: e.matmul(pl_[:, 128:128 + T], ones_f[:, :], acc2[:, cc, 0:T], start=(cc == 0), stop=(cc == 1)),
                 reads=[TT["acc2"], t_const], writes=[plk])
        P.op('dve', lambda e: e.tensor_scalar(cst[:, 0, 0:T], pl_[:, 0:T], 1.0 / 256, None, ALU.mult), reads=[plk], writes=[TT["cst"]])
        P.op('dve', lambda e: e.tensor_tensor(cst[:, 1, 0:T], cst[:, 0, 0:T], cst[:, 0, 0:T], ALU.mult), reads=[TT["cst"]], writes=[TT["cst"]])
        P.op('dve', lambda e: e.scalar_tensor_tensor(cst[:, 1, 0:T], pl_[:, 128:128 + T], 1.0 / 256, cst[:, 1, 0:T], ALU.mult, ALU.subtract),
             reads=[plk, TT["cst"]], writes=[TT["cst"]])
        P.op('act', lambda e: e.activation(cst[:, 1, 0:T], cst[:, 1, 0:T], AF.Ln, bias=EPS, scale=1.0), reads=[TT["cst"]], writes=[TT["cst"]])
        P.op('act', lambda e: e.activation(cst[:, 1, 0:T], cst[:, 1, 0:T], AF.Exp, scale=-0.5), reads=[TT["cst"]], writes=[TT["cst"]])
        P.op('dve', lambda e: e.tensor_tensor(acc[:, :, 0:T], acc[:, :, 0:T], cst[:, 0:1, 0:T].to_broadcast([128, 2, T]), ALU.subtract),
             reads=[TT["acc"], TT["cst"]], writes=[TT["acc"]])
        P.op('dve', lambda e: e.tensor_tensor(acc[:, :, 0:T], acc[:, :, 0:T], cst[:, 1:2, 0:T].to_broadcast([128, 2, T]), ALU.mult),
             reads=[TT["acc"], TT["cst"]], writes=[TT["acc"]])
        P.end_chain()
        yield
        for hl in range(2):
            P.op('dve', lambda e, hl=hl: e.tensor_scalar(qinTz[:, :, hl, 0:T], qinT[:, :, 0:T], mhl[:, hl:hl + 1], None, ALU.mult),
                 reads=[TT["qinT"], t_const], writes=[TT["qinTz"]])
        pm, pmk = psum()
        wsv = wsTs if sample else wsT
        twsv = TT["wss"] if sample else TT["ws"]
        tbsr = TT["S0f"]
        for h in range(4):
            c2, hl = h // 2, h % 2
            o_ap = pm[hl * 64:(hl + 1) * 64, c2 * 128:c2 * 128 + T]
            P.op('pe', lambda e, o_ap=o_ap, h=h: e.matmul(o_ap, vln[0:T, h * 64:(h + 1) * 64], wsv[0:T, h, 0:T], start=True, stop=False),
                 reads=[TT["vln"], twsv], writes=[pmk])
            P.op('pe', lambda e, o_ap=o_ap, h=h: e.matmul(o_ap, ones_f[0:1, 0:64], (bs_row_s4[0:1, h // 2, h % 2, 0:T] if sample else bs_row[0:1, h, 0:T]), start=False, stop=True),
                 reads=[twsv, tbsr, t_const], writes=[pmk])
        P.op('dve', lambda e: e.tensor_tensor(ycatT[:, 0:2, 0:T], uT[:, :, 0:T], v3(pm, 2), ALU.mult), reads=[pmk, TT["uT"]], writes=[TT["ycat_a"]])

        pk, pkk = psum()
        pkb = pk.bitcast(BF16)
        for hp in range(4):
            P.op('pe', lambda e, hp=hp: e.transpose(pkb[0:T, hp * 128:(hp + 1) * 128], kdT[:, hp, 0:T], ident_b[:, :]),
                 reads=[TT["kdT"], t_const], writes=[pkk])
        P.op('act', lambda e: e.copy(kd_tok[0:T, :], pkb[0:T, 0:512]), reads=[pkk], writes=[TT["kd_tok"]])
        sc = [psum(), psum()]
        for h in range(8):
            hp, hl = h // 2, h % 2
            pt, ptk = sc[hl]
            P.op('pe', lambda e, pt=pt, hp=hp, hl=hl: e.matmul(pt[0:T, hp * 128:hp * 128 + T], kinT[hl * 64:(hl + 1) * 64, hp, 0:T],
                                                               qinT[hl * 64:(hl + 1) * 64, hp, 0:T], start=True, stop=True),
                 reads=[TT["kinT"], TT["qinT"]], writes=[ptk])
        for hl in range(2):
            pt, ptk = sc[hl]
            P.op('dve', lambda e, pt=pt, hl=hl: e.tensor_tensor(PT[0:T, hl, :, 0:T], v3(pt, 4)[0:T], mask[0:T, 0:T].unsqueeze(1).to_broadcast([T, 4, T]), ALU.mult),
                 reads=[ptk, t_const], writes=[TT["PT"]])
        if not sample:
            for q in range(2):
                P.op('dve', lambda e, q=q: e.tensor_scalar(kdmp[:, q, :], kd_tok[:, :], mpar[:, q:q + 1], None, ALU.mult),
                     reads=[TT["kd_tok"], t_const], writes=[TT["kdm"]])
            ubanks = []
            for c in range(8):
                if c % 2 == 0:
                    ubanks.append(psum(hold=True))
                pu_, puk_ = ubanks[c // 2]
                q = c % 2
                b32 = 32 * (c // 2)
                for h in range(8):
                    hp, hl = h // 2, h % 2
                    P.op('pe', lambda e, pu_=pu_, hp=hp, hl=hl, h=h, q=q, b32=b32: e.matmul(
                        pu_[hl * 64:(hl + 1) * 64, q * 256 + hp * 64:q * 256 + (hp + 1) * 64], kdmp[b32:b32 + 32, q, h * 64:(h + 1) * 64],
                        v_tok[b32:b32 + 32, h * 64:(h + 1) * 64], start=True, stop=True, tile_position=(b32, 64 * hl)),
                        reads=[TT["kdm"], TT["v_tok"]], writes=[puk_])
        if sample:
            s0bufs = [S0f, xb[0][:, 0, :].rearrange("p (n h v) -> p n h v", n=4, v=64)]
            s0toks = [TT["S0f"], t_xb[0]]

            def s0_load(g4):
                for hl in range(2):
                    P.dma('sp', s0bufs[g4 % 2][hl * 64:(hl + 1) * 64, :, :, :],
                          sh[l, 4 * g4:4 * g4 + 4].rearrange("n (hp hl) k v -> hl k n hp v", hl=2)[hl], writes=[s0toks[g4 % 2]])
            s0_load(0)
            s0_load(1)
            for g4 in range(4):
                sbuf_, stok_ = s0bufs[g4 % 2], s0toks[g4 % 2]
                P.op('act', lambda e, g4=g4, sbuf_=sbuf_: e.copy(S0b[:, 4 * g4:4 * g4 + 4, :, :], sbuf_[:, :, :, :]), reads=[stok_],
                     writes=[TT["S0b"]] + list(FTOK[1].values()))
                for pr in range(2):
                    pu_, puk_ = psum()
                    for q in range(2):
                        n = 4 * g4 + 2 * pr + q
                        P.op('dve', lambda e, n=n, q=q: e.tensor_scalar(kdm[:, q, :], kd_tok[0:TS, :], oneh[:, n:n + 1], None, ALU.mult),
                             reads=[TT["kd_tok"], t_const], writes=[TT["kdm"]])
                        for h in range(8):
                            hp, hl = h // 2, h % 2
                            P.op('pe', lambda e, pu_=pu_, q=q, hp=hp, hl=hl, h=h: e.matmul(
                                pu_[hl * 64:(hl + 1) * 64, q * 256 + hp * 64:q * 256 + (hp + 1) * 64], kdm[:, q, h * 64:(h + 1) * 64],
                                v_tok[0:TS, h * 64:(h + 1) * 64], start=True, stop=True), reads=[TT["kdm"], TT["v_tok"]], writes=[puk_])
                    n0 = 2 * pr
                    sv = sbuf_[:, n0:n0 + 2, :, :]
                    gb = gT[:, :, 4 * g4 + n0:4 * g4 + n0 + 2].rearrange("p h n -> p n h").unsqueeze(3).to_broadcast([128, 2, 4, 64])
                    P.op('dve', lambda e, sv=sv, gb=gb: e.tensor_tensor(sv, sv, gb, ALU.mult), reads=[stok_, TT["gT"], TT["S0b"]], writes=[stok_])
                    P.op('dve', lambda e, sv=sv, pu_=pu_: e.tensor_tensor(sv, sv, pu_[:, 0:512].rearrange("p (n h v) -> p n h v", n=2, v=64), ALU.add),
                         reads=[stok_, puk_], writes=[stok_])
                for hl in range(2):
                    P.dma('sp', hs_o[l, 4 * g4:4 * g4 + 4].rearrange("n (hp hl) k v -> hl k n hp v", hl=2)[hl],
                          sbuf_[hl * 64:(hl + 1) * 64, :, :, :], reads=[stok_])
                if g4 + 2 < 4:
                    s0_load(g4 + 2)
        for cc in range(2):
            P.op('act', lambda e, cc=cc: e.activation(ycatT[:, 6 + cc, 0:T], acc[:, cc, 0:T], AF.Silu, bias=clb[:, l, cc:cc + 1], scale=clg[:, l, cc:cc + 1]),
                 reads=[TT["acc"], t_par], writes=[TT["ycat_c"]])

        yield
        if not sample:
            for c in range(8):
                pu_, puk_ = ubanks[c // 2]
                q = c % 2
                P.op('dve', lambda e, c=c: e.tensor_tensor(Sg[:, :, :], S[:, :, :], gT[:, :, c:c + 1].to_broadcast([128, 4, 64]), ALU.mult),
                     reads=[TT["S"], TT["gT"]], writes=[TT["Sg"]])
                P.op('dve', lambda e, c=c: e.tensor_copy(Sb[:, c, :, :], S[:, :, :]), reads=[TT["S"]], writes=[TT["Sb"]])
                P.op('dve', lambda e, pu_=pu_, q=q: e.tensor_tensor(S[:, :, :], Sg[:, :, :], pu_[:, q * 256:(q + 1) * 256].rearrange("p (h v) -> p h v", v=64), ALU.add),
                     reads=[TT["Sg"], puk_], writes=[TT["S"]])
                if c % 2 == 1:
                    psum_release(puk_)
        yield
        po, pok = psum()
        for h in range(8):
            hp, hl = h // 2, h % 2
            o_ap = po[hl * 64:(hl + 1) * 64, hp * 128:hp * 128 + T]
            P.op('pe', lambda e, o_ap=o_ap, h=h, hp=hp, hl=hl: e.matmul(o_ap, v_tok[0:T, h * 64:(h + 1) * 64], PT[0:T, hl, hp, 0:T], start=True, stop=False),
                 reads=[TT["v_tok"], TT["PT"]], writes=[pok])
            for c in range(nch):
                st_ap = S0b[:, c, hp, :] if sample else Sb[:, c, hp, :]
                P.op('pe', lambda e, st_ap=st_ap, hp=hp, hl=hl, c=c: e.matmul(
                    po[hl * 64:(hl + 1) * 64, hp * 128 + c * C:hp * 128 + (c + 1) * C], st_ap, qinTz[:, hp, hl, c * C:(c + 1) * C],
                    start=False, stop=(c == nch - 1)), reads=[TT["S0b"] if sample else TT["Sb"], TT["qinTz"]], writes=[pok])
        yield
        P.op('act', lambda e: e.activation(osq[:, :, 0:T], v3(po, 4), AF.Square), reads=[pok], writes=[TT["osq"]])
        pss, pssk = psum()
        if sample:
            for hp in range(4):
                P.op('pe', lambda e, hp=hp: e.matmul(pss[:, hp * 128:hp * 128 + T], bones_b[:, :], osq[:, hp, 0:T], start=True, stop=True),
                     reads=[TT["osq"], t_const], writes=[pssk])
        else:
            P.op('pe', lambda e: e.matmul(pss[:, 0:512], bones_b[:, :], osq[:, :, :].rearrange("p c t -> p (c t)"), start=True, stop=True),
                 reads=[TT["osq"], t_const], writes=[pssk])
        P.op('act', lambda e: e.activation(r1[:, :, 0:T], v3(pss, 4), AF.Ln, bias=EPS, scale=1.0 / 64), reads=[pssk], writes=[TT["r1"]])
        P.op('act', lambda e: e.activation(r1[:, :, 0:T], r1[:, :, 0:T], AF.Exp, scale=-0.5), reads=[TT["r1"]], writes=[TT["r1"]])
        P.op('dve', lambda e: e.tensor_tensor(t1[:, :, 0:T], v3(po, 4), r1[:, :, 0:T], ALU.mult), reads=[pok, TT["r1"]], writes=[TT["t1"]])
        P.op('dve', lambda e: e.tensor_tensor(ycatT[:, 2:6, 0:T], t1[:, :, 0:T], sgT[:, :, 0:T], ALU.mult), reads=[TT["t1"], TT["sgT"]], writes=[TT["ycat_b"]])
        if last_prompt:
            for hl in range(2):
                P.dma('sp', hp_o[l].rearrange("(hp hl) k v -> hl k hp v", hl=2)[hl], S[hl * 64:(hl + 1) * 64, :, :], reads=[TT["S"]])

        yb = [psum(), psum()]
        korder = [0, 1, 6, 7, 2, 3, 4, 5]
        ytok = {0: "ycat_a", 1: "ycat_a", 6: "ycat_c", 7: "ycat_c", 2: "ycat_b", 3: "ycat_b", 4: "ycat_b", 5: "ycat_b"}
        for kk in range(0, 8, 4):
            for h2 in range(2):
                pt, ptk = yb[h2]
                for k in korder[kk:kk + 4]:
                    P.op('pe', lambda e, pt=pt, k=k, h2=h2: e.matmul(pt[0:T, :], ycatT[:, k, 0:T], wout[:, k, h2 * 512:(h2 + 1) * 512],
                                                                      start=(k == korder[0]), stop=(k == korder[-1])),
                         reads=[TT[ytok[k]]] + wotk, writes=[ptk])
        postnorm_store(l, 0, ti, xpar, 0, yb, sample)
        yield


    def phase_mix(l, rw, hooks=None, bg=None):
        setup_mod(l, 0, part=1)
        gens = [mixer_tile(l, rw, ti, ti == 16, ti == 15) for ti in range(17)]
        load_x(l, 0, 0, 0, 0)
        load_x(l, 0, 1, 1, 0)
        P.op('pool', lambda e: e.memset(S[:, :, :], 0.0), writes=[T_["S"]])
        P.op('pool', lambda e: e.memset(xgT[0][:, :, 0:30], 0.0), writes=[T_["xg0"]])
        next(gens[0])
        prep_layer_mixer(l, rw)
        next(gens[0])
        next(gens[0])
        fold_gn(l, rw)
        prep_layer_mixer_b(l, rw)
        if bg is not None:
            bg()
        setup_mod(l, 0, part=2)
        build_gate(False)
        P.begin_chain()
        next(gens[1])
        P.end_chain()
        next(gens[0])
        P.merge_chains()
        for ti in range(17):
            nxt = ti + 1 < 17
            if bg is not None:
                bg()
            if ti + 2 < 17:
                load_x(l, 0, ti + 2, (ti + 2) % 3, 0)
            if ti == 16:
                build_gate(True)
                for h in range(4):
                    P.dma('sp', bs_row_s4[0:1, h // 2, h % 2, :].rearrange("p (n t) -> p n t", t=LS),
                          a_b_s[l:l + 1, h, 0:4].unsqueeze(1).to_broadcast([1, NSEQ_S, LS]), writes=[T_["S0f"]])
            if ti == 0:
                prep_layer_mixer_c(l, rw)
            next(gens[ti])
            if nxt:
                next(gens[ti + 1])
                if ti + 1 == 16 and hooks is not None:
                    hooks[1]()
            next(gens[ti])
            if nxt:
                next(gens[ti + 1])
                if ti + 1 == 15 and hooks is not None:
                    hooks[0]()
            next(gens[ti])
            P.begin_chain()
            next(gens[ti])
            P.end_chain()
            if ti + 2 < 17:
                P.begin_chain()
                next(gens[ti + 2])
                P.end_chain()
            if nxt:
                next(gens[ti + 1])
            P.merge_chains()

    if debug == 'mlp':
        while not mod1_state['done']:
            mod1_bg()
        load_w_up(0, 1, [6, 7])
        load_w_down(0, 0)
        cur['first'] = True
        cur['last'] = True
        phase_mlp(0, 1, 0)
    elif debug == 'mix':
        cur['first'] = True
        cur['last'] = True
        while not mod1_state['done']:
            mod1_bg()
        phase_mix(0, 0)
    else:
        cur['first'] = True
        phase_mix(0, 0, hooks=(lambda: load_w_up(0, 1, [6, 7]), lambda: load_w_down(0, 0, range(24))), bg=mod1_bg)
        while not mod1_state['done']:
            mod1_bg()
        cur['first'] = False
        load_w_down(0, 0, range(24, 32))
        phase_mlp(0, 1, 0, after_last_up=lambda: load_w_in_out(1, 1))
        load_w_up(1, 0, range(6))
        phase_mix(1, 1, hooks=(lambda: load_w_up(1, 0, [6, 7]), lambda: load_w_down(1, 1, range(24))))
        load_w_down(1, 1, range(24, 32))
        cur['last'] = True
        phase_mlp(1, 0, 1)

    P.finalize()
    with nc.Block() as block:
        @block.tensor
        def _(e):
            P.emit_engine('pe', e)

        @block.scalar
        def _(e):
            P.emit_engine('act', e)

        @block.vector
        def _(e):
            P.emit_engine('dve', e)

        @block.gpsimd
        def _(e):
            P.emit_engine('pool', e)

        @block.sync
        def _(e):
            P.emit_engine('sp', e)
            P.final_waits('sp', e)
    es.close()
    return nc


def make_in_maps(inputs):
    f = lambda a: np.ascontiguousarray(np.asarray(a, dtype=np.float32))
    maps = []
    for i in range(NCORES):
        m = {}
        m["xp"] = f(inputs["x_prompt"][i])
        m["xs"] = f(inputs["x_sample"][NSEQ_S * i:NSEQ_S * (i + 1)].reshape(TS, D))
        m["sh"] = f(inputs["state_hgrn"][:, NSEQ_S * i:NSEQ_S * (i + 1)])
        m["scv"] = f(inputs["state_conv"][:, NSEQ_S * i:NSEQ_S * (i + 1)])
        m["c"] = f(np.concatenate([inputs["c_prompt"][i:i + 1], inputs["c_sample"][NSEQ_S * i:NSEQ_S * (i + 1)]], 0))
        for k in ["w_ada", "b_ada", "g_pre_mix", "g_post_mix", "g_pre_mlp", "g_post_mlp", "w_in", "a_ln_g",
                  "a_ln_b", "a_w_s", "a_b_s", "b_lb", "b_gn_g", "c_w_dw", "c_b_dw", "c_ln_g", "c_ln_b",
                  "w_out", "w_up", "w_down"]:
            m[k] = f(inputs[k])
        maps.append(m)
    return maps


def kernel(**inputs):
    nc = build()
    maps = make_in_maps(inputs)
    res = run_bass_kernel_spmd(nc, maps, core_ids=list(range(NCORES)))
    r = res.results
    y_prompt = np.stack([r[i]["yp"] for i in range(NCORES)], 0)
    y_sample = np.concatenate([r[i]["ys"].reshape(NSEQ_S, LS, D) for i in range(NCORES)], 0)
    hgrn_prompt = np.stack([r[i]["hp"] for i in range(NCORES)], 1)
    hgrn_sample = np.concatenate([r[i]["hs"] for i in range(NCORES)], 1)
    conv_prompt = np.stack([r[i]["cp"] for i in range(NCORES)], 1)
    conv_sample = np.concatenate([r[i]["cs"] for i in range(NCORES)], 1)
    gmlp_v = np.concatenate([r[i]["gv"] for i in range(NCORES)], 1)
    return (y_prompt, y_sample, hgrn_prompt, hgrn_sample, conv_prompt, conv_sample, gmlp_v)
```
